# Optimizing a Trainium2 kernel written in Bass

```python
import math
import jax, jax.numpy as jnp
from jax import lax
import numpy as np


D_MODEL = 1024
BATCH = 4
SEQ = 8192
DEPTH = 2

CHUNK = 64
Q_BLOCK = 128
EPS = 1e-6

MLA_HEADS = 6
MLA_Q_RANK = 256
MLA_KV_RANK = 128
MLA_NOPE = 64
MLA_ROPE = 32
MLA_V = 64
ROPE_THETA = 10000.0

DIFF_HEADS = 6
DIFF_QK = 32
DIFF_V = 2 * DIFF_QK

CH_HEADS = 4
CH_HD = 64
CH_LEFT = 8
CH_BAND = (CH_LEFT + 1) * CHUNK
REL_MAX = 256
N_REL = REL_MAX + CHUNK

MLA_WIDTH = MLA_HEADS * MLA_V
DIFF_WIDTH = DIFF_HEADS * DIFF_V
CH_WIDTH = CH_HEADS * CH_HD
MIX_WIDTH = MLA_WIDTH + DIFF_WIDTH + CH_WIDTH

IN_MLA = MLA_Q_RANK + MLA_KV_RANK + MLA_ROPE
IN_DIFF = 2 * DIFF_HEADS * 2 * DIFF_QK + DIFF_HEADS * DIFF_V
IN_CH = 3 * CH_WIDTH
IN_WIDTH = IN_MLA + IN_DIFF + IN_CH

D_FF = 2816
CONV_W = 3

kernel_name = "hybrid_parallel_heads_streaming_encoder"


def rms_norm(x, g):
    xf = x.astype(jnp.float32)
    y = xf * lax.rsqrt(jnp.mean(xf * xf, axis=-1, keepdims=True) + EPS)
    return (y * g.astype(jnp.float32)).astype(x.dtype)


def rope_cos_sin(pos, dim):
    half = dim // 2
    inv = ROPE_THETA ** (-jnp.arange(half, dtype=jnp.float32) / half)
    ang = pos.astype(jnp.float32)[..., None] * inv
    return jnp.cos(ang), jnp.sin(ang)


def apply_rope(x, cos, sin):
    half = x.shape[-1] // 2
    xf = x.astype(jnp.float32)
    x1, x2 = xf[..., :half], xf[..., half:]
    return jnp.concatenate([x1 * cos - x2 * sin, x1 * sin + x2 * cos], axis=-1).astype(x.dtype)


def alibi_slopes(n):
    return 2.0 ** (-8.0 * jnp.arange(1, n + 1, dtype=jnp.float32) / n)


def sweep_query_blocks(block_fn, seq):
    out = lax.map(block_fn, jnp.arange(seq // Q_BLOCK))
    out = jnp.moveaxis(out, 0, 1)
    return out.reshape(out.shape[0], seq, *out.shape[3:])


def chunk_causal_mask(start, seq):
    q_chunk = (start + jnp.arange(Q_BLOCK)) // CHUNK
    k_chunk = jnp.arange(seq) // CHUNK
    return k_chunk[None, :] <= q_chunk[:, None]


def mla_attention(h, pos, q_norm, w_uq, kv_norm, w_ukv):
    B, S, _ = h.shape
    c_q, c_kv, k_r = jnp.split(h, [MLA_Q_RANK, MLA_Q_RANK + MLA_KV_RANK], axis=-1)
    cos, sin = rope_cos_sin(pos, MLA_ROPE)
    q = (rms_norm(c_q, q_norm) @ w_uq).reshape(B, S, MLA_HEADS, MLA_NOPE + MLA_ROPE)
    q_nope = q[..., :MLA_NOPE]
    q_rope = apply_rope(q[..., MLA_NOPE:], cos[:, :, None, :], sin[:, :, None, :])
    kv = (rms_norm(c_kv, kv_norm) @ w_ukv).reshape(B, S, MLA_HEADS, MLA_NOPE + MLA_V)
    k_nope, v = kv[..., :MLA_NOPE], kv[..., MLA_NOPE:]
    k_rope = apply_rope(k_r, cos, sin)
    scale = (MLA_NOPE + MLA_ROPE) ** -0.5

    def block(i):
        start = i * Q_BLOCK
        qn = lax.dynamic_slice_in_dim(q_nope, start, Q_BLOCK, axis=1)
        qr = lax.dynamic_slice_in_dim(q_rope, start, Q_BLOCK, axis=1)
        s = (jnp.einsum('bqhd,bkhd->bhqk', qn, k_nope)
             + jnp.einsum('bqhr,bkr->bhqk', qr, k_rope)).astype(jnp.float32) * scale
        s = jnp.where(chunk_causal_mask(start, S), s, -jnp.inf)
        p = jax.nn.softmax(s, axis=-1)
        return jnp.einsum('bhqk,bkhd->bqhd', p.astype(v.dtype), v)

    return sweep_query_blocks(block, S).reshape(B, S, MLA_WIDTH)


def diff_attention(h, pos, lam, sub_norm, lam_init, slopes):
    B, S, _ = h.shape
    qk_w = DIFF_HEADS * 2 * DIFF_QK
    q, k, v = jnp.split(h, [qk_w, 2 * qk_w], axis=-1)
    q = q.reshape(B, S, DIFF_HEADS, 2, DIFF_QK)
    k = k.reshape(B, S, DIFF_HEADS, 2, DIFF_QK)
    v = v.reshape(B, S, DIFF_HEADS, DIFF_V)
    lamf = lam.astype(jnp.float32)
    lam_full = (jnp.exp(jnp.sum(lamf[0] * lamf[1])) - jnp.exp(jnp.sum(lamf[2] * lamf[3]))
                + lam_init)
    scale = DIFF_QK ** -0.5
    posf = pos.astype(jnp.float32)

    def block(i):
        start = i * Q_BLOCK
        qb = lax.dynamic_slice_in_dim(q, start, Q_BLOCK, axis=1)
        pq = lax.dynamic_slice_in_dim(posf, start, Q_BLOCK, axis=1)
        s = jnp.einsum('bqhmd,bkhmd->bmhqk', qb, k).astype(jnp.float32) * scale
        dist = jnp.abs(pq[:, :, None] - posf[:, None, :])
        s = s - slopes[None, None, :, None, None] * dist[:, None, None]
        s = jnp.where(chunk_causal_mask(start, S), s, -jnp.inf)
        p = jax.nn.softmax(s, axis=-1)
        a = p[:, 0] - lam_full * p[:, 1]
        return jnp.einsum('bhqk,bkhd->bqhd', a.astype(v.dtype), v)

    o = sweep_query_blocks(block, S)
    o = rms_norm(o, sub_norm.reshape(DIFF_HEADS, DIFF_V)) * (1.0 - lam_init)
    return o.reshape(B, S, DIFF_WIDTH)


def chunk_attention(h, rel_bias):
    B, S, _ = h.shape
    nc = S // CHUNK
    q, k, v = [t.reshape(B, nc, CHUNK, CH_HEADS, CH_HD) for t in jnp.split(h, 3, axis=-1)]
    band_idx = jnp.arange(nc)[:, None] + jnp.arange(CH_LEFT + 1)[None, :]

    def band(t):
        tp = jnp.pad(t, ((0, 0), (CH_LEFT, 0), (0, 0), (0, 0), (0, 0)))
        return tp[:, band_idx].reshape(B, nc, CH_BAND, CH_HEADS, CH_HD)

    kb, vb = band(k), band(v)
    s = jnp.einsum('bnqhd,bnkhd->bnhqk', q, kb).astype(jnp.float32) * (CH_HD ** -0.5)
    rel = (CH_LEFT * CHUNK + jnp.arange(CHUNK))[:, None] - jnp.arange(CH_BAND)[None, :]
    rel_idx = jnp.clip(rel, -(CHUNK - 1), REL_MAX) + (CHUNK - 1)
    bias = rel_bias[:, rel_idx].astype(jnp.float32)
    key_chunk = jnp.arange(nc)[:, None] - CH_LEFT + (jnp.arange(CH_BAND) // CHUNK)[None, :]
    valid = key_chunk >= 0
    s = jnp.where(valid[None, :, None, None, :], s + bias[None, None], -jnp.inf)
    p = jax.nn.softmax(s, axis=-1)
    o = jnp.einsum('bnhqk,bnkhd->bnqhd', p.astype(vb.dtype), vb)
    return o.reshape(B, S, CH_WIDTH)


def conv_ffn(x, w_in, conv_w, conv_b, w_out):
    h = x @ w_in
    c = h.shape[-1]
    h = lax.conv_general_dilated(h, conv_w[:, None, :].astype(h.dtype), window_strides=(1,),
                                 padding=[(CONV_W - 1, 0)],
                                 dimension_numbers=('NWC', 'WIO', 'NWC'),
                                 feature_group_count=c) + conv_b
    u, g = jnp.split(h, 2, axis=-1)
    return (jax.nn.silu(g) * u) @ w_out


def setup_inputs(seed: int = 0) -> dict:
    key = jax.random.key(seed)
    ks = jax.random.split(key, 24)
    L, D = DEPTH, D_MODEL
    f32 = jnp.float32

    def w(k, shape, fan_in):
        return jax.random.normal(k, shape, f32) * fan_in ** -0.5

    def gain(k, shape):
        return 1.0 + 0.01 * jax.random.normal(k, shape, f32)

    x = jax.random.normal(ks[0], (BATCH, SEQ, D), f32)
    offsets = jax.random.randint(ks[1], (BATCH, 1), 0, 4096, dtype=jnp.int32)
    positions = offsets + jnp.arange(SEQ, dtype=jnp.int32)[None, :]
    return {
        'x': x,
        'positions': positions,
        'attn_norm': gain(ks[2], (L, D)),
        'w_in': w(ks[3], (L, D, IN_WIDTH), D),
        'mla_q_norm': gain(ks[4], (L, MLA_Q_RANK)),
        'mla_w_uq': w(ks[5], (L, MLA_Q_RANK, MLA_HEADS * (MLA_NOPE + MLA_ROPE)), MLA_Q_RANK),
        'mla_kv_norm': gain(ks[6], (L, MLA_KV_RANK)),
        'mla_w_ukv': w(ks[7], (L, MLA_KV_RANK, MLA_HEADS * (MLA_NOPE + MLA_V)), MLA_KV_RANK),
        'diff_lambda': 0.1 * jax.random.normal(ks[8], (L, 4, DIFF_QK), f32),
        'diff_norm': gain(ks[9], (L, DIFF_WIDTH)),
        'chunk_rel_bias': 0.1 * jax.random.normal(ks[10], (L, CH_HEADS, N_REL), f32),
        'mla_out_norm': gain(ks[11], (L, MLA_WIDTH)),
        'chunk_out_norm': gain(ks[12], (L, CH_WIDTH)),
        'w_out': w(ks[13], (L, MIX_WIDTH, D), MIX_WIDTH),
        'ffn_norm': gain(ks[14], (L, D)),
        'w_ffn_in': w(ks[15], (L, D, 2 * D_FF), D),
        'ffn_conv_w': w(ks[16], (L, CONV_W, 2 * D_FF), CONV_W),
        'ffn_conv_b': 0.01 * jax.random.normal(ks[17], (L, 2 * D_FF), f32),
        'w_ffn_out': w(ks[18], (L, D_FF, D), D_FF),
        'final_norm': gain(ks[19], (D,)),
    }


def reference(x, positions, attn_norm, w_in, mla_q_norm, mla_w_uq, mla_kv_norm, mla_w_ukv,
              diff_lambda, diff_norm, chunk_rel_bias, mla_out_norm, chunk_out_norm, w_out,
              ffn_norm, w_ffn_in, ffn_conv_w, ffn_conv_b, w_ffn_out, final_norm):
    slopes = alibi_slopes(DIFF_HEADS)
    for l in range(DEPTH):
        lam_init = 0.8 - 0.6 * math.exp(-0.3 * l)
        proj = rms_norm(x, attn_norm[l]) @ w_in[l]
        h_mla, h_diff, h_ch = jnp.split(proj, [IN_MLA, IN_MLA + IN_DIFF], axis=-1)
        o_mla = rms_norm(mla_attention(h_mla, positions, mla_q_norm[l], mla_w_uq[l],
                                       mla_kv_norm[l], mla_w_ukv[l]), mla_out_norm[l])
        o_diff = diff_attention(h_diff, positions, diff_lambda[l], diff_norm[l], lam_init, slopes)
        o_ch = rms_norm(chunk_attention(h_ch, chunk_rel_bias[l]), chunk_out_norm[l])
        x = x + jnp.concatenate([o_mla, o_diff, o_ch], axis=-1) @ w_out[l]
        x = x + conv_ffn(rms_norm(x, ffn_norm[l]), w_ffn_in[l], ffn_conv_w[l], ffn_conv_b[l],
                         w_ffn_out[l])
    return rms_norm(x, final_norm)
```

```python
import contextlib

ENGS = ("pe", "act", "dve", "pool", "sp")
ENGOBJ = {"pe": "tensor", "act": "scalar", "dve": "vector", "pool": "gpsimd", "sp": "sync"}


class Res:
    __slots__ = ("name", "lw", "rd", "slot", "base", "dcount", "excl")

    def __init__(self, name="", excl=False):
        self.name = name
        self.excl = excl
        self.lw = None
        self.rd = []
        self.slot = None
        self.base = 0
        self.dcount = 0


class Op:
    __slots__ = ("eng", "fn", "deps", "ddeps", "pos", "sig", "count", "dres", "kind")

    def __init__(self, eng, fn, kind):
        self.eng = eng
        self.fn = fn
        self.kind = kind
        self.deps = []
        self.ddeps = []
        self.sig = False
        self.count = 0
        self.dres = None


class Prog:
    def __init__(self, nc, stack, nslots=88, same_engine_sync=True):
        self.nc = nc
        self.same_engine_sync = same_engine_sync
        self.esem = {e: stack.enter_context(nc.semaphore("s_" + e)) for e in ENGS}
        self.ecount = {e: 0 for e in ENGS}
        self.slots = [[stack.enter_context(nc.semaphore("d%d" % i)), 0] for i in range(nslots)]
        self.free = list(range(nslots))
        self.total_ops = 0
        self.begin()

    def begin(self):
        self.streams = {e: [] for e in ENGS}
        self.dma_res = []

    def _add(self, op, reads, writes):
        if any(r.excl for r in reads):
            writes = list(writes) + [r for r in reads if r.excl and r not in writes]
            reads = [r for r in reads if not r.excl]
        deps = []
        for r in reads:
            if r.lw is not None:
                deps.append(r.lw)
        for w in writes:
            if w.lw is not None:
                deps.append(w.lw)
            deps.extend(w.rd)
        seen = set()
        for d in deps:
            if id(d) in seen or d is op:
                continue
            seen.add(id(d))
            if d.kind == "d":
                op.ddeps.append((d.dres, d.dres.dcount * 16))
            else:
                if d.eng == op.eng:
                    if d.eng == "pe":
                        continue
                    if not self.same_engine_sync:
                        continue
                op.deps.append(d)
        for r in reads:
            r.rd.append(op)
        for w in writes:
            w.lw = op
            w.rd = []
        op.pos = len(self.streams[op.eng])
        self.streams[op.eng].append(op)
        return op

    def op(self, eng, fn, reads=(), writes=(), mm=False):
        o = Op(eng, fn, "mm" if mm else "c")
        return self._add(o, reads, writes)

    def dma(self, queue, out_ap, in_ap, dst, reads=(), writes=(), **kw):
        if dst.slot is None:
            dst.slot = self.free.pop()
            dst.base = self.slots[dst.slot][1]
            dst.dcount = 0
            self.dma_res.append(dst)
        o = Op(queue, lambda e: e.dma_start(out=out_ap, in_=in_ap, **kw), "d")
        o.dres = dst
        self._add(o, reads, writes)
        dst.dcount += 1
        return o

    def end(self):
        nc = self.nc
        waited = {e: {f: -1 for f in ENGS} for e in ENGS}
        dwaited = {e: {} for e in ENGS}
        plan = {e: [] for e in ENGS}
        for e in ENGS:
            for op in self.streams[e]:
                best = {}
                for d in op.deps:
                    if d.pos > waited[e][d.eng]:
                        if d.eng not in best or d.pos > best[d.eng].pos:
                            best[d.eng] = d
                for f, d in best.items():
                    waited[e][f] = d.pos
                    d.sig = True
                dws = []
                for (res, cnt) in op.ddeps:
                    if dwaited[e].get(id(res), 0) < cnt:
                        dwaited[e][id(res)] = cnt
                        dws.append((res, cnt))
                plan[e].append((op, list(best.values()), dws))
        for e in ENGS:
            c = self.ecount[e]
            for op in self.streams[e]:
                if op.sig:
                    c += 1
                    op.count = c
            self.ecount[e] = c
        esem = self.esem
        slots = self.slots
        dma_res = list(self.dma_res)
        with nc.Block() as block:
            def run(e):
                def body(eng):
                    for (op, cw, dw) in plan[e]:
                        for d in cw:
                            eng.wait_ge(esem[d.eng], d.count)
                        for (res, cnt) in dw:
                            eng.wait_ge(slots[res.slot][0], res.base + cnt)
                        ins = op.fn(eng)
                        if op.kind == "d":
                            ins.then_inc(slots[op.dres.slot][0], 16)
                        elif op.sig:
                            ins.then_inc(esem[e], 1)
                    if e == "sp":
                        for r in dma_res:
                            eng.wait_ge(slots[r.slot][0], r.base + r.dcount * 16)
                return body

            for e in ENGS:
                getattr(block, ENGOBJ[e])(run(e))
        for r in dma_res:
            slots[r.slot][1] = r.base + r.dcount * 16
            self.free.append(r.slot)
            r.slot = None
            r.lw = None
            r.rd = []
        n = sum(len(self.streams[e]) for e in ENGS)
        self.total_ops += n
        self.begin()
        return n
import math
import numpy as np
import ml_dtypes
import concourse.bass as bass
import concourse.mybir as mybir
from concourse.bass_utils import run_bass_kernel_spmd

F32 = mybir.dt.float32
BF16 = mybir.dt.bfloat16
I32 = mybir.dt.int32
AF = mybir.ActivationFunctionType
ALU = mybir.AluOpType
AX = mybir.AxisListType

D = 1024
DEPTH = 2
EPS = 1e-6
NCOL_IN = 2368
DFF = 2816
TWO_PI = 2.0 * math.pi
NEG = -30000.0
SLOPES = [2.0 ** (-8.0 * (i + 1) / 6) for i in range(6)]
DUMMY_MLA = 256
DUMMY_FFN = 0


class Tl:
    __slots__ = ("t", "r")

    def __init__(self, t, r):
        self.t = t
        self.r = r


class K:
    def __init__(self, S, debug=False):
        self.S = S
        self.NT = S // 512
        self.NB = S // 128
        self.debug = debug
        self.nc = bass.Bass("TRN2", target_bir_lowering=False)
        self.gst = contextlib.ExitStack()
        self.P = Prog(self.nc, self.gst)
        self.st = None
        self.dres = {}
        self.uid = 0

    def din(self, name, shape, dt):
        return self.nc.dram_tensor(name, list(shape), dt, kind="ExternalInput").ap()

    def dscr(self, name, shape, dt):
        kind = "ExternalOutput" if self.debug else "Internal"
        return self.nc.dram_tensor(name, list(shape), dt, kind=kind).ap()

    def sb(self, name, shape, dt):
        self.uid += 1
        name = "sb%d_%s" % (self.uid, name)
        t = self.st.enter_context(self.nc.sbuf_tensor(name, list(shape), dt))
        return Tl(t, Res(name))

    def ps(self, name, shape, dt):
        self.uid += 1
        name = "ps%d_%s" % (self.uid, name)
        if dt == BF16 and list(shape) == [128, 512]:
            shape = [128, 1024]
        t = self.st.enter_context(self.nc.psum_tensor(name, list(shape), dt))
        return Tl(t, Res(name, excl=True))

    def dr(self, name):
        if name not in self.dres:
            self.dres[name] = Res(name)
        return self.dres[name]

    @contextlib.contextmanager
    def phase(self, name):
        self.st = contextlib.ExitStack()
        self.dres = {}
        with self.st:
            yield
            n = self.P.end()
        self.st = None

    def load(self, tl, dst_ap, src_ap, src_name=None, q="sp", **kw):
        reads = [self.dr(src_name)] if src_name else []
        self.P.dma(q, dst_ap, src_ap, tl.r, reads=reads, writes=[tl.r], **kw)

    def store(self, dname, dst_ap, tl, src_ap, q="pool", **kw):
        r = self.dr(dname)
        self.P.dma(q, dst_ap, src_ap, r, reads=[tl.r], writes=[r], **kw)

    def mm(self, out, lhsT, rhs, start, stop, reads, writes):
        self.P.op("pe", lambda e: e.matmul(out, lhsT=lhsT, rhs=rhs, start=start, stop=stop),
                  reads=reads, writes=writes, mm=True)

    def tr(self, out, in_, ident, reads, writes):
        self.P.op("pe", lambda e: e.transpose(out, in_, ident), reads=reads, writes=writes, mm=True)

    def act(self, out, in_, func, reads, writes, bias=None, scale=None, accum=None):
        kw = {}
        if bias is not None:
            kw["bias"] = bias
        if scale is not None:
            kw["scale"] = scale
        if accum is not None:
            kw["accum_out"] = accum
        self.P.op("act", lambda e: e.activation(out=out, in_=in_, func=func, **kw), reads=reads, writes=writes)

    def ts(self, eng, out, in0, s1, s2, op0, op1, reads, writes):
        if op1 is None:
            self.P.op(eng, lambda e: e.tensor_scalar(out=out, in0=in0, scalar1=s1, scalar2=None, op0=op0),
                      reads=reads, writes=writes)
        else:
            self.P.op(eng, lambda e: e.tensor_scalar(out=out, in0=in0, scalar1=s1, scalar2=s2, op0=op0, op1=op1),
                      reads=reads, writes=writes)

    def tt(self, eng, out, in0, in1, op, reads, writes):
        self.P.op(eng, lambda e: e.tensor_tensor(out=out, in0=in0, in1=in1, op=op), reads=reads, writes=writes)

    def stt(self, eng, out, in0, scalar, in1, op0, op1, reads, writes):
        self.P.op(eng, lambda e: e.scalar_tensor_tensor(out=out, in0=in0, scalar=scalar, in1=in1, op0=op0, op1=op1),
                  reads=reads, writes=writes)

    def cp(self, eng, out, in_, reads, writes):
        if eng == "act":
            self.P.op(eng, lambda e: e.copy(out=out, in_=in_), reads=reads, writes=writes)
        else:
            self.P.op(eng, lambda e: e.tensor_copy(out=out, in_=in_), reads=reads, writes=writes)

    def memset(self, eng, ap, val, writes):
        self.P.op(eng, lambda e: e.memset(ap, val), reads=(), writes=writes)

    def recip(self, out, in_, reads, writes):
        self.P.op("dve", lambda e: e.reciprocal(out=out, in_=in_), reads=reads, writes=writes)

    def rstd(self, out, ss, n, epsb, reads, writes):
        self.act(out, ss, AF.Ln, reads=list(reads) + [epsb.r], writes=writes, bias=epsb.t[:, 0:1], scale=1.0 / n)
        self.act(out, out, AF.Exp, reads=writes, writes=writes, scale=-0.5)

    def load_w(self, wt, dst3, src2, nchunk, ncol, stg, src_name=None):
        CW = 2048
        k = 0
        for c in range(nchunk):
            for c0 in range(0, ncol, CW):
                w = min(CW, ncol - c0)
                s = stg[k % len(stg)]
                k += 1
                self.load(s, s.t[:, 0:w], src2[c * 128:(c + 1) * 128, c0:c0 + w])
                self.cp("pool", dst3[:, c, c0:c0 + w], s.t[:, 0:w], reads=[s.r], writes=[wt.r])

    def declare(self):
        S = self.S
        self.x_in = self.din("x", [S, D], F32)
        self.pos_in = self.din("pos", [S // 128, 128], I32)
        self.identf_in = self.din("identf", [128, 128], F32)
        self.invf_in = self.din("invf", [128, 2], F32)
        self.cmask_in = self.din("cmask", [8, 128, 512], F32)
        self.w_in = self.din("w_in", [DEPTH, D, NCOL_IN], F32)
        self.w_uq = self.din("w_uq", [DEPTH, 256, 768], F32)
        self.w_ukv = self.din("w_ukv", [DEPTH, 128, 768], F32)
        self.w_out = self.din("w_out", [DEPTH, D, D], F32)
        self.w_f1 = self.din("w_f1", [DEPTH, D, 2 * DFF], F32)
        self.w_f2 = self.din("w_f2", [DEPTH, DFF, D], F32)
        self.g_attn = self.din("g_attn", [DEPTH, 128, 8], F32)
        self.g_ffn = self.din("g_ffn", [DEPTH, 128, 8], F32)
        self.g_q = self.din("g_q", [DEPTH, 128, 2], F32)
        self.g_kv = self.din("g_kv", [DEPTH, 128, 1], F32)
        self.g_o = self.din("g_o", [DEPTH, 128, 8], F32)
        self.lam_in = self.din("lam", [DEPTH, 1, 128], F32)
        self.rb_in = self.din("rb", [DEPTH, 4, 320], F32)
        self.cw_in = self.din("cw", [DEPTH, 128, 44 * 3], F32)
        self.cb_in = self.din("cb", [DEPTH, 128, 44], F32)
        self.g_fin = self.din("g_fin", [1, D], F32)
        self.out = self.nc.dram_tensor("out", [S, D], F32, kind="ExternalOutput").ap()
        self.posf = self.dscr("posf", [1, S], F32)
        self.negposk = self.dscr("negposk", [128, S // 128], F32)
        self.cosT = self.dscr("cosT", [self.NT, 128, 512], F32)
        self.sinT = self.dscr("sinT", [self.NT, 128, 512], F32)
        self.QTn_mla = self.dscr("QTn_mla", [6, 64, S], BF16)
        self.QTr_mla = self.dscr("QTr_mla", [6, 32, S], BF16)
        self.KTn_mla = self.dscr("KTn_mla", [6, 64, S], BF16)
        self.KTr_mla = self.dscr("KTr_mla", [32, S], BF16)
        self.V_all = self.dscr("V_all", [16, S, 128], BF16)
        self.QT_diff = self.dscr("QT_diff", [6, 64, S], BF16)
        self.KT_diff = self.dscr("KT_diff", [6, 64, S], BF16)
        self.QT_ch = self.dscr("QT_ch", [4, 64, S], BF16)
        self.KT_ch = self.dscr("KT_ch", [4, 64, S], BF16)
        self.O_tok = self.dscr("O_tok", [S, D], F32)
        self.x1 = self.dscr("x1", [S, D], F32)
        self.x2 = self.dscr("x2", [S, D], F32)
        self.ext = self.dscr("ext", [4, 128, 1536], F32)

    def p0(self):
        S, NB, NT = self.S, self.NB, self.NT
        with self.phase("p0"):
            idf = self.sb("idf", [128, 128], F32)
            self.load(idf, idf.t[:], self.identf_in)
            invf = self.sb("invf", [128, 2], F32)
            self.load(invf, invf.t[:], self.invf_in)
            pi = self.sb("pi", [NB, 128], I32)
            self.load(pi, pi.t[:], self.pos_in)
            pf = self.sb("pf", [NB, 128], F32)
            self.cp("dve", pf.t[:], pi.t[:], reads=[pi.r], writes=[pf.r])
            self.store("posf", self.posf.rearrange("o (j p) -> (o j) p", p=128), pf, pf.t[:])
            pst = self.ps("pst", [128, 512], F32)
            self.tr(pst.t[:, 0:NB], pf.t[0:NB, :], idf.t[0:NB, 0:NB], reads=[pf.r, idf.r], writes=[pst.r])
            npk = self.sb("npk", [128, NB], F32)
            self.ts("dve", npk.t[:], pst.t[:, 0:NB], -1.0, None, ALU.mult, None, reads=[pst.r], writes=[npk.r])
            self.store("negposk", self.negposk, npk, npk.t[:])
            pbc = [self.sb("pbc%d" % i, [128, 512], F32) for i in range(2)]
            ang = self.sb("ang", [128, 512], F32)
            ni = self.sb("ni", [128, 512], I32)
            nf = self.sb("nf", [128, 512], F32)
            y = self.sb("y", [128, 512], F32)
            res = [self.sb("res%d" % i, [128, 512], F32) for i in range(2)]
            k = 0
            for t in range(NT):
                pb = pbc[t % 2]
                self.load(pb, pb.t[:], self.posf[:, t * 512:(t + 1) * 512].partition_broadcast(128), src_name="posf")
                for which in range(2):
                    col = invf.t[:, which:which + 1]
                    if which == 0:
                        self.ts("dve", ang.t[:], pb.t[:], col, math.pi / 2, ALU.mult, ALU.add, reads=[pb.r, invf.r], writes=[ang.r])
                    else:
                        self.ts("dve", ang.t[:], pb.t[:], col, None, ALU.mult, None, reads=[pb.r, invf.r], writes=[ang.r])
                    self.ts("dve", ni.t[:], ang.t[:], 1.0 / TWO_PI, None, ALU.mult, None, reads=[ang.r], writes=[ni.r])
                    self.cp("dve", nf.t[:], ni.t[:], reads=[ni.r], writes=[nf.r])
                    self.stt("dve", y.t[:], nf.t[:], -TWO_PI, ang.t[:], ALU.mult, ALU.add, reads=[nf.r, ang.r], writes=[y.r])
                    self.ts("dve", nf.t[:], y.t[:], math.pi, -TWO_PI, ALU.is_gt, ALU.mult, reads=[y.r], writes=[nf.r])
                    self.tt("dve", y.t[:], y.t[:], nf.t[:], ALU.add, reads=[y.r, nf.r], writes=[y.r])
                    self.ts("dve", nf.t[:], y.t[:], -math.pi, TWO_PI, ALU.is_lt, ALU.mult, reads=[y.r], writes=[nf.r])
                    self.tt("dve", y.t[:], y.t[:], nf.t[:], ALU.add, reads=[y.r, nf.r], writes=[y.r])
                    self.ts("dve", y.t[:], y.t[:], math.pi, -math.pi, ALU.min, ALU.max, reads=[y.r], writes=[y.r])
                    r_ = res[k % 2]
                    k += 1
                    self.act(r_.t[:], y.t[:], AF.Sin, reads=[y.r], writes=[r_.r])
                    dst = self.cosT if which == 0 else self.sinT
                    self.store("cosT" if which == 0 else "sinT", dst[t], r_, r_.t[:])

    def norm_T(self, xt, ns, gT, xn, xnT, pT, idb, epsb, ss, junk, evac_engs=("act", "dve")):
        for s in range(ns):
            self.act(junk.t[:], xt.t[:, s, :], AF.Square, reads=[xt.r], writes=[junk.r, ss.r], accum=ss.t[:, s:s + 1])
        self.rstd(ss.t[:, 0:ns], ss.t[:, 0:ns], float(D), epsb, reads=[ss.r], writes=[ss.r])
        for s in range(ns):
            self.ts("dve", xn.t[:, s, :], xt.t[:, s, :], ss.t[:, s:s + 1], None, ALU.mult, None, reads=[xt.r, ss.r], writes=[xn.r])
        for c in range(8):
            p = pT[c % len(pT)]
            for s in range(ns):
                self.tr(p.t[:, s * 128:(s + 1) * 128], xn.t[:, s, c * 128:(c + 1) * 128], idb.t[:], reads=[xn.r, idb.r], writes=[p.r])
            eng = evac_engs[c % len(evac_engs)]
            if eng == "act":
                self.act(xnT.t[:, c, :], p.t[:, 0:ns * 128], AF.Copy, reads=[p.r, gT.r], writes=[xnT.r], scale=gT.t[:, c:c + 1])
            else:
                self.ts("dve", xnT.t[:, c, :], p.t[:, 0:ns * 128], gT.t[:, c:c + 1], None, ALU.mult, None, reads=[p.r, gT.r], writes=[xnT.r])

    def p1(self, l, xsrc, xsrc_name):
        S, NT = self.S, self.NT
        with self.phase("p1"):
            epsb = self.sb("epsb", [128, 2], F32)
            self.memset("pool", epsb.t[:, 0:1], EPS, writes=[epsb.r])
            self.memset("pool", epsb.t[:, 1:2], -0.5, writes=[epsb.r])
            idf = self.sb("idf", [128, 128], F32)
            self.load(idf, idf.t[:], self.identf_in)
            idb = self.sb("idb", [128, 128], BF16)
            self.cp("dve", idb.t[:], idf.t[:], reads=[idf.r], writes=[idb.r])
            gT = self.sb("gT", [128, 8], F32)
            self.load(gT, gT.t[:], self.g_attn[l])
            gq = self.sb("gq", [128, 2], F32)
            self.load(gq, gq.t[:], self.g_q[l])
            gkv = self.sb("gkv", [128, 1], F32)
            self.load(gkv, gkv.t[:], self.g_kv[l])
            stg = [self.sb("stg%d" % i, [128, 2048], F32) for i in range(2)]
            win = self.sb("win", [128, 8, NCOL_IN], BF16)
            self.load_w(win, win.t, self.w_in[l], 8, NCOL_IN, stg)
            wuq = self.sb("wuq", [128, 2, 768], BF16)
            self.load_w(wuq, wuq.t, self.w_uq[l], 2, 768, stg)
            wukv = self.sb("wukv", [128, 1, 768], BF16)
            self.load_w(wukv, wukv.t, self.w_ukv[l], 1, 768, stg)
            xt = [self.sb("xt%d" % i, [128, 4, D], F32) for i in range(2)]
            xns = [self.sb("xn%d" % i, [128, 4, D], BF16) for i in range(2)]
            xnTs = [self.sb("xnT%d" % i, [128, 8, 512], BF16) for i in range(2)]
            sss = [self.sb("ss%d" % i, [128, 4], F32) for i in range(2)]
            junks = [self.sb("junk%d" % i, [128, D], BF16) for i in range(2)]
            cs = [self.sb("cs%d" % i, [128, 512], F32) for i in range(2)]
            sn = [self.sb("sn%d" % i, [128, 512], F32) for i in range(2)]
            pT = [self.ps("pT%d" % i, [128, 512], BF16) for i in range(2)]
            pA = [self.ps("pA%d" % i, [128, 512], F32) for i in range(4)]
            pB = [self.ps("pB%d" % i, [128, 512], F32) for i in range(2)]
            vst = [self.sb("vst%d" % i, [128, 16, 128], BF16) for i in range(4)]
            for v in vst:
                self.memset("pool", v.t[:, :, 64:128], 1.0, writes=[v.r])
            cqns = [self.sb("cqn%d" % i, [128, 384], BF16) for i in range(2)]
            cqTs = [self.sb("cqT%d" % i, [128, 2, 512], BF16) for i in range(2)]
            ckvTs = [self.sb("ckvT%d" % i, [128, 512], BF16) for i in range(2)]
            ss2s = [self.sb("ss2_%d" % i, [128, 2], F32) for i in range(2)]
            fst = [self.sb("fst%d" % i, [128, 512], BF16) for i in range(4)]
            rt1 = self.sb("rt1", [128, 512], F32)
            rt2 = self.sb("rt2", [128, 512], F32)
            fk = [0]
            sq = [0]

            def fm_out(ps_ap, rows, scale, dsts, preads):
                f = fst[fk[0] % 4]
                fk[0] += 1
                if fk[0] % 2 == 0:
                    self.act(f.t[0:rows, :], ps_ap, AF.Copy, reads=preads, writes=[f.r], scale=scale)
                else:
                    self.ts("dve", f.t[0:rows, :], ps_ap, scale, None, ALU.mult, None, reads=preads, writes=[f.r])
                for (dn, dap, r0, r1) in dsts:
                    sq[0] += 1
                    self.store(dn, dap, f, f.t[r0:r1, :], q=("sp" if sq[0] % 2 else "pool"))

            def tile_loads(t):
                x = xt[t % 2]
                tok = slice(t * 512, (t + 1) * 512)
                self.load(x, x.t[:], xsrc[tok, :].rearrange("(s p) c -> p s c", p=128), src_name=xsrc_name)
                self.load(cs[t % 2], cs[t % 2].t[:], self.cosT[t], src_name="cosT")
                self.load(sn[t % 2], sn[t % 2].t[:], self.sinT[t], src_name="sinT")

            tile_loads(0)
            for t in range(NT):
                x = xt[t % 2]
                tok = slice(t * 512, (t + 1) * 512)
                c_ = cs[t % 2]
                s_ = sn[t % 2]
                if t + 1 < NT:
                    tile_loads(t + 1)
                xn, xnT, ss, junk = xns[t % 2], xnTs[t % 2], sss[t % 2], junks[t % 2]
                cqT, ckvT = cqTs[t % 2], ckvTs[t % 2]
                self.norm_T(x, 4, gT, xn, xnT, pT, idb, epsb, ss, junk)
                for s in range(4):
                    pa0 = pA[(2 * s) % 4]
                    pa1 = pA[(2 * s + 1) % 4]
                    for hf, pa in ((0, pa0), (1, pa1)):
                        for c in range(8):
                            self.mm(pa.t[:], xnT.t[:, c, s * 128:(s + 1) * 128], win.t[:, c, hf * 512:(hf + 1) * 512],
                                    c == 0, c == 7, reads=[xnT.r, win.r], writes=[pa.r])
                    cqn, ss2 = cqns[s % 2], ss2s[s % 2]
                    V = vst[s]
                    junk = junks[(s + 1) % 2]
                    self.act(junk.t[:, 0:256], pa0.t[:, 0:256], AF.Square, reads=[pa0.r], writes=[junk.r, ss2.r], accum=ss2.t[:, 0:1])
                    self.act(junk.t[:, 0:128], pa0.t[:, 256:384], AF.Square, reads=[pa0.r], writes=[junk.r, ss2.r], accum=ss2.t[:, 1:2])
                    self.rstd(ss2.t[:, 0:1], ss2.t[:, 0:1], 256.0, epsb, reads=[ss2.r], writes=[ss2.r])
                    self.rstd(ss2.t[:, 1:2], ss2.t[:, 1:2], 128.0, epsb, reads=[ss2.r], writes=[ss2.r])
                    self.ts("dve", cqn.t[:, 0:256], pa0.t[:, 0:256], ss2.t[:, 0:1], None, ALU.mult, None, reads=[pa0.r, ss2.r], writes=[cqn.r])
                    self.ts("dve", cqn.t[:, 256:384], pa0.t[:, 256:384], ss2.t[:, 1:2], None, ALU.mult, None, reads=[pa0.r, ss2.r], writes=[cqn.r])
                    self.cp("act", V.t[:, 6:8, 0:64], pa0.t[:, 384:512].rearrange("p (h c) -> p h c", c=64), reads=[pa0.r], writes=[V.r])
                    self.cp("act", V.t[:, 8:16, 0:64], pa1.t[:, 0:512].rearrange("p (h c) -> p h c", c=64), reads=[pa1.r], writes=[V.r])
                    p = pT[s % 2]
                    for j in range(3):
                        self.tr(p.t[:, j * 128:(j + 1) * 128], cqn.t[:, j * 128:(j + 1) * 128], idb.t[:], reads=[cqn.r, idb.r], writes=[p.r])
                    for j in range(2):
                        self.ts("dve", cqT.t[:, j, s * 128:(s + 1) * 128], p.t[:, j * 128:(j + 1) * 128], gq.t[:, j:j + 1], None, ALU.mult, None,
                                reads=[p.r, gq.r], writes=[cqT.r])
                    self.ts("dve", ckvT.t[:, s * 128:(s + 1) * 128], p.t[:, 256:384], gkv.t[:, 0:1], None, ALU.mult, None,
                            reads=[p.r, gkv.r], writes=[ckvT.r])
                    pb = pB[s % 2]
                    self.mm(pb.t[:, 0:384], ckvT.t[:, s * 128:(s + 1) * 128], wukv.t[:, 0, 384:768], True, True, reads=[ckvT.r, wukv.r], writes=[pb.r])
                    self.cp("act", V.t[:, 0:6, 0:64], pb.t[:, 0:384].rearrange("p (h c) -> p h c", c=64), reads=[pb.r], writes=[V.r])
                    t0 = t * 512 + s * 128
                    self.store("V_all", self.V_all[:, t0:t0 + 128, :].rearrange("h p c -> p h c"), V, V.t[:, :, :], q="pool")
                pk = [0]

                def fm_mm(col0, m, rhsT, nchunk, wt, wsel):
                    pa = pA[pk[0] % 4]
                    pk[0] += 1
                    for c in range(nchunk):
                        self.mm(pa.t[0:m, :], wsel(c, col0, m), rhsT(c), c == 0, c == nchunk - 1, reads=[wt.r, xnT.r, cqT.r, ckvT.r], writes=[pa.r])
                    return pa

                w_in_sel = lambda c, col0, m: win.t[:, c, col0:col0 + m]
                x_rhs = lambda c: xnT.t[:, c, :]
                sc_d = 32 ** -0.5
                sc_c = 64 ** -0.5
                sc_m = 96 ** -0.5
                for j in range(3):
                    pa = fm_mm(1024 + j * 128, 128, x_rhs, 8, win, w_in_sel)
                    fm_out(pa.t[:, :], 128, sc_d, [("QT_diff", self.QT_diff[2 * j:2 * j + 2, :, tok].rearrange("h r c -> (h r) c"), 0, 128)], [pa.r])
                for j in range(3):
                    pa = fm_mm(1408 + j * 128, 128, x_rhs, 8, win, w_in_sel)
                    fm_out(pa.t[:, :], 128, 1.0, [("KT_diff", self.KT_diff[2 * j:2 * j + 2, :, tok].rearrange("h r c -> (h r) c"), 0, 128)], [pa.r])
                for j in range(2):
                    pa = fm_mm(1792 + j * 128, 128, x_rhs, 8, win, w_in_sel)
                    fm_out(pa.t[:, :], 128, sc_c, [("QT_ch", self.QT_ch[2 * j:2 * j + 2, :, tok].rearrange("h r c -> (h r) c"), 0, 128)], [pa.r])
                for j in range(2):
                    pa = fm_mm(2048 + j * 128, 128, x_rhs, 8, win, w_in_sel)
                    fm_out(pa.t[:, :], 128, 1.0, [("KT_ch", self.KT_ch[2 * j:2 * j + 2, :, tok].rearrange("h r c -> (h r) c"), 0, 128)], [pa.r])
                paA = fm_mm(2304, 32, x_rhs, 8, win, w_in_sel)
                paB = fm_mm(2336, 32, x_rhs, 8, win, w_in_sel)
                self.tt("dve", rt1.t[0:32, :], paA.t[0:32, :], c_.t[0:32, :], ALU.mult, reads=[paA.r, c_.r], writes=[rt1.r])
                self.tt("dve", rt2.t[0:32, :], paB.t[0:32, :], s_.t[0:32, :], ALU.mult, reads=[paB.r, s_.r], writes=[rt2.r])
                f = fst[fk[0] % 4]
                fk[0] += 1
                self.tt("dve", f.t[0:32, :], rt1.t[0:32, :], rt2.t[0:32, :], ALU.add, reads=[rt1.r, rt2.r], writes=[f.r])
                self.store("KTr_mla", self.KTr_mla[:, tok], f, f.t[0:32, :], q="sp")
                wuq_sel = lambda c, col0, m: wuq.t[:, c, col0:col0 + m]
                cq_rhs = lambda c: cqT.t[:, c, :]
                for j in range(3):
                    pa = fm_mm(j * 128, 128, cq_rhs, 2, wuq, wuq_sel)
                    fm_out(pa.t[:, :], 128, sc_m, [("QTn_mla", self.QTn_mla[2 * j:2 * j + 2, :, tok].rearrange("h r c -> (h r) c"), 0, 128)], [pa.r])
                for (h0, nh) in ((0, 4), (4, 2)):
                    m = nh * 32
                    paA = fm_mm(384 + h0 * 32, m, cq_rhs, 2, wuq, wuq_sel)
                    paB = fm_mm(576 + h0 * 32, m, cq_rhs, 2, wuq, wuq_sel)
                    self.stt("dve", rt1.t[0:m, :], paA.t[0:m, :], sc_m, c_.t[0:m, :], ALU.mult, ALU.mult, reads=[paA.r, c_.r], writes=[rt1.r])
                    self.stt("dve", rt2.t[0:m, :], paB.t[0:m, :], sc_m, s_.t[0:m, :], ALU.mult, ALU.mult, reads=[paB.r, s_.r], writes=[rt2.r])
                    f = fst[fk[0] % 4]
                    fk[0] += 1
                    self.tt("dve", f.t[0:m, :], rt1.t[0:m, :], rt2.t[0:m, :], ALU.add, reads=[rt1.r, rt2.r], writes=[f.r])
                    self.store("QTr_mla", self.QTr_mla[h0:h0 + nh, :, tok].rearrange("h r c -> (h r) c"), f, f.t[0:m, :], q="sp")
                wukv_sel = lambda c, col0, m: wukv.t[:, 0, col0:col0 + m]
                ckv_rhs = lambda c: ckvT.t[:, :]
                for j in range(3):
                    pa = fm_mm(j * 128, 128, ckv_rhs, 1, wukv, wukv_sel)
                    fm_out(pa.t[:, :], 128, 1.0, [("KTn_mla", self.KTn_mla[2 * j:2 * j + 2, :, tok].rearrange("h r c -> (h r) c"), 0, 128)], [pa.r])

    def attn(self, l, kind):
        S, NT, NB = self.S, self.NT, self.NB
        nh = {"mla": 6, "diff": 6, "ch": 4}[kind]
        Kd = {"mla": 96, "diff": 64, "ch": 64}[kind]
        QT_s = {"mla": None, "diff": self.QT_diff, "ch": self.QT_ch}[kind]
        KT_s = {"mla": None, "diff": self.KT_diff, "ch": self.KT_ch}[kind]
        vbase = {"mla": 0, "diff": 6, "ch": 12}[kind]
        colbase = {"mla": 0, "diff": 384, "ch": 768}[kind]
        nmaps = 2 if kind == "diff" else 1
        lam_init = 0.8 - 0.6 * math.exp(-0.3 * l)
        with self.phase("attn_" + kind):
            idf = self.sb("idf", [128, 128], F32)
            self.load(idf, idf.t[:], self.identf_in)
            qts = [self.sb("qt%d" % i, [Kd, S], BF16) for i in range(2)]
            kts = [self.sb("kt%d" % i, [Kd, S], BF16) for i in range(2)]
            vts = [self.sb("vt%d" % i, [128, NB, 128], BF16) for i in range(2)]
            if kind == "diff":
                Sp = [self.ps("Sp%d" % i, [128, 2, 512], F32) for i in range(2)]
            else:
                Sp = [self.ps("Sp%d" % i, [128, 512], F32) for i in range(3)]
                Dm = self.ps("Dm", [128, 512], F32)
            Ac = [self.ps("Ac%d" % i, [128, 512], F32) for i in range(2)]
            Tp = [self.ps("Tp%d" % i, [128, 4, 128], F32) for i in range(2)]
            if kind == "diff":
                Pt = [self.sb("Pt%d" % i, [128, 2, 512], BF16) for i in range(3)]
            else:
                Pt = [self.sb("Pt%d" % i, [128, 512], BF16) for i in range(4)]
            accS = [self.sb("accS%d" % i, [128, 512], F32) for i in range(2)]
            ost = [self.sb("ost%d" % i, [128, 4, 64], F32) for i in range(2)]
            rc = [self.sb("rc%d" % i, [128, 4, 1], F32) for i in range(2)]
            if kind != "mla":
                if kind == "diff":
                    Tt = [self.sb("Tt%d" % i, [128, 2, 512], F32) for i in range(3)]
                else:
                    Tt = [self.sb("Tt%d" % i, [128, 512], F32) for i in range(4)]
            if kind == "diff":
                posq = self.sb("posq", [128, S], F32)
                for c0 in range(0, S, 2048):
                    w = min(2048, S - c0)
                    self.load(posq, posq.t[:, c0:c0 + w], self.posf[:, c0:c0 + w].partition_broadcast(128))
                npk = self.sb("npk", [128, NB], F32)
                self.load(npk, npk.t[:], self.negposk)
                Dt = [self.sb("Dt%d" % i, [128, 512], F32) for i in range(2)]
                zcol = self.sb("zcol", [128, 1], F32)
                self.memset("pool", zcol.t[:], 0.0, writes=[zcol.r])
                tmp = self.sb("tmp", [128, 4, 64], F32)
                lt = self.sb("lt", [128, 128], F32)
                self.load(lt, lt.t[:], self.lam_in[l].partition_broadcast(128))
                lp = self.sb("lp", [128, 64], F32)
                ls = self.sb("ls", [128, 2], F32)
                lam = self.sb("lam", [128, 1], F32)
                self.tt("dve", lp.t[:, 0:32], lt.t[:, 0:32], lt.t[:, 32:64], ALU.mult, reads=[lt.r], writes=[lp.r])
                self.tt("dve", lp.t[:, 32:64], lt.t[:, 64:96], lt.t[:, 96:128], ALU.mult, reads=[lt.r], writes=[lp.r])
                self.P.op("dve", lambda e: e.reduce_sum(out=ls.t[:, 0:1], in_=lp.t[:, 0:32], axis=AX.X), reads=[lp.r], writes=[ls.r])
                self.P.op("dve", lambda e: e.reduce_sum(out=ls.t[:, 1:2], in_=lp.t[:, 32:64], axis=AX.X), reads=[lp.r], writes=[ls.r])
                self.act(ls.t[:], ls.t[:], AF.Exp, reads=[ls.r], writes=[ls.r])
                self.tt("dve", lam.t[:], ls.t[:, 0:1], ls.t[:, 1:2], ALU.subtract, reads=[ls.r], writes=[lam.r])
                self.ts("dve", lam.t[:], lam.t[:], lam_init, None, ALU.add, None, reads=[lam.r], writes=[lam.r])
            if kind == "ch":
                cm = self.sb("cm", [128, 8, 512], F32)
                for i in range(8):
                    self.load(cm, cm.t[:, i, :], self.cmask_in[i])
                Bt = [self.sb("Bt%d" % i, [128, 8, 512], F32) for i in range(2)]
                rbt = self.sb("rbt", [128, 320], F32)
                ex = self.sb("ex", [128, 1536], F32)
                for hh in range(4):
                    self.load(rbt, rbt.t[:], self.rb_in[l, hh:hh + 1, :].partition_broadcast(128))
                    self.cp("dve", ex.t[:, 0:448], rbt.t[:, 0:1].to_broadcast([128, 448]), reads=[rbt.r], writes=[ex.r])
                    self.cp("dve", ex.t[:, 448:768], rbt.t[:, :], reads=[rbt.r], writes=[ex.r])
                    self.cp("dve", ex.t[:, 768:1536], rbt.t[:, 319:320].to_broadcast([128, 768]), reads=[rbt.r], writes=[ex.r])
                    self.store("ext", self.ext[hh], ex, ex.t[:])
            LAG = 2
            NS = len(Sp)

            def head_loads(h):
                qt_, kt_, vt_ = qts[h % 2], kts[h % 2], vts[h % 2]
                if kind == "mla":
                    self.load(qt_, qt_.t[0:64, :], self.QTn_mla[h])
                    self.load(qt_, qt_.t[64:96, :], self.QTr_mla[h])
                    self.load(kt_, kt_.t[0:64, :], self.KTn_mla[h])
                    self.load(kt_, kt_.t[64:96, :], self.KTr_mla)
                else:
                    self.load(qt_, qt_.t[:], QT_s[h, 0:Kd, :])
                    self.load(kt_, kt_.t[:], KT_s[h, 0:Kd, :])
                for j0 in range(0, NB, 16):
                    j1 = min(NB, j0 + 16)
                    self.load(vt_, vt_.t[:, j0:j1, :], self.V_all[vbase + h, j0 * 128:j1 * 128, :].rearrange("(j p) c -> p j c", p=128))
                if kind == "ch":
                    B = Bt[h % 2]
                    for i in range(8):
                        delta = 512 - 128 * i
                        src = bass.AP(self.ext.tensor, h * 128 * 1536 + delta + 511, [[1535, 128], [1, 512]])
                        self.load(B, B.t[:, i, :], src, src_name="ext")
                    self.tt("pool", B.t[:], B.t[:], cm.t[:], ALU.add, reads=[B.r, cm.r], writes=[B.r])

            steps = []
            for h in range(nh):
                hfirst = len(steps)
                for qt in range(NT):
                    if kind == "ch":
                        kbs = list(range(max(0, 4 * qt - 4), 4 * qt + 4))
                    else:
                        kbs = list(range(0, 4 * qt + 4))
                    for idx, kb in enumerate(kbs):
                        for m in range(nmaps):
                            steps.append(dict(h=h, qt=qt, kb=kb, m=m, first=(idx == 0), last=(idx == len(kbs) - 1),
                                              hstep=len(steps) - hfirst, i=len(steps)))
            cur = {}

            def stageA(st):
                h, qt, kb, m = st["h"], st["qt"], st["kb"], st["m"]
                if st["i"] == 0:
                    head_loads(0)
                if st["hstep"] == LAG and h + 1 < nh:
                    head_loads(h + 1)
                qt_, kt_ = qts[h % 2], kts[h % 2]
                q0 = qt * 512
                j = kb - 4 * qt
                c0 = 128 * j if (j > 0 and kind != "ch") else 0
                st["c0"] = c0
                if kind == "diff" and m == 0:
                    Dk = Dt[(st["i"] // 2) % 2]
                    cur["Dk"] = Dk
                    self.act(Dk.t[:, c0:512], posq.t[:, q0 + c0:q0 + 512], AF.Abs, reads=[posq.r, npk.r], writes=[Dk.r],
                             bias=npk.t[:, kb:kb + 1], scale=1.0)
                sp = Sp[st["i"] % NS]
                if kind == "diff":
                    r0, r1 = 32 * m, 32 * m + 32
                else:
                    r0, r1 = 0, Kd
                self.mm(sp.t[:, c0:512], kt_.t[r0:r1, kb * 128:(kb + 1) * 128], qt_.t[r0:r1, q0 + c0:q0 + 512], True, True,
                        reads=[kt_.r, qt_.r], writes=[sp.r])
                if DUMMY_MLA > 0:
                    self.mm(Dm.t[:, 0:DUMMY_MLA], kt_.t[r0:r1, kb * 128:(kb + 1) * 128], qt_.t[r0:r1, q0:q0 + DUMMY_MLA], True, True,
                            reads=[kt_.r, qt_.r], writes=[Dm.r])
                pt = Pt[st["i"] % len(Pt)]
                st["pt"] = pt
                if kind == "mla":
                    self.act(pt.t[:, c0:512], sp.t[:, c0:512], AF.Exp, reads=[sp.r], writes=[pt.r])
                else:
                    tt_ = Tt[st["i"] % len(Tt)]
                    if kind == "diff":
                        Dk = cur["Dk"]
                        self.stt("dve", tt_.t[:, c0:512], Dk.t[:, c0:512], -SLOPES[h], sp.t[:, c0:512], ALU.mult, ALU.add,
                                 reads=[Dk.r, sp.r], writes=[tt_.r])
                    else:
                        i = kb - (4 * qt - 4)
                        B = Bt[h % 2]
                        self.tt("dve", tt_.t[:, :], B.t[:, i, :], sp.t[:, :], ALU.add, reads=[B.r, sp.r], writes=[tt_.r])
                    self.act(pt.t[:, c0:512], tt_.t[:, c0:512], AF.Exp, reads=[tt_.r], writes=[pt.r])
                if j >= 0 and kind != "ch":
                    self.memset("pool", pt.t[64:128, c0:c0 + 64], 0.0, writes=[pt.r])

            def stageD(st):
                h, qt, kb, m = st["h"], st["qt"], st["kb"], st["m"]
                vt_ = vts[h % 2]
                c0 = st["c0"]
                pt = st["pt"]
                q0 = qt * 512
                acc = Ac[m] if kind == "diff" else Ac[qt % 2]
                self.mm(acc.t[:, c0:512], vt_.t[:, kb, :], pt.t[:, c0:512], st["first"], st["last"], reads=[vt_.r, pt.r], writes=[acc.r])
                if not (st["last"] and m == nmaps - 1):
                    return
                o_ = ost[qt % 2]
                tps = []
                for mm_ in range(nmaps):
                    acc = Ac[mm_] if kind == "diff" else Ac[qt % 2]
                    tp = Tp[mm_] if kind == "diff" else Tp[qt % 2]
                    rc_ = rc[mm_] if kind == "diff" else rc[qt % 2]
                    a_ = accS[mm_] if kind == "diff" else accS[qt % 2]
                    self.cp("act", a_.t[:], acc.t[:], reads=[acc.r], writes=[a_.r])
                    for s_ in range(4):
                        self.tr(tp.t[:, s_, :], a_.t[:, s_ * 128:(s_ + 1) * 128], idf.t[:], reads=[a_.r, idf.r], writes=[tp.r])
                    self.recip(rc_.t[:], tp.t[:, :, 64:65], reads=[tp.r], writes=[rc_.r])
                    tps.append((tp, rc_))
                if kind == "diff":
                    (t0_, rc0), (t1_, rc1) = tps
                    self.ts("dve", rc1.t[:], rc1.t[:], lam.t[:, 0:1], None, ALU.mult, None, reads=[rc1.r, lam.r], writes=[rc1.r])
                    for s_ in range(4):
                        self.ts("dve", tmp.t[:, s_, :], t1_.t[:, s_, 0:64], rc1.t[:, s_, :], None, ALU.mult, None, reads=[t1_.r, rc1.r], writes=[tmp.r])
                        self.stt("dve", o_.t[:, s_, :], t0_.t[:, s_, 0:64], rc0.t[:, s_, :], tmp.t[:, s_, :], ALU.mult, ALU.subtract,
                                 reads=[t0_.r, rc0.r, tmp.r], writes=[o_.r])
                else:
                    (t0_, rc0), = tps
                    for s_ in range(4):
                        self.ts("dve", o_.t[:, s_, :], t0_.t[:, s_, 0:64], rc0.t[:, s_, :], None, ALU.mult, None, reads=[t0_.r, rc0.r], writes=[o_.r])
                col = colbase + h * 64
                self.store("O_tok", self.O_tok[q0:q0 + 512, col:col + 64].rearrange("(s p) c -> p s c", p=128), o_, o_.t[:])

            def finalize_diff(h, qt):
                o_ = ost[qt % 2]
                q0 = qt * 512
                tps = []
                for mm_ in range(2):
                    acc, tp, rc_, a_ = Ac[mm_], Tp[mm_], rc[mm_], accS[mm_]
                    self.cp("act", a_.t[:], acc.t[:], reads=[acc.r], writes=[a_.r])
                    for s_ in range(4):
                        self.tr(tp.t[:, s_, :], a_.t[:, s_ * 128:(s_ + 1) * 128], idf.t[:], reads=[a_.r, idf.r], writes=[tp.r])
                    self.recip(rc_.t[:], tp.t[:, :, 64:65], reads=[tp.r], writes=[rc_.r])
                    tps.append((tp, rc_))
                (t0_, rc0), (t1_, rc1) = tps
                self.ts("dve", rc1.t[:], rc1.t[:], lam.t[:, 0:1], None, ALU.mult, None, reads=[rc1.r, lam.r], writes=[rc1.r])
                for s_ in range(4):
                    self.ts("dve", tmp.t[:, s_, :], t1_.t[:, s_, 0:64], rc1.t[:, s_, :], None, ALU.mult, None, reads=[t1_.r, rc1.r], writes=[tmp.r])
                    self.stt("dve", o_.t[:, s_, :], t0_.t[:, s_, 0:64], rc0.t[:, s_, :], tmp.t[:, s_, :], ALU.mult, ALU.subtract,
                             reads=[t0_.r, rc0.r, tmp.r], writes=[o_.r])
                col = colbase + h * 64
                self.store("O_tok", self.O_tok[q0:q0 + 512, col:col + 64].rearrange("(s p) c -> p s c", p=128), o_, o_.t[:])

            def pairA(st):
                h, qt, kb = st["h"], st["qt"], st["kb"]
                if st["i"] == 0:
                    head_loads(0)
                if st["hstep"] == 2 and h + 1 < nh:
                    head_loads(h + 1)
                qt_, kt_ = qts[h % 2], kts[h % 2]
                q0 = qt * 512
                j = kb - 4 * qt
                c0 = 128 * j if j > 0 else 0
                st["c0"] = c0
                Dk = Dt[st["i"] % 2]
                self.act(Dk.t[:, c0:512], posq.t[:, q0 + c0:q0 + 512], AF.Abs, reads=[posq.r, npk.r], writes=[Dk.r],
                         bias=npk.t[:, kb:kb + 1], scale=1.0)
                sp = Sp[st["i"] % 2]
                for m in range(2):
                    r0, r1 = 32 * m, 32 * m + 32
                    self.mm(sp.t[:, m, c0:512], kt_.t[r0:r1, kb * 128:(kb + 1) * 128], qt_.t[r0:r1, q0 + c0:q0 + 512], True, True,
                            reads=[kt_.r, qt_.r], writes=[sp.r])
                st["Dk"] = Dk
                st["sp"] = sp
                st["j"] = j

            def pairB(st):
                h = st["h"]
                c0, Dk, sp, j = st["c0"], st["Dk"], st["sp"], st["j"]
                tt_ = Tt[st["i"] % 3]
                pt = Pt[st["i"] % 3]
                st["pt"] = pt
                n = 512 - c0
                self.stt("dve", tt_.t[:, :, c0:512], Dk.t[:, c0:512].unsqueeze(1).to_broadcast([128, 2, n]), -SLOPES[h], sp.t[:, :, c0:512],
                         ALU.mult, ALU.add, reads=[Dk.r, sp.r], writes=[tt_.r])
                self.act(pt.t[:, :, c0:512], tt_.t[:, :, c0:512], AF.Exp, reads=[tt_.r], writes=[pt.r])
                if j >= 0:
                    self.memset("pool", pt.t[64:128, :, c0:c0 + 64], 0.0, writes=[pt.r])

            def pairD(st):
                h, qt, kb = st["h"], st["qt"], st["kb"]
                vt_ = vts[h % 2]
                c0 = st["c0"]
                pt = st["pt"]
                for m in range(2):
                    self.mm(Ac[m].t[:, c0:512], vt_.t[:, kb, :], pt.t[:, m, c0:512], st["first"], st["last"], reads=[vt_.r, pt.r], writes=[Ac[m].r])
                if st["last"]:
                    finalize_diff(h, qt)

            if kind == "diff":
                psteps = []
                for h in range(nh):
                    hfirst = len(psteps)
                    for qt in range(NT):
                        kbs = list(range(0, 4 * qt + 4))
                        for idx, kb in enumerate(kbs):
                            psteps.append(dict(h=h, qt=qt, kb=kb, first=(idx == 0), last=(idx == len(kbs) - 1),
                                               hstep=len(psteps) - hfirst, i=len(psteps)))
                N = len(psteps)
                for i in range(N + 2):
                    if i < N:
                        pairA(psteps[i])
                    if 0 <= i - 1 < N:
                        pairB(psteps[i - 1])
                    if i - 2 >= 0:
                        pairD(psteps[i - 2])
            else:
                N = len(steps)
                for i in range(N + LAG):
                    if i < N:
                        stageA(steps[i])
                    if i - LAG >= 0:
                        stageD(steps[i - LAG])

    def p3a(self, l, xsrc, xsrc_name):
        S, NT = self.S, self.NT
        lam_init = 0.8 - 0.6 * math.exp(-0.3 * l)
        with self.phase("p3a"):
            epsb = self.sb("epsb", [128, 2], F32)
            self.memset("pool", epsb.t[:, 0:1], EPS, writes=[epsb.r])
            self.memset("pool", epsb.t[:, 1:2], -0.5, writes=[epsb.r])
            idf = self.sb("idf", [128, 128], F32)
            self.load(idf, idf.t[:], self.identf_in)
            idb = self.sb("idb", [128, 128], BF16)
            self.cp("dve", idb.t[:], idf.t[:], reads=[idf.r], writes=[idb.r])
            go = self.sb("go", [128, 8], F32)
            self.load(go, go.t[:], self.g_o[l])
            stg = [self.sb("stg%d" % i, [128, 2048], F32) for i in range(2)]
            wo = self.sb("wo", [128, 8, D], BF16)
            self.load_w(wo, wo.t, self.w_out[l], 8, D, stg)
            ot = [self.sb("ot%d" % i, [128, 4, D], F32) for i in range(2)]
            xt = [self.sb("xt%d" % i, [128, 4, D], F32) for i in range(2)]
            on = self.sb("on", [128, 4, D], BF16)
            onT = self.sb("onT", [128, 8, 512], BF16)
            xo = [self.sb("xo%d" % i, [128, 4, D], F32) for i in range(2)]
            ssg = self.sb("ssg", [128, 4, 8], F32)
            junk = self.sb("junk", [128, 384], BF16)
            pT = [self.ps("pT%d" % i, [128, 512], BF16) for i in range(2)]
            pA = [self.ps("pA%d" % i, [128, 512], F32) for i in range(4)]
            groups = [(0, 384)] + [(384 + 64 * g, 64) for g in range(6)] + [(768, 256)]
            for t in range(NT):
                tok = slice(t * 512, (t + 1) * 512)
                o = ot[t % 2]
                x = xt[t % 2]
                self.load(o, o.t[:], self.O_tok[tok, :].rearrange("(s p) c -> p s c", p=128), src_name="O_tok")
                self.load(x, x.t[:], xsrc[tok, :].rearrange("(s p) c -> p s c", p=128), src_name=xsrc_name)
                for s in range(4):
                    for gi, (c0, n) in enumerate(groups):
                        self.act(junk.t[:, 0:n], o.t[:, s, c0:c0 + n], AF.Square, reads=[o.r], writes=[junk.r, ssg.r], accum=ssg.t[:, s, gi:gi + 1])
                self.act(ssg.t[:, :, 0:1], ssg.t[:, :, 0:1], AF.Ln, reads=[ssg.r, epsb.r], writes=[ssg.r], bias=epsb.t[:, 0:1], scale=1.0 / 384)
                self.act(ssg.t[:, :, 1:7], ssg.t[:, :, 1:7], AF.Ln, reads=[ssg.r, epsb.r], writes=[ssg.r], bias=epsb.t[:, 0:1], scale=1.0 / 64)
                self.act(ssg.t[:, :, 7:8], ssg.t[:, :, 7:8], AF.Ln, reads=[ssg.r, epsb.r], writes=[ssg.r], bias=epsb.t[:, 0:1], scale=1.0 / 256)
                self.act(ssg.t[:], ssg.t[:], AF.Exp, reads=[ssg.r], writes=[ssg.r], scale=-0.5)
                for s in range(4):
                    for gi, (c0, n) in enumerate(groups):
                        eng = "dve"
                        self.ts(eng, on.t[:, s, c0:c0 + n], o.t[:, s, c0:c0 + n], ssg.t[:, s, gi:gi + 1], None, ALU.mult, None, reads=[o.r, ssg.r], writes=[on.r])
                for c in range(8):
                    p = pT[c % 2]
                    for s in range(4):
                        self.tr(p.t[:, s * 128:(s + 1) * 128], on.t[:, s, c * 128:(c + 1) * 128], idb.t[:], reads=[on.r, idb.r], writes=[p.r])
                    if c in (3, 4, 5):
                        self.ts("dve", onT.t[:, c, :], p.t[:, 0:512], go.t[:, c:c + 1], 1.0 - lam_init, ALU.mult, ALU.mult, reads=[p.r, go.r], writes=[onT.r])
                    else:
                        self.act(onT.t[:, c, :], p.t[:, 0:512], AF.Copy, reads=[p.r, go.r], writes=[onT.r], scale=go.t[:, c:c + 1])
                xo_ = xo[t % 2]
                k = 0
                for s in range(4):
                    for hf in range(2):
                        pa = pA[k % 4]
                        k += 1
                        for c in range(8):
                            self.mm(pa.t[:], onT.t[:, c, s * 128:(s + 1) * 128], wo.t[:, c, hf * 512:(hf + 1) * 512], c == 0, c == 7, reads=[onT.r, wo.r], writes=[pa.r])
                        self.tt("dve", xo_.t[:, s, hf * 512:(hf + 1) * 512], x.t[:, s, hf * 512:(hf + 1) * 512], pa.t[:], ALU.add, reads=[x.r, pa.r], writes=[xo_.r])
                self.store("x1", self.x1[tok, :].rearrange("(s p) c -> p s c", p=128), xo_, xo_.t[:])

    def p3b(self, l, final):
        S = self.S
        TT = 256
        NTT = S // TT
        with self.phase("p3b"):
            epsb = self.sb("epsb", [128, 2], F32)
            self.memset("pool", epsb.t[:, 0:1], EPS, writes=[epsb.r])
            self.memset("pool", epsb.t[:, 1:2], -0.5, writes=[epsb.r])
            idf = self.sb("idf", [128, 128], F32)
            self.load(idf, idf.t[:], self.identf_in)
            idb = self.sb("idb", [128, 128], BF16)
            self.cp("dve", idb.t[:], idf.t[:], reads=[idf.r], writes=[idb.r])
            gT = self.sb("gT", [128, 8], F32)
            self.load(gT, gT.t[:], self.g_ffn[l])
            cw = self.sb("cw", [128, 44, 3], F32)
            self.load(cw, cw.t[:], self.cw_in[l].rearrange("p (j k) -> p j k", k=3))
            cb = self.sb("cb", [128, 44], F32)
            self.load(cb, cb.t[:], self.cb_in[l])
            stg = [self.sb("stg%d" % i, [128, 512], F32) for i in range(2)]
            wf1 = self.sb("wf1", [128, 8, 2 * DFF], BF16)
            wf2 = self.sb("wf2", [128, 22, D], BF16)

            def lw(wt, src2, nchunk, ncol):
                k = 0
                for c in range(nchunk):
                    for c0 in range(0, ncol, 512):
                        w = min(512, ncol - c0)
                        s = stg[k % 2]
                        k += 1
                        self.load(s, s.t[:, 0:w], src2[c * 128:(c + 1) * 128, c0:c0 + w])
                        self.cp("pool", wt.t[:, c, c0:c0 + w], s.t[:, 0:w], reads=[s.r], writes=[wt.r])
            lw(wf1, self.w_f1[l], 8, 2 * DFF)
            lw(wf2, self.w_f2[l], 22, D)
            if final:
                gfb = self.sb("gfb", [128, D], F32)
                self.load(gfb, gfb.t[:], self.g_fin.partition_broadcast(128))
            xt = [self.sb("xt%d" % i, [128, 2, D], F32) for i in range(1)]
            xn = self.sb("xn", [128, 2, D], BF16)
            xnT = self.sb("xnT", [128, 8, TT], BF16)
            actTs = [self.sb("actT%d" % i, [128, 22, TT], BF16) for i in range(2)]
            ss = self.sb("ss", [128, 4], F32)
            junk = self.sb("junk", [128, D], BF16)
            hal = self.sb("hal", [128, 44, 2], F32)
            hal2 = self.sb("hal2", [128, 44, 2], F32)
            c1 = [self.sb("c1_%d" % i, [128, TT], F32) for i in range(4)]
            c2 = c1
            c3 = c1
            sg = [self.sb("sg%d" % i, [128, TT], F32) for i in range(2)]
            stt_ = self.sb("cst", [128, 44, 2], F32)
            self.memset("pool", stt_.t[:], 0.0, writes=[stt_.r])
            xo = self.sb("xo", [128, 2, D], F32)
            pT = [self.ps("pT%d" % i, [128, 512], BF16) for i in range(2)]
            pH = [self.ps("pH%d" % i, [128, 512], F32) for i in range(4)]
            pY = [self.ps("pY%d" % i, [128, 512], F32) for i in range(2)]
            kk = 0

            dmy = pT[1].t[:, 0:1024].bitcast(F32)

            def o_chunk(tp, i):
                tokp = slice(tp * TT, (tp + 1) * TT)
                aT = actTs[tp % 2]
                if i == 0:
                    self.load(xo, xo.t[:], self.x1[tokp, :].rearrange("(s p) c -> p s c", p=128))
                g0 = 0 if i < 11 else 2
                ii = (i % 11) * 2
                for g in (g0, g0 + 1):
                    s_, hf = divmod(g, 2)
                    py = pY[g % 2]
                    for k_ in (ii, ii + 1):
                        self.mm(py.t[:], aT.t[:, k_, s_ * 128:(s_ + 1) * 128], wf2.t[:, k_, hf * 512:(hf + 1) * 512], k_ == 0, k_ == 21,
                                reads=[aT.r, wf2.r], writes=[py.r])
                if i % 11 == 10:
                    for g in (g0, g0 + 1):
                        s_, hf = divmod(g, 2)
                        py = pY[g % 2]
                        self.tt("dve", xo.t[:, s_, hf * 512:(hf + 1) * 512], xo.t[:, s_, hf * 512:(hf + 1) * 512], py.t[:], ALU.add,
                                reads=[xo.r, py.r], writes=[xo.r])
                if i == 21:
                    if not final:
                        self.store("x2", self.x2[tokp, :].rearrange("(s p) c -> p s c", p=128), xo, xo.t[:])
                    else:
                        for s_ in range(2):
                            self.act(junk.t[:], xo.t[:, s_, :], AF.Square, reads=[xo.r], writes=[junk.r, ss.r], accum=ss.t[:, 2 + s_:3 + s_])
                        self.rstd(ss.t[:, 2:4], ss.t[:, 2:4], float(D), epsb, reads=[ss.r], writes=[ss.r])
                        for s_ in range(2):
                            self.stt("dve", xo.t[:, s_, :], xo.t[:, s_, :], ss.t[:, 2 + s_:3 + s_], gfb.t[:], ALU.mult, ALU.mult,
                                     reads=[xo.r, ss.r, gfb.r], writes=[xo.r])
                        self.store("out", self.out[tokp, :].rearrange("(s p) c -> p s c", p=128), xo, xo.t[:])

            def emit_O(tp, i):
                if i is None:
                    for i_ in range(22):
                        o_chunk(tp, i_)
                else:
                    o_chunk(tp, i)

            for t in range(NTT):
                tok = slice(t * TT, (t + 1) * TT)
                x = xt[0]
                actT = actTs[t % 2]
                self.load(x, x.t[:], self.x1[tok, :].rearrange("(s p) c -> p s c", p=128))
                self.norm_T(x, 2, gT, xn, xnT, pT, idb, epsb, ss, junk)
                self.tt("dve", hal.t[:, :, 0], cw.t[:, :, 1], stt_.t[:, :, 1], ALU.mult, reads=[cw.r, stt_.r], writes=[hal.r])
                self.tt("dve", hal2.t[:, :, 0], cw.t[:, :, 0], stt_.t[:, :, 0], ALU.mult, reads=[cw.r, stt_.r], writes=[hal2.r])
                self.tt("dve", hal.t[:, :, 1], cw.t[:, :, 0], stt_.t[:, :, 1], ALU.mult, reads=[cw.r, stt_.r, hal.r], writes=[hal.r])
                self.tt("dve", hal.t[:, :, 0], hal.t[:, :, 0], hal2.t[:, :, 0], ALU.add, reads=[hal.r, hal2.r], writes=[hal.r])
                for i in range(22):
                    un = []
                    for which in range(2):
                        idx = i + 22 * which
                        col0 = idx * 128
                        ph = pH[kk % 4]
                        a = c1[kk % 4]
                        kk += 1
                        for c in range(8):
                            self.mm(ph.t[:, 0:TT], wf1.t[:, c, col0:col0 + 128], xnT.t[:, c, :], c == 0, c == 7, reads=[wf1.r, xnT.r], writes=[ph.r])
                        un.append((idx, ph, a))
                    for (idx, ph, a) in un:
                        self.act(a.t[:], ph.t[:, 0:TT], AF.Identity, reads=[ph.r, cw.r, cb.r], writes=[a.r], bias=cb.t[:, idx:idx + 1], scale=cw.t[:, idx, 2:3])
                    for (idx, ph, a) in un:
                        self.stt("dve", a.t[:, 1:TT], ph.t[:, 0:TT - 1], cw.t[:, idx, 1:2], a.t[:, 1:TT], ALU.mult, ALU.add, reads=[ph.r, cw.r, a.r], writes=[a.r])
                    for (idx, ph, a) in un:
                        self.stt("dve", a.t[:, 2:TT], ph.t[:, 0:TT - 2], cw.t[:, idx, 0:1], a.t[:, 2:TT], ALU.mult, ALU.add, reads=[ph.r, cw.r, a.r], writes=[a.r])
                    for (idx, ph, a) in un:
                        self.tt("pool", a.t[:, 0:2], a.t[:, 0:2], hal.t[:, idx, :], ALU.add, reads=[a.r, hal.r], writes=[a.r])
                    for (idx, ph, a) in un:
                        self.cp("act", stt_.t[:, idx, :], ph.t[:, TT - 2:TT], reads=[ph.r], writes=[stt_.r])
                    s_ = sg[i % 2]
                    self.act(s_.t[:], un[1][2].t[:], AF.Silu, reads=[un[1][2].r], writes=[s_.r])
                    self.tt("pool", actT.t[:, i, :], s_.t[:], un[0][2].t[:], ALU.mult, reads=[s_.r, un[0][2].r], writes=[actT.r])
                    if t > 0:
                        emit_O(t - 1, i)
                    for _ in range(DUMMY_FFN):
                        self.mm(dmy, wf2.t[:, 0, 0:128], wf2.t[:, 1, 0:512], True, True, reads=[wf2.r], writes=[pT[1].r])
            emit_O(NTT - 1, None)

    def build(self):
        import os
        ph = os.environ.get("K_PHASES", "p0,p1,mla,diff,ch,p3a,p3b").split(",")
        nl = int(os.environ.get("K_LAYERS", str(DEPTH)))
        self.declare()
        if "p0" in ph:
            self.p0()
        xsrc, xname = self.x_in, None
        for l in range(nl):
            if "p1" in ph:
                self.p1(l, xsrc, xname)
            for kd in ("mla", "diff", "ch"):
                if kd in ph:
                    self.attn(l, kd)
            if "p3a" in ph:
                self.p3a(l, xsrc, xname)
            if "p3b" in ph:
                self.p3b(l, final=(l == DEPTH - 1))
            xsrc, xname = self.x2, None
        self.gst.close()
        return self.nc


def _host_consts():
    identf = np.eye(128, dtype=np.float32)
    p = np.arange(128)
    inv = (10000.0 ** (-(np.arange(16, dtype=np.float32)) / 16)).astype(np.float32)
    invf = np.zeros((128, 2), np.float32)
    invf[:, 0] = inv[p % 16]
    sign = np.where((p % 32) < 16, -1.0, 1.0).astype(np.float32)
    invf[:, 1] = inv[p % 16] * sign
    cmask = np.zeros((8, 128, 512), np.float32)
    ki = np.arange(128)[:, None] // 64
    qi = np.arange(512)[None, :] // 64
    for i in range(8):
        delta = 512 - 128 * i
        d = delta // 64 + qi - ki
        cmask[i] = np.where((d >= 0) & (d <= 8), 0.0, NEG)
    return identf, invf, cmask


def _prep_weights(inp):
    f = lambda a: np.ascontiguousarray(np.asarray(a, dtype=np.float32))
    L = DEPTH
    w_in = f(inp["w_in"])
    cq, ckv, kr = w_in[:, :, 0:256], w_in[:, :, 256:384], w_in[:, :, 384:416]
    dq, dk, dv = w_in[:, :, 416:800], w_in[:, :, 800:1184], w_in[:, :, 1184:1568]
    chq, chk, chv = w_in[:, :, 1568:1824], w_in[:, :, 1824:2080], w_in[:, :, 2080:2336]
    swap = np.concatenate([np.arange(16, 32), np.arange(0, 16)])
    w_in_d = np.concatenate([cq, ckv, dv, chv, dq, dk, chq, chk, kr, kr[:, :, swap]], axis=2)
    assert w_in_d.shape[2] == NCOL_IN
    wuq = f(inp["mla_w_uq"]).reshape(L, 256, 6, 96)
    nope = wuq[..., 0:64].reshape(L, 256, 384)
    rope = wuq[..., 64:96]
    w_uq_d = np.concatenate([nope, rope.reshape(L, 256, 192), rope[..., swap].reshape(L, 256, 192)], axis=2)
    wukv = f(inp["mla_w_ukv"]).reshape(L, 128, 6, 128)
    w_ukv_d = np.concatenate([wukv[..., 0:64].reshape(L, 128, 384), wukv[..., 64:128].reshape(L, 128, 384)], axis=2)

    def colT(g, n):
        return np.ascontiguousarray(f(g).reshape(L, n, 128).transpose(0, 2, 1))
    g_o = np.concatenate([f(inp["mla_out_norm"]), f(inp["diff_norm"]), f(inp["chunk_out_norm"])], axis=1)
    cwt = f(inp["ffn_conv_w"])
    cw = np.ascontiguousarray(cwt.reshape(L, 3, 44, 128).transpose(0, 3, 2, 1)).reshape(L, 128, 132)
    cb = np.ascontiguousarray(f(inp["ffn_conv_b"]).reshape(L, 44, 128).transpose(0, 2, 1))
    return {
        "w_in": np.ascontiguousarray(w_in_d), "w_uq": np.ascontiguousarray(w_uq_d), "w_ukv": np.ascontiguousarray(w_ukv_d),
        "w_out": f(inp["w_out"]), "w_f1": f(inp["w_ffn_in"]), "w_f2": f(inp["w_ffn_out"]),
        "g_attn": colT(inp["attn_norm"], 8), "g_ffn": colT(inp["ffn_norm"], 8),
        "g_q": colT(inp["mla_q_norm"], 2), "g_kv": colT(inp["mla_kv_norm"], 1), "g_o": colT(g_o, 8),
        "lam": f(inp["diff_lambda"]).reshape(L, 1, 128), "rb": f(inp["chunk_rel_bias"]),
        "cw": cw, "cb": cb, "g_fin": f(inp["final_norm"]).reshape(1, D),
    }


_NC_CACHE = {}


def run_cores(inp, S, ncores, debug=False):
    key = (S, debug)
    if key not in _NC_CACHE:
        _NC_CACHE[key] = K(S, debug).build()
    nc = _NC_CACHE[key]
    identf, invf, cmask = _host_consts()
    wd = _prep_weights(inp)
    x = np.asarray(inp["x"], dtype=np.float32)
    pos = np.asarray(inp["positions"]).astype(np.int32)
    in_maps = []
    for b in range(ncores):
        m = dict(wd)
        m["x"] = np.ascontiguousarray(x[b, :S])
        m["pos"] = np.ascontiguousarray(pos[b, :S].reshape(S // 128, 128))
        m["identf"] = identf
        m["invf"] = invf
        m["cmask"] = cmask
        in_maps.append(m)
    res = run_bass_kernel_spmd(nc, in_maps, core_ids=list(range(ncores)))
    return res


def kernel(**inputs):
    B, S = inputs["x"].shape[0], inputs["x"].shape[1]
    res = run_cores(inputs, S, B)
    return np.stack([np.asarray(r["out"], dtype=np.float32) for r in res.results], axis=0)
```

```python
import contextlib

ENGS = ("pe", "act", "dve", "pool", "sp")
ENGOBJ = {"pe": "tensor", "act": "scalar", "dve": "vector", "pool": "gpsimd", "sp": "sync"}


class Res:
    __slots__ = ("name", "lw", "rd", "slot", "base", "dcount", "excl")

    def __init__(self, name="", excl=False):
        self.name = name
        self.excl = excl
        self.lw = None
        self.rd = []
        self.slot = None
        self.base = 0
        self.dcount = 0


class Op:
    __slots__ = ("eng", "fn", "deps", "ddeps", "pos", "sig", "count", "dres", "kind")

    def __init__(self, eng, fn, kind):
        self.eng = eng
        self.fn = fn
        self.kind = kind
        self.deps = []
        self.ddeps = []
        self.sig = False
        self.count = 0
        self.dres = None


class Prog:
    def __init__(self, nc, stack, nslots=88, same_engine_sync=True):
        self.nc = nc
        self.same_engine_sync = same_engine_sync
        self.esem = {e: stack.enter_context(nc.semaphore("s_" + e)) for e in ENGS}
        self.ecount = {e: 0 for e in ENGS}
        self.slots = [[stack.enter_context(nc.semaphore("d%d" % i)), 0] for i in range(nslots)]
        self.free = list(range(nslots))
        self.total_ops = 0
        self.begin()

    def begin(self):
        self.streams = {e: [] for e in ENGS}
        self.dma_res = []

    def _add(self, op, reads, writes):
        if any(r.excl for r in reads):
            writes = list(writes) + [r for r in reads if r.excl and r not in writes]
            reads = [r for r in reads if not r.excl]
        deps = []
        for r in reads:
            if r.lw is not None:
                deps.append(r.lw)
        for w in writes:
            if w.lw is not None:
                deps.append(w.lw)
            deps.extend(w.rd)
        seen = set()
        for d in deps:
            if id(d) in seen or d is op:
                continue
            seen.add(id(d))
            if d.kind == "d":
                op.ddeps.append((d.dres, d.dres.dcount * 16))
            else:
                if d.eng == op.eng:
                    if d.eng == "pe":
                        continue
                    if not self.same_engine_sync:
                        continue
                op.deps.append(d)
        for r in reads:
            r.rd.append(op)
        for w in writes:
            w.lw = op
            w.rd = []
        op.pos = len(self.streams[op.eng])
        self.streams[op.eng].append(op)
        return op

    def op(self, eng, fn, reads=(), writes=(), mm=False):
        o = Op(eng, fn, "mm" if mm else "c")
        return self._add(o, reads, writes)

    def dma(self, queue, out_ap, in_ap, dst, reads=(), writes=(), **kw):
        if dst.slot is None:
            dst.slot = self.free.pop()
            dst.base = self.slots[dst.slot][1]
            dst.dcount = 0
            self.dma_res.append(dst)
        o = Op(queue, lambda e: e.dma_start(out=out_ap, in_=in_ap, **kw), "d")
        o.dres = dst
        self._add(o, reads, writes)
        dst.dcount += 1
        return o

    def end(self):
        nc = self.nc
        waited = {e: {f: -1 for f in ENGS} for e in ENGS}
        dwaited = {e: {} for e in ENGS}
        plan = {e: [] for e in ENGS}
        for e in ENGS:
            for op in self.streams[e]:
                best = {}
                for d in op.deps:
                    if d.pos > waited[e][d.eng]:
                        if d.eng not in best or d.pos > best[d.eng].pos:
                            best[d.eng] = d
                for f, d in best.items():
                    waited[e][f] = d.pos
                    d.sig = True
                dws = []
                for (res, cnt) in op.ddeps:
                    if dwaited[e].get(id(res), 0) < cnt:
                        dwaited[e][id(res)] = cnt
                        dws.append((res, cnt))
                plan[e].append((op, list(best.values()), dws))
        for e in ENGS:
            c = self.ecount[e]
            for op in self.streams[e]:
                if op.sig:
                    c += 1
                    op.count = c
            self.ecount[e] = c
        esem = self.esem
        slots = self.slots
        dma_res = list(self.dma_res)
        with nc.Block() as block:
            def run(e):
                def body(eng):
                    for (op, cw, dw) in plan[e]:
                        for d in cw:
                            eng.wait_ge(esem[d.eng], d.count)
                        for (res, cnt) in dw:
                            eng.wait_ge(slots[res.slot][0], res.base + cnt)
                        ins = op.fn(eng)
                        if op.kind == "d":
                            ins.then_inc(slots[op.dres.slot][0], 16)
                        elif op.sig:
                            ins.then_inc(esem[e], 1)
                    if e == "sp":
                        for r in dma_res:
                            eng.wait_ge(slots[r.slot][0], r.base + r.dcount * 16)
                return body

            for e in ENGS:
                getattr(block, ENGOBJ[e])(run(e))
        for r in dma_res:
            slots[r.slot][1] = r.base + r.dcount * 16
            self.free.append(r.slot)
            r.slot = None
            r.lw = None
            r.rd = []
        n = sum(len(self.streams[e]) for e in ENGS)
        self.total_ops += n
        self.begin()
        return n
import math
import numpy as np
import ml_dtypes
import concourse.bass as bass
import concourse.mybir as mybir
from concourse.bass_utils import run_bass_kernel_spmd

F32 = mybir.dt.float32
BF16 = mybir.dt.bfloat16
I32 = mybir.dt.int32
AF = mybir.ActivationFunctionType
ALU = mybir.AluOpType
AX = mybir.AxisListType

D = 1024
DEPTH = 2
EPS = 1e-6
NCOL_IN = 2368
DFF = 2816
TWO_PI = 2.0 * math.pi
NEG = -30000.0
SLOPES = [2.0 ** (-8.0 * (i + 1) / 6) for i in range(6)]
DUMMY_MLA = 256
DUMMY_FFN = 0


class Tl:
    __slots__ = ("t", "r")

    def __init__(self, t, r):
        self.t = t
        self.r = r


class K:
    def __init__(self, S, debug=False):
        self.S = S
        self.NT = S // 512
        self.NB = S // 128
        self.debug = debug
        self.nc = bass.Bass("TRN2", target_bir_lowering=False)
        self.gst = contextlib.ExitStack()
        self.P = Prog(self.nc, self.gst)
        self.st = None
        self.dres = {}
        self.uid = 0

    def din(self, name, shape, dt):
        return self.nc.dram_tensor(name, list(shape), dt, kind="ExternalInput").ap()

    def dscr(self, name, shape, dt):
        kind = "ExternalOutput" if self.debug else "Internal"
        return self.nc.dram_tensor(name, list(shape), dt, kind=kind).ap()

    def sb(self, name, shape, dt):
        self.uid += 1
        name = "sb%d_%s" % (self.uid, name)
        t = self.st.enter_context(self.nc.sbuf_tensor(name, list(shape), dt))
        return Tl(t, Res(name))

    def ps(self, name, shape, dt):
        self.uid += 1
        name = "ps%d_%s" % (self.uid, name)
        if dt == BF16 and list(shape) == [128, 512]:
            shape = [128, 1024]
        t = self.st.enter_context(self.nc.psum_tensor(name, list(shape), dt))
        return Tl(t, Res(name, excl=True))

    def dr(self, name):
        if name not in self.dres:
            self.dres[name] = Res(name)
        return self.dres[name]

    @contextlib.contextmanager
    def phase(self, name):
        self.st = contextlib.ExitStack()
        self.dres = {}
        with self.st:
            yield
            n = self.P.end()
        self.st = None

    def load(self, tl, dst_ap, src_ap, src_name=None, q="sp", **kw):
        reads = [self.dr(src_name)] if src_name else []
        self.P.dma(q, dst_ap, src_ap, tl.r, reads=reads, writes=[tl.r], **kw)

    def store(self, dname, dst_ap, tl, src_ap, q="pool", **kw):
        r = self.dr(dname)
        self.P.dma(q, dst_ap, src_ap, r, reads=[tl.r], writes=[r], **kw)

    def mm(self, out, lhsT, rhs, start, stop, reads, writes):
        self.P.op("pe", lambda e: e.matmul(out, lhsT=lhsT, rhs=rhs, start=start, stop=stop),
                  reads=reads, writes=writes, mm=True)

    def tr(self, out, in_, ident, reads, writes):
        self.P.op("pe", lambda e: e.transpose(out, in_, ident), reads=reads, writes=writes, mm=True)

    def act(self, out, in_, func, reads, writes, bias=None, scale=None, accum=None):
        kw = {}
        if bias is not None:
            kw["bias"] = bias
        if scale is not None:
            kw["scale"] = scale
        if accum is not None:
            kw["accum_out"] = accum
        self.P.op("act", lambda e: e.activation(out=out, in_=in_, func=func, **kw), reads=reads, writes=writes)

    def ts(self, eng, out, in0, s1, s2, op0, op1, reads, writes):
        if op1 is None:
            self.P.op(eng, lambda e: e.tensor_scalar(out=out, in0=in0, scalar1=s1, scalar2=None, op0=op0),
                      reads=reads, writes=writes)
        else:
            self.P.op(eng, lambda e: e.tensor_scalar(out=out, in0=in0, scalar1=s1, scalar2=s2, op0=op0, op1=op1),
                      reads=reads, writes=writes)

    def tt(self, eng, out, in0, in1, op, reads, writes):
        self.P.op(eng, lambda e: e.tensor_tensor(out=out, in0=in0, in1=in1, op=op), reads=reads, writes=writes)

    def stt(self, eng, out, in0, scalar, in1, op0, op1, reads, writes):
        self.P.op(eng, lambda e: e.scalar_tensor_tensor(out=out, in0=in0, scalar=scalar, in1=in1, op0=op0, op1=op1),
                  reads=reads, writes=writes)

    def cp(self, eng, out, in_, reads, writes):
        if eng == "act":
            self.P.op(eng, lambda e: e.copy(out=out, in_=in_), reads=reads, writes=writes)
        else:
            self.P.op(eng, lambda e: e.tensor_copy(out=out, in_=in_), reads=reads, writes=writes)

    def memset(self, eng, ap, val, writes):
        self.P.op(eng, lambda e: e.memset(ap, val), reads=(), writes=writes)

    def recip(self, out, in_, reads, writes):
        self.P.op("dve", lambda e: e.reciprocal(out=out, in_=in_), reads=reads, writes=writes)

    def rstd(self, out, ss, n, epsb, reads, writes):
        self.act(out, ss, AF.Ln, reads=list(reads) + [epsb.r], writes=writes, bias=epsb.t[:, 0:1], scale=1.0 / n)
        self.act(out, out, AF.Exp, reads=writes, writes=writes, scale=-0.5)

    def load_w(self, wt, dst3, src2, nchunk, ncol, stg, src_name=None):
        CW = 2048
        k = 0
        for c in range(nchunk):
            for c0 in range(0, ncol, CW):
                w = min(CW, ncol - c0)
                s = stg[k % len(stg)]
                k += 1
                self.load(s, s.t[:, 0:w], src2[c * 128:(c + 1) * 128, c0:c0 + w])
                self.cp("pool", dst3[:, c, c0:c0 + w], s.t[:, 0:w], reads=[s.r], writes=[wt.r])

    def declare(self):
        S = self.S
        self.x_in = self.din("x", [S, D], F32)
        self.pos_in = self.din("pos", [S // 128, 128], I32)
        self.identf_in = self.din("identf", [128, 128], F32)
        self.invf_in = self.din("invf", [128, 2], F32)
        self.cmask_in = self.din("cmask", [8, 128, 512], F32)
        self.w_in = self.din("w_in", [DEPTH, D, NCOL_IN], F32)
        self.w_uq = self.din("w_uq", [DEPTH, 256, 768], F32)
        self.w_ukv = self.din("w_ukv", [DEPTH, 128, 768], F32)
        self.w_out = self.din("w_out", [DEPTH, D, D], F32)
        self.w_f1 = self.din("w_f1", [DEPTH, D, 2 * DFF], F32)
        self.w_f2 = self.din("w_f2", [DEPTH, DFF, D], F32)
        self.g_attn = self.din("g_attn", [DEPTH, 128, 8], F32)
        self.g_ffn = self.din("g_ffn", [DEPTH, 128, 8], F32)
        self.g_q = self.din("g_q", [DEPTH, 128, 2], F32)
        self.g_kv = self.din("g_kv", [DEPTH, 128, 1], F32)
        self.g_o = self.din("g_o", [DEPTH, 128, 8], F32)
        self.lam_in = self.din("lam", [DEPTH, 1, 128], F32)
        self.rb_in = self.din("rb", [DEPTH, 4, 320], F32)
        self.cw_in = self.din("cw", [DEPTH, 128, 44 * 3], F32)
        self.cb_in = self.din("cb", [DEPTH, 128, 44], F32)
        self.g_fin = self.din("g_fin", [1, D], F32)
        self.out = self.nc.dram_tensor("out", [S, D], F32, kind="ExternalOutput").ap()
        self.posf = self.dscr("posf", [1, S], F32)
        self.negposk = self.dscr("negposk", [128, S // 128], F32)
        self.cosT = self.dscr("cosT", [self.NT, 128, 512], F32)
        self.sinT = self.dscr("sinT", [self.NT, 128, 512], F32)
        self.QTn_mla = self.dscr("QTn_mla", [6, 64, S], BF16)
        self.QTr_mla = self.dscr("QTr_mla", [6, 32, S], BF16)
        self.KTn_mla = self.dscr("KTn_mla", [6, 64, S], BF16)
        self.KTr_mla = self.dscr("KTr_mla", [32, S], BF16)
        self.V_all = self.dscr("V_all", [16, S, 128], BF16)
        self.QT_diff = self.dscr("QT_diff", [6, 64, S], BF16)
        self.KT_diff = self.dscr("KT_diff", [6, 64, S], BF16)
        self.QT_ch = self.dscr("QT_ch", [4, 64, S], BF16)
        self.KT_ch = self.dscr("KT_ch", [4, 64, S], BF16)
        self.O_tok = self.dscr("O_tok", [S, D], F32)
        self.x1 = self.dscr("x1", [S, D], F32)
        self.x2 = self.dscr("x2", [S, D], F32)
        self.ext = self.dscr("ext", [4, 128, 1536], F32)

    def p0(self):
        S, NB, NT = self.S, self.NB, self.NT
        with self.phase("p0"):
            idf = self.sb("idf", [128, 128], F32)
            self.load(idf, idf.t[:], self.identf_in)
            invf = self.sb("invf", [128, 2], F32)
            self.load(invf, invf.t[:], self.invf_in)
            pi = self.sb("pi", [NB, 128], I32)
            self.load(pi, pi.t[:], self.pos_in)
            pf = self.sb("pf", [NB, 128], F32)
            self.cp("dve", pf.t[:], pi.t[:], reads=[pi.r], writes=[pf.r])
            self.store("posf", self.posf.rearrange("o (j p) -> (o j) p", p=128), pf, pf.t[:])
            pst = self.ps("pst", [128, 512], F32)
            self.tr(pst.t[:, 0:NB], pf.t[0:NB, :], idf.t[0:NB, 0:NB], reads=[pf.r, idf.r], writes=[pst.r])
            npk = self.sb("npk", [128, NB], F32)
            self.ts("dve", npk.t[:], pst.t[:, 0:NB], -1.0, None, ALU.mult, None, reads=[pst.r], writes=[npk.r])
            self.store("negposk", self.negposk, npk, npk.t[:])
            pbc = [self.sb("pbc%d" % i, [128, 512], F32) for i in range(2)]
            ang = self.sb("ang", [128, 512], F32)
            ni = self.sb("ni", [128, 512], I32)
            nf = self.sb("nf", [128, 512], F32)
            y = self.sb("y", [128, 512], F32)
            res = [self.sb("res%d" % i, [128, 512], F32) for i in range(2)]
            k = 0
            for t in range(NT):
                pb = pbc[t % 2]
                self.load(pb, pb.t[:], self.posf[:, t * 512:(t + 1) * 512].partition_broadcast(128), src_name="posf")
                for which in range(2):
                    col = invf.t[:, which:which + 1]
                    if which == 0:
                        self.ts("dve", ang.t[:], pb.t[:], col, math.pi / 2, ALU.mult, ALU.add, reads=[pb.r, invf.r], writes=[ang.r])
                    else:
                        self.ts("dve", ang.t[:], pb.t[:], col, None, ALU.mult, None, reads=[pb.r, invf.r], writes=[ang.r])
                    self.ts("dve", ni.t[:], ang.t[:], 1.0 / TWO_PI, None, ALU.mult, None, reads=[ang.r], writes=[ni.r])
                    self.cp("dve", nf.t[:], ni.t[:], reads=[ni.r], writes=[nf.r])
                    self.stt("dve", y.t[:], nf.t[:], -TWO_PI, ang.t[:], ALU.mult, ALU.add, reads=[nf.r, ang.r], writes=[y.r])
                    self.ts("dve", nf.t[:], y.t[:], math.pi, -TWO_PI, ALU.is_gt, ALU.mult, reads=[y.r], writes=[nf.r])
                    self.tt("dve", y.t[:], y.t[:], nf.t[:], ALU.add, reads=[y.r, nf.r], writes=[y.r])
                    self.ts("dve", nf.t[:], y.t[:], -math.pi, TWO_PI, ALU.is_lt, ALU.mult, reads=[y.r], writes=[nf.r])
                    self.tt("dve", y.t[:], y.t[:], nf.t[:], ALU.add, reads=[y.r, nf.r], writes=[y.r])
                    self.ts("dve", y.t[:], y.t[:], math.pi, -math.pi, ALU.min, ALU.max, reads=[y.r], writes=[y.r])
                    r_ = res[k % 2]
                    k += 1
                    self.act(r_.t[:], y.t[:], AF.Sin, reads=[y.r], writes=[r_.r])
                    dst = self.cosT if which == 0 else self.sinT
                    self.store("cosT" if which == 0 else "sinT", dst[t], r_, r_.t[:])

    def norm_T(self, xt, ns, gT, xn, xnT, pT, idb, epsb, ss, junk, evac_engs=("act", "dve")):
        for s in range(ns):
            self.act(junk.t[:], xt.t[:, s, :], AF.Square, reads=[xt.r], writes=[junk.r, ss.r], accum=ss.t[:, s:s + 1])
        self.rstd(ss.t[:, 0:ns], ss.t[:, 0:ns], float(D), epsb, reads=[ss.r], writes=[ss.r])
        for s in range(ns):
            self.ts("dve", xn.t[:, s, :], xt.t[:, s, :], ss.t[:, s:s + 1], None, ALU.mult, None, reads=[xt.r, ss.r], writes=[xn.r])
        for c in range(8):
            p = pT[c % len(pT)]
            for s in range(ns):
                self.tr(p.t[:, s * 128:(s + 1) * 128], xn.t[:, s, c * 128:(c + 1) * 128], idb.t[:], reads=[xn.r, idb.r], writes=[p.r])
            eng = evac_engs[c % len(evac_engs)]
            if eng == "act":
                self.act(xnT.t[:, c, :], p.t[:, 0:ns * 128], AF.Copy, reads=[p.r, gT.r], writes=[xnT.r], scale=gT.t[:, c:c + 1])
            else:
                self.ts("dve", xnT.t[:, c, :], p.t[:, 0:ns * 128], gT.t[:, c:c + 1], None, ALU.mult, None, reads=[p.r, gT.r], writes=[xnT.r])

    def p1(self, l, xsrc, xsrc_name):
        S, NT = self.S, self.NT
        with self.phase("p1"):
            epsb = self.sb("epsb", [128, 2], F32)
            self.memset("pool", epsb.t[:, 0:1], EPS, writes=[epsb.r])
            self.memset("pool", epsb.t[:, 1:2], -0.5, writes=[epsb.r])
            idf = self.sb("idf", [128, 128], F32)
            self.load(idf, idf.t[:], self.identf_in)
            idb = self.sb("idb", [128, 128], BF16)
            self.cp("dve", idb.t[:], idf.t[:], reads=[idf.r], writes=[idb.r])
            gT = self.sb("gT", [128, 8], F32)
            self.load(gT, gT.t[:], self.g_attn[l])
            gq = self.sb("gq", [128, 2], F32)
            self.load(gq, gq.t[:], self.g_q[l])
            gkv = self.sb("gkv", [128, 1], F32)
            self.load(gkv, gkv.t[:], self.g_kv[l])
            stg = [self.sb("stg%d" % i, [128, 2048], F32) for i in range(2)]
            win = self.sb("win", [128, 8, NCOL_IN], BF16)
            self.load_w(win, win.t, self.w_in[l], 8, NCOL_IN, stg)
            wuq = self.sb("wuq", [128, 2, 768], BF16)
            self.load_w(wuq, wuq.t, self.w_uq[l], 2, 768, stg)
            wukv = self.sb("wukv", [128, 1, 768], BF16)
            self.load_w(wukv, wukv.t, self.w_ukv[l], 1, 768, stg)
            xt = [self.sb("xt%d" % i, [128, 4, D], F32) for i in range(2)]
            xns = [self.sb("xn%d" % i, [128, 4, D], BF16) for i in range(2)]
            xnTs = [self.sb("xnT%d" % i, [128, 8, 512], BF16) for i in range(2)]
            sss = [self.sb("ss%d" % i, [128, 4], F32) for i in range(2)]
            junks = [self.sb("junk%d" % i, [128, D], BF16) for i in range(2)]
            cs = [self.sb("cs%d" % i, [128, 512], F32) for i in range(2)]
            sn = [self.sb("sn%d" % i, [128, 512], F32) for i in range(2)]
            pT = [self.ps("pT%d" % i, [128, 512], BF16) for i in range(2)]
            pA = [self.ps("pA%d" % i, [128, 512], F32) for i in range(4)]
            pB = [self.ps("pB%d" % i, [128, 512], F32) for i in range(2)]
            vst = [self.sb("vst%d" % i, [128, 16, 128], BF16) for i in range(4)]
            for v in vst:
                self.memset("pool", v.t[:, :, 64:128], 1.0, writes=[v.r])
            cqns = [self.sb("cqn%d" % i, [128, 384], BF16) for i in range(2)]
            cqTs = [self.sb("cqT%d" % i, [128, 2, 512], BF16) for i in range(2)]
            ckvTs = [self.sb("ckvT%d" % i, [128, 512], BF16) for i in range(2)]
            ss2s = [self.sb("ss2_%d" % i, [128, 2], F32) for i in range(2)]
            fst = [self.sb("fst%d" % i, [128, 512], BF16) for i in range(4)]
            rt1 = self.sb("rt1", [128, 512], F32)
            rt2 = self.sb("rt2", [128, 512], F32)
            fk = [0]
            sq = [0]

            def fm_out(ps_ap, rows, scale, dsts, preads):
                f = fst[fk[0] % 4]
                fk[0] += 1
                if fk[0] % 2 == 0:
                    self.act(f.t[0:rows, :], ps_ap, AF.Copy, reads=preads, writes=[f.r], scale=scale)
                else:
                    self.ts("dve", f.t[0:rows, :], ps_ap, scale, None, ALU.mult, None, reads=preads, writes=[f.r])
                for (dn, dap, r0, r1) in dsts:
                    sq[0] += 1
                    self.store(dn, dap, f, f.t[r0:r1, :], q=("sp" if sq[0] % 2 else "pool"))

            def tile_loads(t):
                x = xt[t % 2]
                tok = slice(t * 512, (t + 1) * 512)
                self.load(x, x.t[:], xsrc[tok, :].rearrange("(s p) c -> p s c", p=128), src_name=xsrc_name)
                self.load(cs[t % 2], cs[t % 2].t[:], self.cosT[t], src_name="cosT")
                self.load(sn[t % 2], sn[t % 2].t[:], self.sinT[t], src_name="sinT")

            tile_loads(0)
            for t in range(NT):
                x = xt[t % 2]
                tok = slice(t * 512, (t + 1) * 512)
                c_ = cs[t % 2]
                s_ = sn[t % 2]
                if t + 1 < NT:
                    tile_loads(t + 1)
                xn, xnT, ss, junk = xns[t % 2], xnTs[t % 2], sss[t % 2], junks[t % 2]
                cqT, ckvT = cqTs[t % 2], ckvTs[t % 2]
                self.norm_T(x, 4, gT, xn, xnT, pT, idb, epsb, ss, junk)
                for s in range(4):
                    pa0 = pA[(2 * s) % 4]
                    pa1 = pA[(2 * s + 1) % 4]
                    for hf, pa in ((0, pa0), (1, pa1)):
                        for c in range(8):
                            self.mm(pa.t[:], xnT.t[:, c, s * 128:(s + 1) * 128], win.t[:, c, hf * 512:(hf + 1) * 512],
                                    c == 0, c == 7, reads=[xnT.r, win.r], writes=[pa.r])
                    cqn, ss2 = cqns[s % 2], ss2s[s % 2]
                    V = vst[s]
                    junk = junks[(s + 1) % 2]
                    self.act(junk.t[:, 0:256], pa0.t[:, 0:256], AF.Square, reads=[pa0.r], writes=[junk.r, ss2.r], accum=ss2.t[:, 0:1])
                    self.act(junk.t[:, 0:128], pa0.t[:, 256:384], AF.Square, reads=[pa0.r], writes=[junk.r, ss2.r], accum=ss2.t[:, 1:2])
                    self.rstd(ss2.t[:, 0:1], ss2.t[:, 0:1], 256.0, epsb, reads=[ss2.r], writes=[ss2.r])
                    self.rstd(ss2.t[:, 1:2], ss2.t[:, 1:2], 128.0, epsb, reads=[ss2.r], writes=[ss2.r])
                    self.ts("dve", cqn.t[:, 0:256], pa0.t[:, 0:256], ss2.t[:, 0:1], None, ALU.mult, None, reads=[pa0.r, ss2.r], writes=[cqn.r])
                    self.ts("dve", cqn.t[:, 256:384], pa0.t[:, 256:384], ss2.t[:, 1:2], None, ALU.mult, None, reads=[pa0.r, ss2.r], writes=[cqn.r])
                    self.cp("act", V.t[:, 6:8, 0:64], pa0.t[:, 384:512].rearrange("p (h c) -> p h c", c=64), reads=[pa0.r], writes=[V.r])
                    self.cp("act", V.t[:, 8:16, 0:64], pa1.t[:, 0:512].rearrange("p (h c) -> p h c", c=64), reads=[pa1.r], writes=[V.r])
                    p = pT[s % 2]
                    for j in range(3):
                        self.tr(p.t[:, j * 128:(j + 1) * 128], cqn.t[:, j * 128:(j + 1) * 128], idb.t[:], reads=[cqn.r, idb.r], writes=[p.r])
                    for j in range(2):
                        self.ts("dve", cqT.t[:, j, s * 128:(s + 1) * 128], p.t[:, j * 128:(j + 1) * 128], gq.t[:, j:j + 1], None, ALU.mult, None,
                                reads=[p.r, gq.r], writes=[cqT.r])
                    self.ts("dve", ckvT.t[:, s * 128:(s + 1) * 128], p.t[:, 256:384], gkv.t[:, 0:1], None, ALU.mult, None,
                            reads=[p.r, gkv.r], writes=[ckvT.r])
                    pb = pB[s % 2]
                    self.mm(pb.t[:, 0:384], ckvT.t[:, s * 128:(s + 1) * 128], wukv.t[:, 0, 384:768], True, True, reads=[ckvT.r, wukv.r], writes=[pb.r])
                    self.cp("act", V.t[:, 0:6, 0:64], pb.t[:, 0:384].rearrange("p (h c) -> p h c", c=64), reads=[pb.r], writes=[V.r])
                    t0 = t * 512 + s * 128
                    self.store("V_all", self.V_all[:, t0:t0 + 128, :].rearrange("h p c -> p h c"), V, V.t[:, :, :], q="pool")
                pk = [0]

                def fm_mm(col0, m, rhsT, nchunk, wt, wsel):
                    pa = pA[pk[0] % 4]
                    pk[0] += 1
                    for c in range(nchunk):
                        self.mm(pa.t[0:m, :], wsel(c, col0, m), rhsT(c), c == 0, c == nchunk - 1, reads=[wt.r, xnT.r, cqT.r, ckvT.r], writes=[pa.r])
                    return pa

                w_in_sel = lambda c, col0, m: win.t[:, c, col0:col0 + m]
                x_rhs = lambda c: xnT.t[:, c, :]
                sc_d = 32 ** -0.5
                sc_c = 64 ** -0.5
                sc_m = 96 ** -0.5
                for j in range(3):
                    pa = fm_mm(1024 + j * 128, 128, x_rhs, 8, win, w_in_sel)
                    fm_out(pa.t[:, :], 128, sc_d, [("QT_diff", self.QT_diff[2 * j:2 * j + 2, :, tok].rearrange("h r c -> (h r) c"), 0, 128)], [pa.r])
                for j in range(3):
                    pa = fm_mm(1408 + j * 128, 128, x_rhs, 8, win, w_in_sel)
                    fm_out(pa.t[:, :], 128, 1.0, [("KT_diff", self.KT_diff[2 * j:2 * j + 2, :, tok].rearrange("h r c -> (h r) c"), 0, 128)], [pa.r])
                for j in range(2):
                    pa = fm_mm(1792 + j * 128, 128, x_rhs, 8, win, w_in_sel)
                    fm_out(pa.t[:, :], 128, sc_c, [("QT_ch", self.QT_ch[2 * j:2 * j + 2, :, tok].rearrange("h r c -> (h r) c"), 0, 128)], [pa.r])
                for j in range(2):
                    pa = fm_mm(2048 + j * 128, 128, x_rhs, 8, win, w_in_sel)
                    fm_out(pa.t[:, :], 128, 1.0, [("KT_ch", self.KT_ch[2 * j:2 * j + 2, :, tok].rearrange("h r c -> (h r) c"), 0, 128)], [pa.r])
                paA = fm_mm(2304, 32, x_rhs, 8, win, w_in_sel)
                paB = fm_mm(2336, 32, x_rhs, 8, win, w_in_sel)
                self.tt("dve", rt1.t[0:32, :], paA.t[0:32, :], c_.t[0:32, :], ALU.mult, reads=[paA.r, c_.r], writes=[rt1.r])
                self.tt("dve", rt2.t[0:32, :], paB.t[0:32, :], s_.t[0:32, :], ALU.mult, reads=[paB.r, s_.r], writes=[rt2.r])
                f = fst[fk[0] % 4]
                fk[0] += 1
                self.tt("dve", f.t[0:32, :], rt1.t[0:32, :], rt2.t[0:32, :], ALU.add, reads=[rt1.r, rt2.r], writes=[f.r])
                self.store("KTr_mla", self.KTr_mla[:, tok], f, f.t[0:32, :], q="sp")
                wuq_sel = lambda c, col0, m: wuq.t[:, c, col0:col0 + m]
                cq_rhs = lambda c: cqT.t[:, c, :]
                for j in range(3):
                    pa = fm_mm(j * 128, 128, cq_rhs, 2, wuq, wuq_sel)
                    fm_out(pa.t[:, :], 128, sc_m, [("QTn_mla", self.QTn_mla[2 * j:2 * j + 2, :, tok].rearrange("h r c -> (h r) c"), 0, 128)], [pa.r])
                for (h0, nh) in ((0, 4), (4, 2)):
                    m = nh * 32
                    paA = fm_mm(384 + h0 * 32, m, cq_rhs, 2, wuq, wuq_sel)
                    paB = fm_mm(576 + h0 * 32, m, cq_rhs, 2, wuq, wuq_sel)
                    self.stt("dve", rt1.t[0:m, :], paA.t[0:m, :], sc_m, c_.t[0:m, :], ALU.mult, ALU.mult, reads=[paA.r, c_.r], writes=[rt1.r])
                    self.stt("dve", rt2.t[0:m, :], paB.t[0:m, :], sc_m, s_.t[0:m, :], ALU.mult, ALU.mult, reads=[paB.r, s_.r], writes=[rt2.r])
                    f = fst[fk[0] % 4]
                    fk[0] += 1
                    self.tt("dve", f.t[0:m, :], rt1.t[0:m, :], rt2.t[0:m, :], ALU.add, reads=[rt1.r, rt2.r], writes=[f.r])
                    self.store("QTr_mla", self.QTr_mla[h0:h0 + nh, :, tok].rearrange("h r c -> (h r) c"), f, f.t[0:m, :], q="sp")
                wukv_sel = lambda c, col0, m: wukv.t[:, 0, col0:col0 + m]
                ckv_rhs = lambda c: ckvT.t[:, :]
                for j in range(3):
                    pa = fm_mm(j * 128, 128, ckv_rhs, 1, wukv, wukv_sel)
                    fm_out(pa.t[:, :], 128, 1.0, [("KTn_mla", self.KTn_mla[2 * j:2 * j + 2, :, tok].rearrange("h r c -> (h r) c"), 0, 128)], [pa.r])

    def attn(self, l, kind):
        S, NT, NB = self.S, self.NT, self.NB
        nh = {"mla": 6, "diff": 6, "ch": 4}[kind]
        Kd = {"mla": 96, "diff": 64, "ch": 64}[kind]
        QT_s = {"mla": None, "diff": self.QT_diff, "ch": self.QT_ch}[kind]
        KT_s = {"mla": None, "diff": self.KT_diff, "ch": self.KT_ch}[kind]
        vbase = {"mla": 0, "diff": 6, "ch": 12}[kind]
        colbase = {"mla": 0, "diff": 384, "ch": 768}[kind]
        nmaps = 2 if kind == "diff" else 1
        lam_init = 0.8 - 0.6 * math.exp(-0.3 * l)
        with self.phase("attn_" + kind):
            idf = self.sb("idf", [128, 128], F32)
            self.load(idf, idf.t[:], self.identf_in)
            qts = [self.sb("qt%d" % i, [Kd, S], BF16) for i in range(2)]
            kts = [self.sb("kt%d" % i, [Kd, S], BF16) for i in range(2)]
            vts = [self.sb("vt%d" % i, [128, NB, 128], BF16) for i in range(2)]
            if kind == "diff":
                Sp = [self.ps("Sp%d" % i, [128, 2, 512], F32) for i in range(2)]
            elif kind == "mla":
                Sp = [self.ps("Sp%d" % i, [128, 512], F32) for i in range(3)]
                Dm = self.ps("Dm", [128, 512], F32)
            else:
                Sp = [self.ps("Sp%d" % i, [128, 512], F32) for i in range(4)]
            Ac = [self.ps("Ac%d" % i, [128, 512], F32) for i in range(2)]
            Tp = [self.ps("Tp%d" % i, [128, 4, 128], F32) for i in range(2)]
            if kind == "diff":
                Pt = [self.sb("Pt%d" % i, [128, 2, 512], BF16) for i in range(3)]
            else:
                Pt = [self.sb("Pt%d" % i, [128, 512], BF16) for i in range(4)]
            accS = [self.sb("accS%d" % i, [128, 512], F32) for i in range(2)]
            ost = [self.sb("ost%d" % i, [128, 4, 64], F32) for i in range(2)]
            rc = [self.sb("rc%d" % i, [128, 4, 1], F32) for i in range(2)]
            if kind != "mla":
                if kind == "diff":
                    Tt = [self.sb("Tt%d" % i, [128, 2, 512], F32) for i in range(3)]
                else:
                    Tt = [self.sb("Tt%d" % i, [128, 512], F32) for i in range(4)]
            if kind == "diff":
                posq = self.sb("posq", [128, S], F32)
                for c0 in range(0, S, 2048):
                    w = min(2048, S - c0)
                    self.load(posq, posq.t[:, c0:c0 + w], self.posf[:, c0:c0 + w].partition_broadcast(128))
                npk = self.sb("npk", [128, NB], F32)
                self.load(npk, npk.t[:], self.negposk)
                Dt = [self.sb("Dt%d" % i, [128, 512], F32) for i in range(2)]
                zcol = self.sb("zcol", [128, 1], F32)
                self.memset("pool", zcol.t[:], 0.0, writes=[zcol.r])
                tmp = self.sb("tmp", [128, 4, 64], F32)
                lt = self.sb("lt", [128, 128], F32)
                self.load(lt, lt.t[:], self.lam_in[l].partition_broadcast(128))
                lp = self.sb("lp", [128, 64], F32)
                ls = self.sb("ls", [128, 2], F32)
                lam = self.sb("lam", [128, 1], F32)
                self.tt("dve", lp.t[:, 0:32], lt.t[:, 0:32], lt.t[:, 32:64], ALU.mult, reads=[lt.r], writes=[lp.r])
                self.tt("dve", lp.t[:, 32:64], lt.t[:, 64:96], lt.t[:, 96:128], ALU.mult, reads=[lt.r], writes=[lp.r])
                self.P.op("dve", lambda e: e.reduce_sum(out=ls.t[:, 0:1], in_=lp.t[:, 0:32], axis=AX.X), reads=[lp.r], writes=[ls.r])
                self.P.op("dve", lambda e: e.reduce_sum(out=ls.t[:, 1:2], in_=lp.t[:, 32:64], axis=AX.X), reads=[lp.r], writes=[ls.r])
                self.act(ls.t[:], ls.t[:], AF.Exp, reads=[ls.r], writes=[ls.r])
                self.tt("dve", lam.t[:], ls.t[:, 0:1], ls.t[:, 1:2], ALU.subtract, reads=[ls.r], writes=[lam.r])
                self.ts("dve", lam.t[:], lam.t[:], lam_init, None, ALU.add, None, reads=[lam.r], writes=[lam.r])
            if kind == "ch":
                cm = self.sb("cm", [128, 8, 512], F32)
                for i in range(8):
                    self.load(cm, cm.t[:, i, :], self.cmask_in[i])
                Bt = [self.sb("Bt%d" % i, [128, 8, 512], F32) for i in range(2)]
                rbt = self.sb("rbt", [128, 320], F32)
                ex = self.sb("ex", [128, 1536], F32)
                for hh in range(4):
                    self.load(rbt, rbt.t[:], self.rb_in[l, hh:hh + 1, :].partition_broadcast(128))
                    self.cp("dve", ex.t[:, 0:448], rbt.t[:, 0:1].to_broadcast([128, 448]), reads=[rbt.r], writes=[ex.r])
                    self.cp("dve", ex.t[:, 448:768], rbt.t[:, :], reads=[rbt.r], writes=[ex.r])
                    self.cp("dve", ex.t[:, 768:1536], rbt.t[:, 319:320].to_broadcast([128, 768]), reads=[rbt.r], writes=[ex.r])
                    self.store("ext", self.ext[hh], ex, ex.t[:])
            LAG = 2
            NS = len(Sp)

            def head_loads(h):
                qt_, kt_, vt_ = qts[h % 2], kts[h % 2], vts[h % 2]
                if kind == "mla":
                    self.load(qt_, qt_.t[0:64, :], self.QTn_mla[h])
                    self.load(qt_, qt_.t[64:96, :], self.QTr_mla[h])
                    self.load(kt_, kt_.t[0:64, :], self.KTn_mla[h])
                    self.load(kt_, kt_.t[64:96, :], self.KTr_mla)
                else:
                    self.load(qt_, qt_.t[:], QT_s[h, 0:Kd, :])
                    self.load(kt_, kt_.t[:], KT_s[h, 0:Kd, :])
                for j0 in range(0, NB, 16):
                    j1 = min(NB, j0 + 16)
                    self.load(vt_, vt_.t[:, j0:j1, :], self.V_all[vbase + h, j0 * 128:j1 * 128, :].rearrange("(j p) c -> p j c", p=128))
                if kind == "ch":
                    B = Bt[h % 2]
                    for i in range(8):
                        delta = 512 - 128 * i
                        src = bass.AP(self.ext.tensor, h * 128 * 1536 + delta + 511, [[1535, 128], [1, 512]])
                        self.load(B, B.t[:, i, :], src, src_name="ext")
                    self.tt("pool", B.t[:], B.t[:], cm.t[:], ALU.add, reads=[B.r, cm.r], writes=[B.r])

            steps = []
            for h in range(nh):
                hfirst = len(steps)
                for qt in range(NT):
                    if kind == "ch":
                        kbs = list(range(max(0, 4 * qt - 4), 4 * qt + 4))
                    else:
                        kbs = list(range(0, 4 * qt + 4))
                    for idx, kb in enumerate(kbs):
                        for m in range(nmaps):
                            steps.append(dict(h=h, qt=qt, kb=kb, m=m, first=(idx == 0), last=(idx == len(kbs) - 1),
                                              hstep=len(steps) - hfirst, i=len(steps)))
            cur = {}

            def stageA(st):
                h, qt, kb, m = st["h"], st["qt"], st["kb"], st["m"]
                if st["i"] == 0:
                    head_loads(0)
                if st["hstep"] == LAG and h + 1 < nh:
                    head_loads(h + 1)
                qt_, kt_ = qts[h % 2], kts[h % 2]
                q0 = qt * 512
                j = kb - 4 * qt
                c0 = 128 * j if (j > 0 and kind != "ch") else 0
                st["c0"] = c0
                if kind == "diff" and m == 0:
                    Dk = Dt[(st["i"] // 2) % 2]
                    cur["Dk"] = Dk
                    self.act(Dk.t[:, c0:512], posq.t[:, q0 + c0:q0 + 512], AF.Abs, reads=[posq.r, npk.r], writes=[Dk.r],
                             bias=npk.t[:, kb:kb + 1], scale=1.0)
                sp = Sp[st["i"] % NS]
                if kind == "diff":
                    r0, r1 = 32 * m, 32 * m + 32
                else:
                    r0, r1 = 0, Kd
                self.mm(sp.t[:, c0:512], kt_.t[r0:r1, kb * 128:(kb + 1) * 128], qt_.t[r0:r1, q0 + c0:q0 + 512], True, True,
                        reads=[kt_.r, qt_.r], writes=[sp.r])
                if kind == "mla" and DUMMY_MLA > 0:
                    self.mm(Dm.t[:, 0:DUMMY_MLA], kt_.t[r0:r1, kb * 128:(kb + 1) * 128], qt_.t[r0:r1, q0:q0 + DUMMY_MLA], True, True,
                            reads=[kt_.r, qt_.r], writes=[Dm.r])
                pt = Pt[st["i"] % len(Pt)]
                st["pt"] = pt
                if kind == "mla":
                    self.act(pt.t[:, c0:512], sp.t[:, c0:512], AF.Exp, reads=[sp.r], writes=[pt.r])
                else:
                    tt_ = Tt[st["i"] % len(Tt)]
                    if kind == "diff":
                        Dk = cur["Dk"]
                        self.stt("dve", tt_.t[:, c0:512], Dk.t[:, c0:512], -SLOPES[h], sp.t[:, c0:512], ALU.mult, ALU.add,
                                 reads=[Dk.r, sp.r], writes=[tt_.r])
                    else:
                        i = kb - (4 * qt - 4)
                        B = Bt[h % 2]
                        self.tt("dve", tt_.t[:, :], B.t[:, i, :], sp.t[:, :], ALU.add, reads=[B.r, sp.r], writes=[tt_.r])
                    self.act(pt.t[:, c0:512], tt_.t[:, c0:512], AF.Exp, reads=[tt_.r], writes=[pt.r])
                if j >= 0 and kind != "ch":
                    self.memset("pool", pt.t[64:128, c0:c0 + 64], 0.0, writes=[pt.r])

            def stageD(st):
                h, qt, kb, m = st["h"], st["qt"], st["kb"], st["m"]
                vt_ = vts[h % 2]
                c0 = st["c0"]
                pt = st["pt"]
                q0 = qt * 512
                acc = Ac[m] if kind == "diff" else Ac[qt % 2]
                self.mm(acc.t[:, c0:512], vt_.t[:, kb, :], pt.t[:, c0:512], st["first"], st["last"], reads=[vt_.r, pt.r], writes=[acc.r])
                if not (st["last"] and m == nmaps - 1):
                    return
                o_ = ost[qt % 2]
                tps = []
                for mm_ in range(nmaps):
                    acc = Ac[mm_] if kind == "diff" else Ac[qt % 2]
                    tp = Tp[mm_] if kind == "diff" else Tp[qt % 2]
                    rc_ = rc[mm_] if kind == "diff" else rc[qt % 2]
                    a_ = accS[mm_] if kind == "diff" else accS[qt % 2]
                    self.cp("dve", a_.t[:], acc.t[:], reads=[acc.r], writes=[a_.r])
                    for s_ in range(4):
                        self.tr(tp.t[:, s_, :], a_.t[:, s_ * 128:(s_ + 1) * 128], idf.t[:], reads=[a_.r, idf.r], writes=[tp.r])
                    self.recip(rc_.t[:], tp.t[:, :, 64:65], reads=[tp.r], writes=[rc_.r])
                    tps.append((tp, rc_))
                if kind == "diff":
                    (t0_, rc0), (t1_, rc1) = tps
                    self.ts("dve", rc1.t[:], rc1.t[:], lam.t[:, 0:1], None, ALU.mult, None, reads=[rc1.r, lam.r], writes=[rc1.r])
                    for s_ in range(4):
                        self.ts("dve", tmp.t[:, s_, :], t1_.t[:, s_, 0:64], rc1.t[:, s_, :], None, ALU.mult, None, reads=[t1_.r, rc1.r], writes=[tmp.r])
                        self.stt("dve", o_.t[:, s_, :], t0_.t[:, s_, 0:64], rc0.t[:, s_, :], tmp.t[:, s_, :], ALU.mult, ALU.subtract,
                                 reads=[t0_.r, rc0.r, tmp.r], writes=[o_.r])
                else:
                    (t0_, rc0), = tps
                    for s_ in range(4):
                        self.ts("dve", o_.t[:, s_, :], t0_.t[:, s_, 0:64], rc0.t[:, s_, :], None, ALU.mult, None, reads=[t0_.r, rc0.r], writes=[o_.r])
                col = colbase + h * 64
                self.store("O_tok", self.O_tok[q0:q0 + 512, col:col + 64].rearrange("(s p) c -> p s c", p=128), o_, o_.t[:])

            def finalize_diff(h, qt):
                o_ = ost[qt % 2]
                q0 = qt * 512
                tps = []
                for mm_ in range(2):
                    acc, tp, rc_, a_ = Ac[mm_], Tp[mm_], rc[mm_], accS[mm_]
                    self.cp("dve", a_.t[:], acc.t[:], reads=[acc.r], writes=[a_.r])
                    for s_ in range(4):
                        self.tr(tp.t[:, s_, :], a_.t[:, s_ * 128:(s_ + 1) * 128], idf.t[:], reads=[a_.r, idf.r], writes=[tp.r])
                    self.recip(rc_.t[:], tp.t[:, :, 64:65], reads=[tp.r], writes=[rc_.r])
                    tps.append((tp, rc_))
                (t0_, rc0), (t1_, rc1) = tps
                self.ts("dve", rc1.t[:], rc1.t[:], lam.t[:, 0:1], None, ALU.mult, None, reads=[rc1.r, lam.r], writes=[rc1.r])
                for s_ in range(4):
                    self.ts("dve", tmp.t[:, s_, :], t1_.t[:, s_, 0:64], rc1.t[:, s_, :], None, ALU.mult, None, reads=[t1_.r, rc1.r], writes=[tmp.r])
                    self.stt("dve", o_.t[:, s_, :], t0_.t[:, s_, 0:64], rc0.t[:, s_, :], tmp.t[:, s_, :], ALU.mult, ALU.subtract,
                             reads=[t0_.r, rc0.r, tmp.r], writes=[o_.r])
                col = colbase + h * 64
                self.store("O_tok", self.O_tok[q0:q0 + 512, col:col + 64].rearrange("(s p) c -> p s c", p=128), o_, o_.t[:])

            def pairA(st):
                h, qt, kb = st["h"], st["qt"], st["kb"]
                if st["i"] == 0:
                    head_loads(0)
                if st["hstep"] == 2 and h + 1 < nh:
                    head_loads(h + 1)
                qt_, kt_ = qts[h % 2], kts[h % 2]
                q0 = qt * 512
                j = kb - 4 * qt
                c0 = 128 * j if j > 0 else 0
                st["c0"] = c0
                Dk = Dt[st["i"] % 2]
                self.act(Dk.t[:, c0:512], posq.t[:, q0 + c0:q0 + 512], AF.Abs, reads=[posq.r, npk.r], writes=[Dk.r],
                         bias=npk.t[:, kb:kb + 1], scale=1.0)
                sp = Sp[st["i"] % 2]
                for m in range(2):
                    r0, r1 = 32 * m, 32 * m + 32
                    self.mm(sp.t[:, m, c0:512], kt_.t[r0:r1, kb * 128:(kb + 1) * 128], qt_.t[r0:r1, q0 + c0:q0 + 512], True, True,
                            reads=[kt_.r, qt_.r], writes=[sp.r])
                st["Dk"] = Dk
                st["sp"] = sp
                st["j"] = j

            def pairB(st):
                h = st["h"]
                c0, Dk, sp, j = st["c0"], st["Dk"], st["sp"], st["j"]
                tt_ = Tt[st["i"] % 3]
                pt = Pt[st["i"] % 3]
                st["pt"] = pt
                n = 512 - c0
                self.stt("dve", tt_.t[:, :, c0:512], Dk.t[:, c0:512].unsqueeze(1).to_broadcast([128, 2, n]), -SLOPES[h], sp.t[:, :, c0:512],
                         ALU.mult, ALU.add, reads=[Dk.r, sp.r], writes=[tt_.r])
                self.act(pt.t[:, :, c0:512], tt_.t[:, :, c0:512], AF.Exp, reads=[tt_.r], writes=[pt.r])
                if j >= 0:
                    self.memset("pool", pt.t[64:128, :, c0:c0 + 64], 0.0, writes=[pt.r])

            def pairD(st):
                h, qt, kb = st["h"], st["qt"], st["kb"]
                vt_ = vts[h % 2]
                c0 = st["c0"]
                pt = st["pt"]
                for m in range(2):
                    self.mm(Ac[m].t[:, c0:512], vt_.t[:, kb, :], pt.t[:, m, c0:512], st["first"], st["last"], reads=[vt_.r, pt.r], writes=[Ac[m].r])
                if st["last"]:
                    finalize_diff(h, qt)

            if kind == "diff":
                psteps = []
                for h in range(nh):
                    hfirst = len(psteps)
                    for qt in range(NT):
                        kbs = list(range(0, 4 * qt + 4))
                        for idx, kb in enumerate(kbs):
                            psteps.append(dict(h=h, qt=qt, kb=kb, first=(idx == 0), last=(idx == len(kbs) - 1),
                                               hstep=len(psteps) - hfirst, i=len(psteps)))
                N = len(psteps)
                for i in range(N + 2):
                    if i < N:
                        pairA(psteps[i])
                    if 0 <= i - 1 < N:
                        pairB(psteps[i - 1])
                    if i - 2 >= 0:
                        pairD(psteps[i - 2])
            else:
                N = len(steps)
                for i in range(N + LAG):
                    if i < N:
                        stageA(steps[i])
                    if i - LAG >= 0:
                        stageD(steps[i - LAG])

    def p3a(self, l, xsrc, xsrc_name):
        S, NT = self.S, self.NT
        lam_init = 0.8 - 0.6 * math.exp(-0.3 * l)
        with self.phase("p3a"):
            epsb = self.sb("epsb", [128, 2], F32)
            self.memset("pool", epsb.t[:, 0:1], EPS, writes=[epsb.r])
            self.memset("pool", epsb.t[:, 1:2], -0.5, writes=[epsb.r])
            idf = self.sb("idf", [128, 128], F32)
            self.load(idf, idf.t[:], self.identf_in)
            idb = self.sb("idb", [128, 128], BF16)
            self.cp("dve", idb.t[:], idf.t[:], reads=[idf.r], writes=[idb.r])
            go = self.sb("go", [128, 8], F32)
            self.load(go, go.t[:], self.g_o[l])
            stg = [self.sb("stg%d" % i, [128, 2048], F32) for i in range(2)]
            wo = self.sb("wo", [128, 8, D], BF16)
            self.load_w(wo, wo.t, self.w_out[l], 8, D, stg)
            ot = [self.sb("ot%d" % i, [128, 4, D], F32) for i in range(2)]
            xt = [self.sb("xt%d" % i, [128, 4, D], F32) for i in range(2)]
            on = self.sb("on", [128, 4, D], BF16)
            onT = self.sb("onT", [128, 8, 512], BF16)
            xo = [self.sb("xo%d" % i, [128, 4, D], F32) for i in range(2)]
            ssg = self.sb("ssg", [128, 4, 8], F32)
            junk = self.sb("junk", [128, 384], BF16)
            pT = [self.ps("pT%d" % i, [128, 512], BF16) for i in range(2)]
            pA = [self.ps("pA%d" % i, [128, 512], F32) for i in range(4)]
            groups = [(0, 384)] + [(384 + 64 * g, 64) for g in range(6)] + [(768, 256)]
            for t in range(NT):
                tok = slice(t * 512, (t + 1) * 512)
                o = ot[t % 2]
                x = xt[t % 2]
                self.load(o, o.t[:], self.O_tok[tok, :].rearrange("(s p) c -> p s c", p=128), src_name="O_tok")
                self.load(x, x.t[:], xsrc[tok, :].rearrange("(s p) c -> p s c", p=128), src_name=xsrc_name)
                for s in range(4):
                    for gi, (c0, n) in enumerate(groups):
                        self.act(junk.t[:, 0:n], o.t[:, s, c0:c0 + n], AF.Square, reads=[o.r], writes=[junk.r, ssg.r], accum=ssg.t[:, s, gi:gi + 1])
                self.act(ssg.t[:, :, 0:1], ssg.t[:, :, 0:1], AF.Ln, reads=[ssg.r, epsb.r], writes=[ssg.r], bias=epsb.t[:, 0:1], scale=1.0 / 384)
                self.act(ssg.t[:, :, 1:7], ssg.t[:, :, 1:7], AF.Ln, reads=[ssg.r, epsb.r], writes=[ssg.r], bias=epsb.t[:, 0:1], scale=1.0 / 64)
                self.act(ssg.t[:, :, 7:8], ssg.t[:, :, 7:8], AF.Ln, reads=[ssg.r, epsb.r], writes=[ssg.r], bias=epsb.t[:, 0:1], scale=1.0 / 256)
                self.act(ssg.t[:], ssg.t[:], AF.Exp, reads=[ssg.r], writes=[ssg.r], scale=-0.5)
                for s in range(4):
                    for gi, (c0, n) in enumerate(groups):
                        eng = "dve"
                        self.ts(eng, on.t[:, s, c0:c0 + n], o.t[:, s, c0:c0 + n], ssg.t[:, s, gi:gi + 1], None, ALU.mult, None, reads=[o.r, ssg.r], writes=[on.r])
                for c in range(8):
                    p = pT[c % 2]
                    for s in range(4):
                        self.tr(p.t[:, s * 128:(s + 1) * 128], on.t[:, s, c * 128:(c + 1) * 128], idb.t[:], reads=[on.r, idb.r], writes=[p.r])
                    if c in (3, 4, 5):
                        self.ts("dve", onT.t[:, c, :], p.t[:, 0:512], go.t[:, c:c + 1], 1.0 - lam_init, ALU.mult, ALU.mult, reads=[p.r, go.r], writes=[onT.r])
                    else:
                        self.act(onT.t[:, c, :], p.t[:, 0:512], AF.Copy, reads=[p.r, go.r], writes=[onT.r], scale=go.t[:, c:c + 1])
                xo_ = xo[t % 2]
                k = 0
                for s in range(4):
                    for hf in range(2):
                        pa = pA[k % 4]
                        k += 1
                        for c in range(8):
                            self.mm(pa.t[:], onT.t[:, c, s * 128:(s + 1) * 128], wo.t[:, c, hf * 512:(hf + 1) * 512], c == 0, c == 7, reads=[onT.r, wo.r], writes=[pa.r])
                        self.tt("dve", xo_.t[:, s, hf * 512:(hf + 1) * 512], x.t[:, s, hf * 512:(hf + 1) * 512], pa.t[:], ALU.add, reads=[x.r, pa.r], writes=[xo_.r])
                self.store("x1", self.x1[tok, :].rearrange("(s p) c -> p s c", p=128), xo_, xo_.t[:])

    def p3b(self, l, final):
        S = self.S
        TT = 256
        NTT = S // TT
        with self.phase("p3b"):
            epsb = self.sb("epsb", [128, 2], F32)
            self.memset("pool", epsb.t[:, 0:1], EPS, writes=[epsb.r])
            self.memset("pool", epsb.t[:, 1:2], -0.5, writes=[epsb.r])
            idf = self.sb("idf", [128, 128], F32)
            self.load(idf, idf.t[:], self.identf_in)
            idb = self.sb("idb", [128, 128], BF16)
            self.cp("dve", idb.t[:], idf.t[:], reads=[idf.r], writes=[idb.r])
            gT = self.sb("gT", [128, 8], F32)
            self.load(gT, gT.t[:], self.g_ffn[l])
            cw = self.sb("cw", [128, 44, 3], F32)
            self.load(cw, cw.t[:], self.cw_in[l].rearrange("p (j k) -> p j k", k=3))
            cb = self.sb("cb", [128, 44], F32)
            self.load(cb, cb.t[:], self.cb_in[l])
            stg = [self.sb("stg%d" % i, [128, 512], F32) for i in range(2)]
            wf1 = self.sb("wf1", [128, 8, 2 * DFF], BF16)
            wf2 = self.sb("wf2", [128, 22, D], BF16)

            def lw(wt, src2, nchunk, ncol):
                k = 0
                for c in range(nchunk):
                    for c0 in range(0, ncol, 512):
                        w = min(512, ncol - c0)
                        s = stg[k % 2]
                        k += 1
                        self.load(s, s.t[:, 0:w], src2[c * 128:(c + 1) * 128, c0:c0 + w])
                        self.cp("pool", wt.t[:, c, c0:c0 + w], s.t[:, 0:w], reads=[s.r], writes=[wt.r])
            lw(wf1, self.w_f1[l], 8, 2 * DFF)
            lw(wf2, self.w_f2[l], 22, D)
            if final:
                gfb = self.sb("gfb", [128, D], F32)
                self.load(gfb, gfb.t[:], self.g_fin.partition_broadcast(128))
            xt = [self.sb("xt%d" % i, [128, 2, D], F32) for i in range(1)]
            xn = self.sb("xn", [128, 2, D], BF16)
            xnT = self.sb("xnT", [128, 8, TT], BF16)
            actTs = [self.sb("actT%d" % i, [128, 22, TT], BF16) for i in range(2)]
            ss = self.sb("ss", [128, 4], F32)
            junk = self.sb("junk", [128, D], BF16)
            hal = self.sb("hal", [128, 44, 2], F32)
            hal2 = self.sb("hal2", [128, 44, 2], F32)
            c1 = [self.sb("c1_%d" % i, [128, TT], F32) for i in range(4)]
            c2 = c1
            c3 = c1
            sg = [self.sb("sg%d" % i, [128, TT], F32) for i in range(2)]
            stt_ = self.sb("cst", [128, 44, 2], F32)
            self.memset("pool", stt_.t[:], 0.0, writes=[stt_.r])
            xo = self.sb("xo", [128, 2, D], F32)
            pT = [self.ps("pT%d" % i, [128, 512], BF16) for i in range(2)]
            pH = [self.ps("pH%d" % i, [128, 512], F32) for i in range(4)]
            pY = [self.ps("pY%d" % i, [128, 512], F32) for i in range(2)]
            kk = 0

            dmy = pT[1].t[:, 0:1024].bitcast(F32)

            def o_chunk(tp, i):
                tokp = slice(tp * TT, (tp + 1) * TT)
                aT = actTs[tp % 2]
                if i == 0:
                    self.load(xo, xo.t[:], self.x1[tokp, :].rearrange("(s p) c -> p s c", p=128))
                g0 = 0 if i < 11 else 2
                ii = (i % 11) * 2
                for g in (g0, g0 + 1):
                    s_, hf = divmod(g, 2)
                    py = pY[g % 2]
                    for k_ in (ii, ii + 1):
                        self.mm(py.t[:], aT.t[:, k_, s_ * 128:(s_ + 1) * 128], wf2.t[:, k_, hf * 512:(hf + 1) * 512], k_ == 0, k_ == 21,
                                reads=[aT.r, wf2.r], writes=[py.r])
                if i % 11 == 10:
                    for g in (g0, g0 + 1):
                        s_, hf = divmod(g, 2)
                        py = pY[g % 2]
                        self.tt("dve", xo.t[:, s_, hf * 512:(hf + 1) * 512], xo.t[:, s_, hf * 512:(hf + 1) * 512], py.t[:], ALU.add,
                                reads=[xo.r, py.r], writes=[xo.r])
                if i == 21:
                    if not final:
                        self.store("x2", self.x2[tokp, :].rearrange("(s p) c -> p s c", p=128), xo, xo.t[:])
                    else:
                        for s_ in range(2):
                            self.act(junk.t[:], xo.t[:, s_, :], AF.Square, reads=[xo.r], writes=[junk.r, ss.r], accum=ss.t[:, 2 + s_:3 + s_])
                        self.rstd(ss.t[:, 2:4], ss.t[:, 2:4], float(D), epsb, reads=[ss.r], writes=[ss.r])
                        for s_ in range(2):
                            self.stt("dve", xo.t[:, s_, :], xo.t[:, s_, :], ss.t[:, 2 + s_:3 + s_], gfb.t[:], ALU.mult, ALU.mult,
                                     reads=[xo.r, ss.r, gfb.r], writes=[xo.r])
                        self.store("out", self.out[tokp, :].rearrange("(s p) c -> p s c", p=128), xo, xo.t[:])

            def emit_O(tp, i):
                if i is None:
                    for i_ in range(22):
                        o_chunk(tp, i_)
                else:
                    o_chunk(tp, i)

            for t in range(NTT):
                tok = slice(t * TT, (t + 1) * TT)
                x = xt[0]
                actT = actTs[t % 2]
                self.load(x, x.t[:], self.x1[tok, :].rearrange("(s p) c -> p s c", p=128))
                self.norm_T(x, 2, gT, xn, xnT, pT, idb, epsb, ss, junk)
                self.tt("dve", hal.t[:, :, 0], cw.t[:, :, 1], stt_.t[:, :, 1], ALU.mult, reads=[cw.r, stt_.r], writes=[hal.r])
                self.tt("dve", hal2.t[:, :, 0], cw.t[:, :, 0], stt_.t[:, :, 0], ALU.mult, reads=[cw.r, stt_.r], writes=[hal2.r])
                self.tt("dve", hal.t[:, :, 1], cw.t[:, :, 0], stt_.t[:, :, 1], ALU.mult, reads=[cw.r, stt_.r, hal.r], writes=[hal.r])
                self.tt("dve", hal.t[:, :, 0], hal.t[:, :, 0], hal2.t[:, :, 0], ALU.add, reads=[hal.r, hal2.r], writes=[hal.r])
                for i in range(22):
                    un = []
                    for which in range(2):
                        idx = i + 22 * which
                        col0 = idx * 128
                        ph = pH[kk % 4]
                        a = c1[kk % 4]
                        kk += 1
                        for c in range(8):
                            self.mm(ph.t[:, 0:TT], wf1.t[:, c, col0:col0 + 128], xnT.t[:, c, :], c == 0, c == 7, reads=[wf1.r, xnT.r], writes=[ph.r])
                        un.append((idx, ph, a))
                    for (idx, ph, a) in un:
                        self.act(a.t[:], ph.t[:, 0:TT], AF.Identity, reads=[ph.r, cw.r, cb.r], writes=[a.r], bias=cb.t[:, idx:idx + 1], scale=cw.t[:, idx, 2:3])
                    for (idx, ph, a) in un:
                        self.stt("dve", a.t[:, 1:TT], ph.t[:, 0:TT - 1], cw.t[:, idx, 1:2], a.t[:, 1:TT], ALU.mult, ALU.add, reads=[ph.r, cw.r, a.r], writes=[a.r])
                    for (idx, ph, a) in un:
                        self.stt("dve", a.t[:, 2:TT], ph.t[:, 0:TT - 2], cw.t[:, idx, 0:1], a.t[:, 2:TT], ALU.mult, ALU.add, reads=[ph.r, cw.r, a.r], writes=[a.r])
                    for (idx, ph, a) in un:
                        self.tt("pool", a.t[:, 0:2], a.t[:, 0:2], hal.t[:, idx, :], ALU.add, reads=[a.r, hal.r], writes=[a.r])
                    for (idx, ph, a) in un:
                        self.cp("act", stt_.t[:, idx, :], ph.t[:, TT - 2:TT], reads=[ph.r], writes=[stt_.r])
                    s_ = sg[i % 2]
                    self.act(s_.t[:], un[1][2].t[:], AF.Silu, reads=[un[1][2].r], writes=[s_.r])
                    self.tt("pool", actT.t[:, i, :], s_.t[:], un[0][2].t[:], ALU.mult, reads=[s_.r, un[0][2].r], writes=[actT.r])
                    if t > 0:
                        emit_O(t - 1, i)
                    for _ in range(DUMMY_FFN):
                        self.mm(dmy, wf2.t[:, 0, 0:128], wf2.t[:, 1, 0:512], True, True, reads=[wf2.r], writes=[pT[1].r])
            emit_O(NTT - 1, None)

    def build(self):
        import os
        ph = os.environ.get("K_PHASES", "p0,p1,mla,diff,ch,p3a,p3b").split(",")
        nl = int(os.environ.get("K_LAYERS", str(DEPTH)))
        self.declare()
        if "p0" in ph:
            self.p0()
        xsrc, xname = self.x_in, None
        for l in range(nl):
            if "p1" in ph:
                self.p1(l, xsrc, xname)
            for kd in ("mla", "diff", "ch"):
                if kd in ph:
                    self.attn(l, kd)
            if "p3a" in ph:
                self.p3a(l, xsrc, xname)
            if "p3b" in ph:
                self.p3b(l, final=(l == DEPTH - 1))
            xsrc, xname = self.x2, None
        self.gst.close()
        return self.nc


def _host_consts():
    identf = np.eye(128, dtype=np.float32)
    p = np.arange(128)
    inv = (10000.0 ** (-(np.arange(16, dtype=np.float32)) / 16)).astype(np.float32)
    invf = np.zeros((128, 2), np.float32)
    invf[:, 0] = inv[p % 16]
    sign = np.where((p % 32) < 16, -1.0, 1.0).astype(np.float32)
    invf[:, 1] = inv[p % 16] * sign
    cmask = np.zeros((8, 128, 512), np.float32)
    ki = np.arange(128)[:, None] // 64
    qi = np.arange(512)[None, :] // 64
    for i in range(8):
        delta = 512 - 128 * i
        d = delta // 64 + qi - ki
        cmask[i] = np.where((d >= 0) & (d <= 8), 0.0, NEG)
    return identf, invf, cmask


def _prep_weights(inp):
    f = lambda a: np.ascontiguousarray(np.asarray(a, dtype=np.float32))
    L = DEPTH
    w_in = f(inp["w_in"])
    cq, ckv, kr = w_in[:, :, 0:256], w_in[:, :, 256:384], w_in[:, :, 384:416]
    dq, dk, dv = w_in[:, :, 416:800], w_in[:, :, 800:1184], w_in[:, :, 1184:1568]
    chq, chk, chv = w_in[:, :, 1568:1824], w_in[:, :, 1824:2080], w_in[:, :, 2080:2336]
    swap = np.concatenate([np.arange(16, 32), np.arange(0, 16)])
    w_in_d = np.concatenate([cq, ckv, dv, chv, dq, dk, chq, chk, kr, kr[:, :, swap]], axis=2)
    assert w_in_d.shape[2] == NCOL_IN
    wuq = f(inp["mla_w_uq"]).reshape(L, 256, 6, 96)
    nope = wuq[..., 0:64].reshape(L, 256, 384)
    rope = wuq[..., 64:96]
    w_uq_d = np.concatenate([nope, rope.reshape(L, 256, 192), rope[..., swap].reshape(L, 256, 192)], axis=2)
    wukv = f(inp["mla_w_ukv"]).reshape(L, 128, 6, 128)
    w_ukv_d = np.concatenate([wukv[..., 0:64].reshape(L, 128, 384), wukv[..., 64:128].reshape(L, 128, 384)], axis=2)

    def colT(g, n):
        return np.ascontiguousarray(f(g).reshape(L, n, 128).transpose(0, 2, 1))
    g_o = np.concatenate([f(inp["mla_out_norm"]), f(inp["diff_norm"]), f(inp["chunk_out_norm"])], axis=1)
    cwt = f(inp["ffn_conv_w"])
    cw = np.ascontiguousarray(cwt.reshape(L, 3, 44, 128).transpose(0, 3, 2, 1)).reshape(L, 128, 132)
    cb = np.ascontiguousarray(f(inp["ffn_conv_b"]).reshape(L, 44, 128).transpose(0, 2, 1))
    return {
        "w_in": np.ascontiguousarray(w_in_d), "w_uq": np.ascontiguousarray(w_uq_d), "w_ukv": np.ascontiguousarray(w_ukv_d),
        "w_out": f(inp["w_out"]), "w_f1": f(inp["w_ffn_in"]), "w_f2": f(inp["w_ffn_out"]),
        "g_attn": colT(inp["attn_norm"], 8), "g_ffn": colT(inp["ffn_norm"], 8),
        "g_q": colT(inp["mla_q_norm"], 2), "g_kv": colT(inp["mla_kv_norm"], 1), "g_o": colT(g_o, 8),
        "lam": f(inp["diff_lambda"]).reshape(L, 1, 128), "rb": f(inp["chunk_rel_bias"]),
        "cw": cw, "cb": cb, "g_fin": f(inp["final_norm"]).reshape(1, D),
    }


_NC_CACHE = {}


def run_cores(inp, S, ncores, debug=False):
    key = (S, debug)
    if key not in _NC_CACHE:
        _NC_CACHE[key] = K(S, debug).build()
    nc = _NC_CACHE[key]
    identf, invf, cmask = _host_consts()
    wd = _prep_weights(inp)
    x = np.asarray(inp["x"], dtype=np.float32)
    pos = np.asarray(inp["positions"]).astype(np.int32)
    in_maps = []
    for b in range(ncores):
        m = dict(wd)
        m["x"] = np.ascontiguousarray(x[b, :S])
        m["pos"] = np.ascontiguousarray(pos[b, :S].reshape(S // 128, 128))
        m["identf"] = identf
        m["invf"] = invf
        m["cmask"] = cmask
        in_maps.append(m)
    res = run_bass_kernel_spmd(nc, in_maps, core_ids=list(range(ncores)))
    return res


def kernel(**inputs):
    B, S = inputs["x"].shape[0], inputs["x"].shape[1]
    res = run_cores(inputs, S, B)
    return np.stack([np.asarray(r["out"], dtype=np.float32) for r in res.results], axis=0)
```

```python
import contextlib

ENGS = ("pe", "act", "dve", "pool", "sp")
ENGOBJ = {"pe": "tensor", "act": "scalar", "dve": "vector", "pool": "gpsimd", "sp": "sync"}


class Res:
    __slots__ = ("name", "lw", "rd", "slot", "base", "dcount", "excl")

    def __init__(self, name="", excl=False):
        self.name = name
        self.excl = excl
        self.lw = None
        self.rd = []
        self.slot = None
        self.base = 0
        self.dcount = 0


class Op:
    __slots__ = ("eng", "fn", "deps", "ddeps", "pos", "sig", "count", "dres", "kind")

    def __init__(self, eng, fn, kind):
        self.eng = eng
        self.fn = fn
        self.kind = kind
        self.deps = []
        self.ddeps = []
        self.sig = False
        self.count = 0
        self.dres = None


class Prog:
    def __init__(self, nc, stack, nslots=88, same_engine_sync=True):
        self.nc = nc
        self.same_engine_sync = same_engine_sync
        self.esem = {e: stack.enter_context(nc.semaphore("s_" + e)) for e in ENGS}
        self.ecount = {e: 0 for e in ENGS}
        self.slots = [[stack.enter_context(nc.semaphore("d%d" % i)), 0] for i in range(nslots)]
        self.free = list(range(nslots))
        self.total_ops = 0
        self.begin()

    def begin(self):
        self.streams = {e: [] for e in ENGS}
        self.dma_res = []

    def _add(self, op, reads, writes):
        if any(r.excl for r in reads):
            writes = list(writes) + [r for r in reads if r.excl and r not in writes]
            reads = [r for r in reads if not r.excl]
        deps = []
        for r in reads:
            if r.lw is not None:
                deps.append(r.lw)
        for w in writes:
            if w.lw is not None:
                deps.append(w.lw)
            deps.extend(w.rd)
        seen = set()
        for d in deps:
            if id(d) in seen or d is op:
                continue
            seen.add(id(d))
            if d.kind == "d":
                op.ddeps.append((d.dres, d.dres.dcount * 16))
            else:
                if d.eng == op.eng:
                    if d.eng == "pe":
                        continue
                    if not self.same_engine_sync:
                        continue
                op.deps.append(d)
        for r in reads:
            r.rd.append(op)
        for w in writes:
            w.lw = op
            w.rd = []
        op.pos = len(self.streams[op.eng])
        self.streams[op.eng].append(op)
        return op

    def op(self, eng, fn, reads=(), writes=(), mm=False):
        o = Op(eng, fn, "mm" if mm else "c")
        return self._add(o, reads, writes)

    def dma(self, queue, out_ap, in_ap, dst, reads=(), writes=(), **kw):
        if dst.slot is None:
            dst.slot = self.free.pop()
            dst.base = self.slots[dst.slot][1]
            dst.dcount = 0
            self.dma_res.append(dst)
        o = Op(queue, lambda e: e.dma_start(out=out_ap, in_=in_ap, **kw), "d")
        o.dres = dst
        self._add(o, reads, writes)
        dst.dcount += 1
        return o

    def end(self):
        nc = self.nc
        waited = {e: {f: -1 for f in ENGS} for e in ENGS}
        dwaited = {e: {} for e in ENGS}
        plan = {e: [] for e in ENGS}
        for e in ENGS:
            for op in self.streams[e]:
                best = {}
                for d in op.deps:
                    if d.pos > waited[e][d.eng]:
                        if d.eng not in best or d.pos > best[d.eng].pos:
                            best[d.eng] = d
                for f, d in best.items():
                    waited[e][f] = d.pos
                    d.sig = True
                dws = []
                for (res, cnt) in op.ddeps:
                    if dwaited[e].get(id(res), 0) < cnt:
                        dwaited[e][id(res)] = cnt
                        dws.append((res, cnt))
                plan[e].append((op, list(best.values()), dws))
        for e in ENGS:
            c = self.ecount[e]
            for op in self.streams[e]:
                if op.sig:
                    c += 1
                    op.count = c
            self.ecount[e] = c
        esem = self.esem
        slots = self.slots
        dma_res = list(self.dma_res)
        with nc.Block() as block:
            def run(e):
                def body(eng):
                    for (op, cw, dw) in plan[e]:
                        for d in cw:
                            eng.wait_ge(esem[d.eng], d.count)
                        for (res, cnt) in dw:
                            eng.wait_ge(slots[res.slot][0], res.base + cnt)
                        ins = op.fn(eng)
                        if op.kind == "d":
                            ins.then_inc(slots[op.dres.slot][0], 16)
                        elif op.sig:
                            ins.then_inc(esem[e], 1)
                    if e == "sp":
                        for r in dma_res:
                            eng.wait_ge(slots[r.slot][0], r.base + r.dcount * 16)
                return body

            for e in ENGS:
                getattr(block, ENGOBJ[e])(run(e))
        for r in dma_res:
            slots[r.slot][1] = r.base + r.dcount * 16
            self.free.append(r.slot)
            r.slot = None
            r.lw = None
            r.rd = []
        n = sum(len(self.streams[e]) for e in ENGS)
        self.total_ops += n
        self.begin()
        return n
import math
import numpy as np
import ml_dtypes
import concourse.bass as bass
import concourse.mybir as mybir
from concourse.bass_utils import run_bass_kernel_spmd

F32 = mybir.dt.float32
BF16 = mybir.dt.bfloat16
I32 = mybir.dt.int32
AF = mybir.ActivationFunctionType
ALU = mybir.AluOpType
AX = mybir.AxisListType

D = 1024
DEPTH = 2
EPS = 1e-6
NCOL_IN = 2368
DFF = 2816
TWO_PI = 2.0 * math.pi
NEG = -30000.0
SLOPES = [2.0 ** (-8.0 * (i + 1) / 6) for i in range(6)]
DUMMY_MLA = 256
DUMMY_FFN = 0


class Tl:
    __slots__ = ("t", "r")

    def __init__(self, t, r):
        self.t = t
        self.r = r


class K:
    def __init__(self, S, debug=False):
        self.S = S
        self.NT = S // 512
        self.NB = S // 128
        self.debug = debug
        self.nc = bass.Bass("TRN2", target_bir_lowering=False)
        self.gst = contextlib.ExitStack()
        self.P = Prog(self.nc, self.gst)
        self.st = None
        self.dres = {}
        self.uid = 0

    def din(self, name, shape, dt):
        return self.nc.dram_tensor(name, list(shape), dt, kind="ExternalInput").ap()

    def dscr(self, name, shape, dt):
        kind = "ExternalOutput" if self.debug else "Internal"
        return self.nc.dram_tensor(name, list(shape), dt, kind=kind).ap()

    def sb(self, name, shape, dt):
        self.uid += 1
        name = "sb%d_%s" % (self.uid, name)
        t = self.st.enter_context(self.nc.sbuf_tensor(name, list(shape), dt))
        return Tl(t, Res(name))

    def ps(self, name, shape, dt):
        self.uid += 1
        name = "ps%d_%s" % (self.uid, name)
        if dt == BF16 and list(shape) == [128, 512]:
            shape = [128, 1024]
        t = self.st.enter_context(self.nc.psum_tensor(name, list(shape), dt))
        return Tl(t, Res(name, excl=True))

    def dr(self, name):
        if name not in self.dres:
            self.dres[name] = Res(name)
        return self.dres[name]

    @contextlib.contextmanager
    def phase(self, name):
        self.st = contextlib.ExitStack()
        self.dres = {}
        with self.st:
            yield
            n = self.P.end()
        self.st = None

    def load(self, tl, dst_ap, src_ap, src_name=None, q="sp", **kw):
        reads = [self.dr(src_name)] if src_name else []
        self.P.dma(q, dst_ap, src_ap, tl.r, reads=reads, writes=[tl.r], **kw)

    def store(self, dname, dst_ap, tl, src_ap, q="pool", **kw):
        r = self.dr(dname)
        self.P.dma(q, dst_ap, src_ap, r, reads=[tl.r], writes=[r], **kw)

    def mm(self, out, lhsT, rhs, start, stop, reads, writes):
        self.P.op("pe", lambda e: e.matmul(out, lhsT=lhsT, rhs=rhs, start=start, stop=stop),
                  reads=reads, writes=writes, mm=True)

    def tr(self, out, in_, ident, reads, writes):
        self.P.op("pe", lambda e: e.transpose(out, in_, ident), reads=reads, writes=writes, mm=True)

    def act(self, out, in_, func, reads, writes, bias=None, scale=None, accum=None):
        kw = {}
        if bias is not None:
            kw["bias"] = bias
        if scale is not None:
            kw["scale"] = scale
        if accum is not None:
            kw["accum_out"] = accum
        self.P.op("act", lambda e: e.activation(out=out, in_=in_, func=func, **kw), reads=reads, writes=writes)

    def ts(self, eng, out, in0, s1, s2, op0, op1, reads, writes):
        if op1 is None:
            self.P.op(eng, lambda e: e.tensor_scalar(out=out, in0=in0, scalar1=s1, scalar2=None, op0=op0),
                      reads=reads, writes=writes)
        else:
            self.P.op(eng, lambda e: e.tensor_scalar(out=out, in0=in0, scalar1=s1, scalar2=s2, op0=op0, op1=op1),
                      reads=reads, writes=writes)

    def tt(self, eng, out, in0, in1, op, reads, writes):
        self.P.op(eng, lambda e: e.tensor_tensor(out=out, in0=in0, in1=in1, op=op), reads=reads, writes=writes)

    def stt(self, eng, out, in0, scalar, in1, op0, op1, reads, writes):
        self.P.op(eng, lambda e: e.scalar_tensor_tensor(out=out, in0=in0, scalar=scalar, in1=in1, op0=op0, op1=op1),
                  reads=reads, writes=writes)

    def cp(self, eng, out, in_, reads, writes):
        if eng == "act":
            self.P.op(eng, lambda e: e.copy(out=out, in_=in_), reads=reads, writes=writes)
        else:
            self.P.op(eng, lambda e: e.tensor_copy(out=out, in_=in_), reads=reads, writes=writes)

    def memset(self, eng, ap, val, writes):
        self.P.op(eng, lambda e: e.memset(ap, val), reads=(), writes=writes)

    def recip(self, out, in_, reads, writes):
        self.P.op("dve", lambda e: e.reciprocal(out=out, in_=in_), reads=reads, writes=writes)

    def rstd(self, out, ss, n, epsb, reads, writes):
        self.act(out, ss, AF.Ln, reads=list(reads) + [epsb.r], writes=writes, bias=epsb.t[:, 0:1], scale=1.0 / n)
        self.act(out, out, AF.Exp, reads=writes, writes=writes, scale=-0.5)

    def load_w(self, wt, dst3, src2, nchunk, ncol, stg, src_name=None):
        CW = 2048
        k = 0
        for c in range(nchunk):
            for c0 in range(0, ncol, CW):
                w = min(CW, ncol - c0)
                s = stg[k % len(stg)]
                k += 1
                self.load(s, s.t[:, 0:w], src2[c * 128:(c + 1) * 128, c0:c0 + w])
                self.cp("pool", dst3[:, c, c0:c0 + w], s.t[:, 0:w], reads=[s.r], writes=[wt.r])

    def declare(self):
        S = self.S
        self.x_in = self.din("x", [S, D], F32)
        self.pos_in = self.din("pos", [S // 128, 128], I32)
        self.identf_in = self.din("identf", [128, 128], F32)
        self.invf_in = self.din("invf", [128, 2], F32)
        self.cmask_in = self.din("cmask", [8, 128, 512], F32)
        self.w_in = self.din("w_in", [DEPTH, D, NCOL_IN], F32)
        self.w_uq = self.din("w_uq", [DEPTH, 256, 768], F32)
        self.w_ukv = self.din("w_ukv", [DEPTH, 128, 768], F32)
        self.w_out = self.din("w_out", [DEPTH, D, D], F32)
        self.w_f1 = self.din("w_f1", [DEPTH, D, 2 * DFF], F32)
        self.w_f2 = self.din("w_f2", [DEPTH, DFF, D], F32)
        self.g_attn = self.din("g_attn", [DEPTH, 128, 8], F32)
        self.g_ffn = self.din("g_ffn", [DEPTH, 128, 8], F32)
        self.g_q = self.din("g_q", [DEPTH, 128, 2], F32)
        self.g_kv = self.din("g_kv", [DEPTH, 128, 1], F32)
        self.g_o = self.din("g_o", [DEPTH, 128, 8], F32)
        self.lam_in = self.din("lam", [DEPTH, 1, 128], F32)
        self.rb_in = self.din("rb", [DEPTH, 4, 320], F32)
        self.cw_in = self.din("cw", [DEPTH, 128, 44 * 3], F32)
        self.cb_in = self.din("cb", [DEPTH, 128, 44], F32)
        self.g_fin = self.din("g_fin", [1, D], F32)
        self.out = self.nc.dram_tensor("out", [S, D], F32, kind="ExternalOutput").ap()
        self.posf = self.dscr("posf", [1, S], F32)
        self.negposk = self.dscr("negposk", [128, S // 128], F32)
        self.cosT = self.dscr("cosT", [self.NT, 128, 512], F32)
        self.sinT = self.dscr("sinT", [self.NT, 128, 512], F32)
        self.QTn_mla = self.dscr("QTn_mla", [6, 64, S], BF16)
        self.QTr_mla = self.dscr("QTr_mla", [6, 32, S], BF16)
        self.KTn_mla = self.dscr("KTn_mla", [6, 64, S], BF16)
        self.KTr_mla = self.dscr("KTr_mla", [32, S], BF16)
        self.V_all = self.dscr("V_all", [16, S, 128], BF16)
        self.QT_diff = self.dscr("QT_diff", [6, 64, S], BF16)
        self.KT_diff = self.dscr("KT_diff", [6, 64, S], BF16)
        self.QT_ch = self.dscr("QT_ch", [4, 64, S], BF16)
        self.KT_ch = self.dscr("KT_ch", [4, 64, S], BF16)
        self.O_tok = self.dscr("O_tok", [S, D], F32)
        self.x1 = self.dscr("x1", [S, D], F32)
        self.x2 = self.dscr("x2", [S, D], F32)
        self.ext = self.dscr("ext", [4, 128, 1536], F32)

    def p0(self):
        S, NB, NT = self.S, self.NB, self.NT
        with self.phase("p0"):
            idf = self.sb("idf", [128, 128], F32)
            self.load(idf, idf.t[:], self.identf_in)
            invf = self.sb("invf", [128, 2], F32)
            self.load(invf, invf.t[:], self.invf_in)
            pi = self.sb("pi", [NB, 128], I32)
            self.load(pi, pi.t[:], self.pos_in)
            pf = self.sb("pf", [NB, 128], F32)
            self.cp("dve", pf.t[:], pi.t[:], reads=[pi.r], writes=[pf.r])
            self.store("posf", self.posf.rearrange("o (j p) -> (o j) p", p=128), pf, pf.t[:])
            pst = self.ps("pst", [128, 512], F32)
            self.tr(pst.t[:, 0:NB], pf.t[0:NB, :], idf.t[0:NB, 0:NB], reads=[pf.r, idf.r], writes=[pst.r])
            npk = self.sb("npk", [128, NB], F32)
            self.ts("dve", npk.t[:], pst.t[:, 0:NB], -1.0, None, ALU.mult, None, reads=[pst.r], writes=[npk.r])
            self.store("negposk", self.negposk, npk, npk.t[:])
            pbc = [self.sb("pbc%d" % i, [128, 512], F32) for i in range(2)]
            ang = self.sb("ang", [128, 512], F32)
            ni = self.sb("ni", [128, 512], I32)
            nf = self.sb("nf", [128, 512], F32)
            y = self.sb("y", [128, 512], F32)
            res = [self.sb("res%d" % i, [128, 512], F32) for i in range(2)]
            k = 0
            for t in range(NT):
                pb = pbc[t % 2]
                self.load(pb, pb.t[:], self.posf[:, t * 512:(t + 1) * 512].partition_broadcast(128), src_name="posf")
                for which in range(2):
                    col = invf.t[:, which:which + 1]
                    if which == 0:
                        self.ts("dve", ang.t[:], pb.t[:], col, math.pi / 2, ALU.mult, ALU.add, reads=[pb.r, invf.r], writes=[ang.r])
                    else:
                        self.ts("dve", ang.t[:], pb.t[:], col, None, ALU.mult, None, reads=[pb.r, invf.r], writes=[ang.r])
                    self.ts("dve", ni.t[:], ang.t[:], 1.0 / TWO_PI, None, ALU.mult, None, reads=[ang.r], writes=[ni.r])
                    self.cp("dve", nf.t[:], ni.t[:], reads=[ni.r], writes=[nf.r])
                    self.stt("dve", y.t[:], nf.t[:], -TWO_PI, ang.t[:], ALU.mult, ALU.add, reads=[nf.r, ang.r], writes=[y.r])
                    self.ts("dve", nf.t[:], y.t[:], math.pi, -TWO_PI, ALU.is_gt, ALU.mult, reads=[y.r], writes=[nf.r])
                    self.tt("dve", y.t[:], y.t[:], nf.t[:], ALU.add, reads=[y.r, nf.r], writes=[y.r])
                    self.ts("dve", nf.t[:], y.t[:], -math.pi, TWO_PI, ALU.is_lt, ALU.mult, reads=[y.r], writes=[nf.r])
                    self.tt("dve", y.t[:], y.t[:], nf.t[:], ALU.add, reads=[y.r, nf.r], writes=[y.r])
                    self.ts("dve", y.t[:], y.t[:], math.pi, -math.pi, ALU.min, ALU.max, reads=[y.r], writes=[y.r])
                    r_ = res[k % 2]
                    k += 1
                    self.act(r_.t[:], y.t[:], AF.Sin, reads=[y.r], writes=[r_.r])
                    dst = self.cosT if which == 0 else self.sinT
                    self.store("cosT" if which == 0 else "sinT", dst[t], r_, r_.t[:])

    def norm_T(self, xt, ns, gT, xn, xnT, pT, idb, epsb, ss, junk, evac_engs=("act", "dve")):
        for s in range(ns):
            self.act(junk.t[:], xt.t[:, s, :], AF.Square, reads=[xt.r], writes=[junk.r, ss.r], accum=ss.t[:, s:s + 1])
        self.rstd(ss.t[:, 0:ns], ss.t[:, 0:ns], float(D), epsb, reads=[ss.r], writes=[ss.r])
        for s in range(ns):
            self.ts("dve", xn.t[:, s, :], xt.t[:, s, :], ss.t[:, s:s + 1], None, ALU.mult, None, reads=[xt.r, ss.r], writes=[xn.r])
        for c in range(8):
            p = pT[c % len(pT)]
            for s in range(ns):
                self.tr(p.t[:, s * 128:(s + 1) * 128], xn.t[:, s, c * 128:(c + 1) * 128], idb.t[:], reads=[xn.r, idb.r], writes=[p.r])
            eng = evac_engs[c % len(evac_engs)]
            if eng == "act":
                self.act(xnT.t[:, c, :], p.t[:, 0:ns * 128], AF.Copy, reads=[p.r, gT.r], writes=[xnT.r], scale=gT.t[:, c:c + 1])
            else:
                self.ts("dve", xnT.t[:, c, :], p.t[:, 0:ns * 128], gT.t[:, c:c + 1], None, ALU.mult, None, reads=[p.r, gT.r], writes=[xnT.r])

    def p1(self, l, xsrc, xsrc_name):
        S, NT = self.S, self.NT
        with self.phase("p1"):
            epsb = self.sb("epsb", [128, 2], F32)
            self.memset("pool", epsb.t[:, 0:1], EPS, writes=[epsb.r])
            self.memset("pool", epsb.t[:, 1:2], -0.5, writes=[epsb.r])
            idf = self.sb("idf", [128, 128], F32)
            self.load(idf, idf.t[:], self.identf_in)
            idb = self.sb("idb", [128, 128], BF16)
            self.cp("dve", idb.t[:], idf.t[:], reads=[idf.r], writes=[idb.r])
            gT = self.sb("gT", [128, 8], F32)
            self.load(gT, gT.t[:], self.g_attn[l])
            gq = self.sb("gq", [128, 2], F32)
            self.load(gq, gq.t[:], self.g_q[l])
            gkv = self.sb("gkv", [128, 1], F32)
            self.load(gkv, gkv.t[:], self.g_kv[l])
            stg = [self.sb("stg%d" % i, [128, 2048], F32) for i in range(2)]
            win = self.sb("win", [128, 8, NCOL_IN], BF16)
            self.load_w(win, win.t, self.w_in[l], 8, NCOL_IN, stg)
            wuq = self.sb("wuq", [128, 2, 768], BF16)
            self.load_w(wuq, wuq.t, self.w_uq[l], 2, 768, stg)
            wukv = self.sb("wukv", [128, 1, 768], BF16)
            self.load_w(wukv, wukv.t, self.w_ukv[l], 1, 768, stg)
            xt = [self.sb("xt%d" % i, [128, 4, D], F32) for i in range(2)]
            xns = [self.sb("xn%d" % i, [128, 4, D], BF16) for i in range(2)]
            xnTs = [self.sb("xnT%d" % i, [128, 8, 512], BF16) for i in range(2)]
            sss = [self.sb("ss%d" % i, [128, 4], F32) for i in range(2)]
            junks = [self.sb("junk%d" % i, [128, D], BF16) for i in range(2)]
            cs = [self.sb("cs%d" % i, [128, 512], F32) for i in range(2)]
            sn = [self.sb("sn%d" % i, [128, 512], F32) for i in range(2)]
            pT = [self.ps("pT%d" % i, [128, 512], BF16) for i in range(2)]
            pA = [self.ps("pA%d" % i, [128, 512], F32) for i in range(4)]
            pB = [self.ps("pB%d" % i, [128, 512], F32) for i in range(2)]
            vst = [self.sb("vst%d" % i, [128, 16, 128], BF16) for i in range(4)]
            for v in vst:
                self.memset("pool", v.t[:, :, 64:128], 1.0, writes=[v.r])
            cqns = [self.sb("cqn%d" % i, [128, 384], BF16) for i in range(2)]
            cqTs = [self.sb("cqT%d" % i, [128, 2, 512], BF16) for i in range(2)]
            ckvTs = [self.sb("ckvT%d" % i, [128, 512], BF16) for i in range(2)]
            ss2s = [self.sb("ss2_%d" % i, [128, 2], F32) for i in range(2)]
            fst = [self.sb("fst%d" % i, [128, 512], BF16) for i in range(4)]
            rt1 = self.sb("rt1", [128, 512], F32)
            rt2 = self.sb("rt2", [128, 512], F32)
            fk = [0]
            sq = [0]

            def fm_out(ps_ap, rows, scale, dsts, preads):
                f = fst[fk[0] % 4]
                fk[0] += 1
                if fk[0] % 2 == 0:
                    self.act(f.t[0:rows, :], ps_ap, AF.Copy, reads=preads, writes=[f.r], scale=scale)
                else:
                    self.ts("dve", f.t[0:rows, :], ps_ap, scale, None, ALU.mult, None, reads=preads, writes=[f.r])
                for (dn, dap, r0, r1) in dsts:
                    sq[0] += 1
                    self.store(dn, dap, f, f.t[r0:r1, :], q=("sp" if sq[0] % 2 else "pool"))

            def tile_loads(t):
                x = xt[t % 2]
                tok = slice(t * 512, (t + 1) * 512)
                self.load(x, x.t[:], xsrc[tok, :].rearrange("(s p) c -> p s c", p=128), src_name=xsrc_name)
                self.load(cs[t % 2], cs[t % 2].t[:], self.cosT[t], src_name="cosT")
                self.load(sn[t % 2], sn[t % 2].t[:], self.sinT[t], src_name="sinT")

            tile_loads(0)
            for t in range(NT):
                x = xt[t % 2]
                tok = slice(t * 512, (t + 1) * 512)
                c_ = cs[t % 2]
                s_ = sn[t % 2]
                if t + 1 < NT:
                    tile_loads(t + 1)
                xn, xnT, ss, junk = xns[t % 2], xnTs[t % 2], sss[t % 2], junks[t % 2]
                cqT, ckvT = cqTs[t % 2], ckvTs[t % 2]
                if t == 0:
                    self.norm_T(x, 4, gT, xn, xnT, pT, idb, epsb, ss, junk)
                for s in range(4):
                    pa0 = pA[(2 * s) % 4]
                    pa1 = pA[(2 * s + 1) % 4]
                    for hf, pa in ((0, pa0), (1, pa1)):
                        for c in range(8):
                            self.mm(pa.t[:], xnT.t[:, c, s * 128:(s + 1) * 128], win.t[:, c, hf * 512:(hf + 1) * 512],
                                    c == 0, c == 7, reads=[xnT.r, win.r], writes=[pa.r])
                    cqn, ss2 = cqns[s % 2], ss2s[s % 2]
                    V = vst[s]
                    junk = junks[(s + 1) % 2]
                    self.act(junk.t[:, 0:256], pa0.t[:, 0:256], AF.Square, reads=[pa0.r], writes=[junk.r, ss2.r], accum=ss2.t[:, 0:1])
                    self.act(junk.t[:, 0:128], pa0.t[:, 256:384], AF.Square, reads=[pa0.r], writes=[junk.r, ss2.r], accum=ss2.t[:, 1:2])
                    self.rstd(ss2.t[:, 0:1], ss2.t[:, 0:1], 256.0, epsb, reads=[ss2.r], writes=[ss2.r])
                    self.rstd(ss2.t[:, 1:2], ss2.t[:, 1:2], 128.0, epsb, reads=[ss2.r], writes=[ss2.r])
                    self.ts("dve", cqn.t[:, 0:256], pa0.t[:, 0:256], ss2.t[:, 0:1], None, ALU.mult, None, reads=[pa0.r, ss2.r], writes=[cqn.r])
                    self.ts("dve", cqn.t[:, 256:384], pa0.t[:, 256:384], ss2.t[:, 1:2], None, ALU.mult, None, reads=[pa0.r, ss2.r], writes=[cqn.r])
                    self.cp("act", V.t[:, 6:8, 0:64], pa0.t[:, 384:512].rearrange("p (h c) -> p h c", c=64), reads=[pa0.r], writes=[V.r])
                    self.cp("act", V.t[:, 8:16, 0:64], pa1.t[:, 0:512].rearrange("p (h c) -> p h c", c=64), reads=[pa1.r], writes=[V.r])
                    p = pT[s % 2]
                    for j in range(3):
                        self.tr(p.t[:, j * 128:(j + 1) * 128], cqn.t[:, j * 128:(j + 1) * 128], idb.t[:], reads=[cqn.r, idb.r], writes=[p.r])
                    for j in range(2):
                        self.ts("dve", cqT.t[:, j, s * 128:(s + 1) * 128], p.t[:, j * 128:(j + 1) * 128], gq.t[:, j:j + 1], None, ALU.mult, None,
                                reads=[p.r, gq.r], writes=[cqT.r])
                    self.ts("dve", ckvT.t[:, s * 128:(s + 1) * 128], p.t[:, 256:384], gkv.t[:, 0:1], None, ALU.mult, None,
                            reads=[p.r, gkv.r], writes=[ckvT.r])
                    pb = pB[s % 2]
                    self.mm(pb.t[:, 0:384], ckvT.t[:, s * 128:(s + 1) * 128], wukv.t[:, 0, 384:768], True, True, reads=[ckvT.r, wukv.r], writes=[pb.r])
                    self.cp("act", V.t[:, 0:6, 0:64], pb.t[:, 0:384].rearrange("p (h c) -> p h c", c=64), reads=[pb.r], writes=[V.r])
                    t0 = t * 512 + s * 128
                    self.store("V_all", self.V_all[:, t0:t0 + 128, :].rearrange("h p c -> p h c"), V, V.t[:, :, :], q="pool")
                if t + 1 < NT:
                    t1_ = t + 1
                    self.norm_T(xt[t1_ % 2], 4, gT, xns[t1_ % 2], xnTs[t1_ % 2], pT, idb, epsb, sss[t1_ % 2], junks[t1_ % 2])
                pk = [0]

                def fm_mm(col0, m, rhsT, nchunk, wt, wsel):
                    pa = pA[pk[0] % 4]
                    pk[0] += 1
                    for c in range(nchunk):
                        self.mm(pa.t[0:m, :], wsel(c, col0, m), rhsT(c), c == 0, c == nchunk - 1, reads=[wt.r, xnT.r, cqT.r, ckvT.r], writes=[pa.r])
                    return pa

                w_in_sel = lambda c, col0, m: win.t[:, c, col0:col0 + m]
                x_rhs = lambda c: xnT.t[:, c, :]
                sc_d = 32 ** -0.5
                sc_c = 64 ** -0.5
                sc_m = 96 ** -0.5
                for j in range(3):
                    pa = fm_mm(1024 + j * 128, 128, x_rhs, 8, win, w_in_sel)
                    fm_out(pa.t[:, :], 128, sc_d, [("QT_diff", self.QT_diff[2 * j:2 * j + 2, :, tok].rearrange("h r c -> (h r) c"), 0, 128)], [pa.r])
                for j in range(3):
                    pa = fm_mm(1408 + j * 128, 128, x_rhs, 8, win, w_in_sel)
                    fm_out(pa.t[:, :], 128, 1.0, [("KT_diff", self.KT_diff[2 * j:2 * j + 2, :, tok].rearrange("h r c -> (h r) c"), 0, 128)], [pa.r])
                for j in range(2):
                    pa = fm_mm(1792 + j * 128, 128, x_rhs, 8, win, w_in_sel)
                    fm_out(pa.t[:, :], 128, sc_c, [("QT_ch", self.QT_ch[2 * j:2 * j + 2, :, tok].rearrange("h r c -> (h r) c"), 0, 128)], [pa.r])
                for j in range(2):
                    pa = fm_mm(2048 + j * 128, 128, x_rhs, 8, win, w_in_sel)
                    fm_out(pa.t[:, :], 128, 1.0, [("KT_ch", self.KT_ch[2 * j:2 * j + 2, :, tok].rearrange("h r c -> (h r) c"), 0, 128)], [pa.r])
                paA = fm_mm(2304, 32, x_rhs, 8, win, w_in_sel)
                paB = fm_mm(2336, 32, x_rhs, 8, win, w_in_sel)
                self.tt("dve", rt1.t[0:32, :], paA.t[0:32, :], c_.t[0:32, :], ALU.mult, reads=[paA.r, c_.r], writes=[rt1.r])
                self.tt("dve", rt2.t[0:32, :], paB.t[0:32, :], s_.t[0:32, :], ALU.mult, reads=[paB.r, s_.r], writes=[rt2.r])
                f = fst[fk[0] % 4]
                fk[0] += 1
                self.tt("dve", f.t[0:32, :], rt1.t[0:32, :], rt2.t[0:32, :], ALU.add, reads=[rt1.r, rt2.r], writes=[f.r])
                self.store("KTr_mla", self.KTr_mla[:, tok], f, f.t[0:32, :], q="sp")
                wuq_sel = lambda c, col0, m: wuq.t[:, c, col0:col0 + m]
                cq_rhs = lambda c: cqT.t[:, c, :]
                for j in range(3):
                    pa = fm_mm(j * 128, 128, cq_rhs, 2, wuq, wuq_sel)
                    fm_out(pa.t[:, :], 128, sc_m, [("QTn_mla", self.QTn_mla[2 * j:2 * j + 2, :, tok].rearrange("h r c -> (h r) c"), 0, 128)], [pa.r])
                for (h0, nh) in ((0, 4), (4, 2)):
                    m = nh * 32
                    paA = fm_mm(384 + h0 * 32, m, cq_rhs, 2, wuq, wuq_sel)
                    paB = fm_mm(576 + h0 * 32, m, cq_rhs, 2, wuq, wuq_sel)
                    self.stt("dve", rt1.t[0:m, :], paA.t[0:m, :], sc_m, c_.t[0:m, :], ALU.mult, ALU.mult, reads=[paA.r, c_.r], writes=[rt1.r])
                    self.stt("dve", rt2.t[0:m, :], paB.t[0:m, :], sc_m, s_.t[0:m, :], ALU.mult, ALU.mult, reads=[paB.r, s_.r], writes=[rt2.r])
                    f = fst[fk[0] % 4]
                    fk[0] += 1
                    self.tt("dve", f.t[0:m, :], rt1.t[0:m, :], rt2.t[0:m, :], ALU.add, reads=[rt1.r, rt2.r], writes=[f.r])
                    self.store("QTr_mla", self.QTr_mla[h0:h0 + nh, :, tok].rearrange("h r c -> (h r) c"), f, f.t[0:m, :], q="sp")
                wukv_sel = lambda c, col0, m: wukv.t[:, 0, col0:col0 + m]
                ckv_rhs = lambda c: ckvT.t[:, :]
                for j in range(3):
                    pa = fm_mm(j * 128, 128, ckv_rhs, 1, wukv, wukv_sel)
                    fm_out(pa.t[:, :], 128, 1.0, [("KTn_mla", self.KTn_mla[2 * j:2 * j + 2, :, tok].rearrange("h r c -> (h r) c"), 0, 128)], [pa.r])

    def attn(self, l, kind):
        S, NT, NB = self.S, self.NT, self.NB
        nh = {"mla": 6, "diff": 6, "ch": 4}[kind]
        Kd = {"mla": 96, "diff": 64, "ch": 64}[kind]
        QT_s = {"mla": None, "diff": self.QT_diff, "ch": self.QT_ch}[kind]
        KT_s = {"mla": None, "diff": self.KT_diff, "ch": self.KT_ch}[kind]
        vbase = {"mla": 0, "diff": 6, "ch": 12}[kind]
        colbase = {"mla": 0, "diff": 384, "ch": 768}[kind]
        nmaps = 2 if kind == "diff" else 1
        lam_init = 0.8 - 0.6 * math.exp(-0.3 * l)
        with self.phase("attn_" + kind):
            idf = self.sb("idf", [128, 128], F32)
            self.load(idf, idf.t[:], self.identf_in)
            qts = [self.sb("qt%d" % i, [Kd, S], BF16) for i in range(2)]
            kts = [self.sb("kt%d" % i, [Kd, S], BF16) for i in range(2)]
            vts = [self.sb("vt%d" % i, [128, NB, 128], BF16) for i in range(2)]
            if kind == "diff":
                Sp = [self.ps("Sp%d" % i, [128, 2, 512], F32) for i in range(2)]
            elif kind == "mla":
                Sp = [self.ps("Sp%d" % i, [128, 512], F32) for i in range(3)]
                Dm = self.ps("Dm", [128, 512], F32)
            else:
                Sp = [self.ps("Sp%d" % i, [128, 512], F32) for i in range(4)]
            Ac = [self.ps("Ac%d" % i, [128, 512], F32) for i in range(2)]
            Tp = [self.ps("Tp%d" % i, [128, 4, 128], F32) for i in range(2)]
            if kind == "diff":
                Pt = [self.sb("Pt%d" % i, [128, 2, 512], BF16) for i in range(3)]
            else:
                Pt = [self.sb("Pt%d" % i, [128, 512], BF16) for i in range(4)]
            accS = [self.sb("accS%d" % i, [128, 512], F32) for i in range(2)]
            ost = [self.sb("ost%d" % i, [128, 4, 64], F32) for i in range(2)]
            rc = [self.sb("rc%d" % i, [128, 4, 1], F32) for i in range(2)]
            if kind != "mla":
                if kind == "diff":
                    Tt = [self.sb("Tt%d" % i, [128, 2, 512], F32) for i in range(3)]
                else:
                    Tt = [self.sb("Tt%d" % i, [128, 512], F32) for i in range(4)]
            if kind == "diff":
                posq = self.sb("posq", [128, S], F32)
                for c0 in range(0, S, 2048):
                    w = min(2048, S - c0)
                    self.load(posq, posq.t[:, c0:c0 + w], self.posf[:, c0:c0 + w].partition_broadcast(128))
                npk = self.sb("npk", [128, NB], F32)
                self.load(npk, npk.t[:], self.negposk)
                Dt = [self.sb("Dt%d" % i, [128, 512], F32) for i in range(2)]
                zcol = self.sb("zcol", [128, 1], F32)
                self.memset("pool", zcol.t[:], 0.0, writes=[zcol.r])
                tmp = self.sb("tmp", [128, 4, 64], F32)
                lt = self.sb("lt", [128, 128], F32)
                self.load(lt, lt.t[:], self.lam_in[l].partition_broadcast(128))
                lp = self.sb("lp", [128, 64], F32)
                ls = self.sb("ls", [128, 2], F32)
                lam = self.sb("lam", [128, 1], F32)
                self.tt("dve", lp.t[:, 0:32], lt.t[:, 0:32], lt.t[:, 32:64], ALU.mult, reads=[lt.r], writes=[lp.r])
                self.tt("dve", lp.t[:, 32:64], lt.t[:, 64:96], lt.t[:, 96:128], ALU.mult, reads=[lt.r], writes=[lp.r])
                self.P.op("dve", lambda e: e.reduce_sum(out=ls.t[:, 0:1], in_=lp.t[:, 0:32], axis=AX.X), reads=[lp.r], writes=[ls.r])
                self.P.op("dve", lambda e: e.reduce_sum(out=ls.t[:, 1:2], in_=lp.t[:, 32:64], axis=AX.X), reads=[lp.r], writes=[ls.r])
                self.act(ls.t[:], ls.t[:], AF.Exp, reads=[ls.r], writes=[ls.r])
                self.tt("dve", lam.t[:], ls.t[:, 0:1], ls.t[:, 1:2], ALU.subtract, reads=[ls.r], writes=[lam.r])
                self.ts("dve", lam.t[:], lam.t[:], lam_init, None, ALU.add, None, reads=[lam.r], writes=[lam.r])
            if kind == "ch":
                cm = self.sb("cm", [128, 8, 512], F32)
                for i in range(8):
                    self.load(cm, cm.t[:, i, :], self.cmask_in[i])
                Bt = [self.sb("Bt%d" % i, [128, 8, 512], F32) for i in range(2)]
                rbt = self.sb("rbt", [128, 320], F32)
                ex = self.sb("ex", [128, 1536], F32)
                for hh in range(4):
                    self.load(rbt, rbt.t[:], self.rb_in[l, hh:hh + 1, :].partition_broadcast(128))
                    self.cp("dve", ex.t[:, 0:448], rbt.t[:, 0:1].to_broadcast([128, 448]), reads=[rbt.r], writes=[ex.r])
                    self.cp("dve", ex.t[:, 448:768], rbt.t[:, :], reads=[rbt.r], writes=[ex.r])
                    self.cp("dve", ex.t[:, 768:1536], rbt.t[:, 319:320].to_broadcast([128, 768]), reads=[rbt.r], writes=[ex.r])
                    self.store("ext", self.ext[hh], ex, ex.t[:])
            LAG = 2
            NS = len(Sp)

            def head_loads(h):
                qt_, kt_, vt_ = qts[h % 2], kts[h % 2], vts[h % 2]
                if kind == "mla":
                    self.load(qt_, qt_.t[0:64, :], self.QTn_mla[h])
                    self.load(qt_, qt_.t[64:96, :], self.QTr_mla[h])
                    self.load(kt_, kt_.t[0:64, :], self.KTn_mla[h])
                    self.load(kt_, kt_.t[64:96, :], self.KTr_mla)
                else:
                    self.load(qt_, qt_.t[:], QT_s[h, 0:Kd, :])
                    self.load(kt_, kt_.t[:], KT_s[h, 0:Kd, :])
                for j0 in range(0, NB, 16):
                    j1 = min(NB, j0 + 16)
                    self.load(vt_, vt_.t[:, j0:j1, :], self.V_all[vbase + h, j0 * 128:j1 * 128, :].rearrange("(j p) c -> p j c", p=128))
                if kind == "ch":
                    B = Bt[h % 2]
                    for i in range(8):
                        delta = 512 - 128 * i
                        src = bass.AP(self.ext.tensor, h * 128 * 1536 + delta + 511, [[1535, 128], [1, 512]])
                        self.load(B, B.t[:, i, :], src, src_name="ext")
                    self.tt("pool", B.t[:], B.t[:], cm.t[:], ALU.add, reads=[B.r, cm.r], writes=[B.r])

            steps = []
            for h in range(nh):
                hfirst = len(steps)
                for qt in range(NT):
                    if kind == "ch":
                        kbs = list(range(max(0, 4 * qt - 4), 4 * qt + 4))
                    else:
                        kbs = list(range(0, 4 * qt + 4))
                    for idx, kb in enumerate(kbs):
                        for m in range(nmaps):
                            steps.append(dict(h=h, qt=qt, kb=kb, m=m, first=(idx == 0), last=(idx == len(kbs) - 1),
                                              hstep=len(steps) - hfirst, i=len(steps)))
            cur = {}

            def stageA(st):
                h, qt, kb, m = st["h"], st["qt"], st["kb"], st["m"]
                if st["i"] == 0:
                    head_loads(0)
                if st["hstep"] == LAG and h + 1 < nh:
                    head_loads(h + 1)
                qt_, kt_ = qts[h % 2], kts[h % 2]
                q0 = qt * 512
                j = kb - 4 * qt
                c0 = 128 * j if (j > 0 and kind != "ch") else 0
                st["c0"] = c0
                if kind == "diff" and m == 0:
                    Dk = Dt[(st["i"] // 2) % 2]
                    cur["Dk"] = Dk
                    self.act(Dk.t[:, c0:512], posq.t[:, q0 + c0:q0 + 512], AF.Abs, reads=[posq.r, npk.r], writes=[Dk.r],
                             bias=npk.t[:, kb:kb + 1], scale=1.0)
                sp = Sp[st["i"] % NS]
                if kind == "diff":
                    r0, r1 = 32 * m, 32 * m + 32
                else:
                    r0, r1 = 0, Kd
                self.mm(sp.t[:, c0:512], kt_.t[r0:r1, kb * 128:(kb + 1) * 128], qt_.t[r0:r1, q0 + c0:q0 + 512], True, True,
                        reads=[kt_.r, qt_.r], writes=[sp.r])
                if kind == "mla" and DUMMY_MLA > 0:
                    self.mm(Dm.t[:, 0:DUMMY_MLA], kt_.t[r0:r1, kb * 128:(kb + 1) * 128], qt_.t[r0:r1, q0:q0 + DUMMY_MLA], True, True,
                            reads=[kt_.r, qt_.r], writes=[Dm.r])
                pt = Pt[st["i"] % len(Pt)]
                st["pt"] = pt
                if kind == "mla":
                    self.act(pt.t[:, c0:512], sp.t[:, c0:512], AF.Exp, reads=[sp.r], writes=[pt.r])
                else:
                    tt_ = Tt[st["i"] % len(Tt)]
                    if kind == "diff":
                        Dk = cur["Dk"]
                        self.stt("dve", tt_.t[:, c0:512], Dk.t[:, c0:512], -SLOPES[h], sp.t[:, c0:512], ALU.mult, ALU.add,
                                 reads=[Dk.r, sp.r], writes=[tt_.r])
                    else:
                        i = kb - (4 * qt - 4)
                        B = Bt[h % 2]
                        self.tt("dve", tt_.t[:, :], B.t[:, i, :], sp.t[:, :], ALU.add, reads=[B.r, sp.r], writes=[tt_.r])
                    self.act(pt.t[:, c0:512], tt_.t[:, c0:512], AF.Exp, reads=[tt_.r], writes=[pt.r])
                if j >= 0 and kind != "ch":
                    self.memset("pool", pt.t[64:128, c0:c0 + 64], 0.0, writes=[pt.r])

            def stageD(st):
                h, qt, kb, m = st["h"], st["qt"], st["kb"], st["m"]
                vt_ = vts[h % 2]
                c0 = st["c0"]
                pt = st["pt"]
                q0 = qt * 512
                acc = Ac[m] if kind == "diff" else Ac[qt % 2]
                self.mm(acc.t[:, c0:512], vt_.t[:, kb, :], pt.t[:, c0:512], st["first"], st["last"], reads=[vt_.r, pt.r], writes=[acc.r])
                if not (st["last"] and m == nmaps - 1):
                    return
                self.cp("dve", accS[qt % 2].t[:], Ac[qt % 2].t[:], reads=[Ac[qt % 2].r], writes=[accS[qt % 2].r])
                gpend.append(st)

            def stageF(st):
                h, qt, kb, m = st["h"], st["qt"], st["kb"], st["m"]
                q0 = qt * 512
                o_ = ost[qt % 2]
                tps = []
                for mm_ in range(nmaps):
                    acc = Ac[mm_] if kind == "diff" else Ac[qt % 2]
                    tp = Tp[mm_] if kind == "diff" else Tp[qt % 2]
                    rc_ = rc[mm_] if kind == "diff" else rc[qt % 2]
                    a_ = accS[mm_] if kind == "diff" else accS[qt % 2]
                    for s_ in range(4):
                        self.tr(tp.t[:, s_, :], a_.t[:, s_ * 128:(s_ + 1) * 128], idf.t[:], reads=[a_.r, idf.r], writes=[tp.r])
                    self.recip(rc_.t[:], tp.t[:, :, 64:65], reads=[tp.r], writes=[rc_.r])
                    tps.append((tp, rc_))
                if kind == "diff":
                    (t0_, rc0), (t1_, rc1) = tps
                    self.ts("dve", rc1.t[:], rc1.t[:], lam.t[:, 0:1], None, ALU.mult, None, reads=[rc1.r, lam.r], writes=[rc1.r])
                    for s_ in range(4):
                        self.ts("dve", tmp.t[:, s_, :], t1_.t[:, s_, 0:64], rc1.t[:, s_, :], None, ALU.mult, None, reads=[t1_.r, rc1.r], writes=[tmp.r])
                        self.stt("dve", o_.t[:, s_, :], t0_.t[:, s_, 0:64], rc0.t[:, s_, :], tmp.t[:, s_, :], ALU.mult, ALU.subtract,
                                 reads=[t0_.r, rc0.r, tmp.r], writes=[o_.r])
                else:
                    (t0_, rc0), = tps
                    for s_ in range(4):
                        self.ts("dve", o_.t[:, s_, :], t0_.t[:, s_, 0:64], rc0.t[:, s_, :], None, ALU.mult, None, reads=[t0_.r, rc0.r], writes=[o_.r])
                col = colbase + h * 64
                self.store("O_tok", self.O_tok[q0:q0 + 512, col:col + 64].rearrange("(s p) c -> p s c", p=128), o_, o_.t[:])

            def finalize_diff_evac():
                for mm_ in range(2):
                    self.cp("dve", accS[mm_].t[:], Ac[mm_].t[:], reads=[Ac[mm_].r], writes=[accS[mm_].r])

            def finalize_diff(h, qt):
                o_ = ost[qt % 2]
                q0 = qt * 512
                tps = []
                for mm_ in range(2):
                    acc, tp, rc_, a_ = Ac[mm_], Tp[mm_], rc[mm_], accS[mm_]
                    for s_ in range(4):
                        self.tr(tp.t[:, s_, :], a_.t[:, s_ * 128:(s_ + 1) * 128], idf.t[:], reads=[a_.r, idf.r], writes=[tp.r])
                    self.recip(rc_.t[:], tp.t[:, :, 64:65], reads=[tp.r], writes=[rc_.r])
                    tps.append((tp, rc_))
                (t0_, rc0), (t1_, rc1) = tps
                self.ts("dve", rc1.t[:], rc1.t[:], lam.t[:, 0:1], None, ALU.mult, None, reads=[rc1.r, lam.r], writes=[rc1.r])
                for s_ in range(4):
                    self.ts("dve", tmp.t[:, s_, :], t1_.t[:, s_, 0:64], rc1.t[:, s_, :], None, ALU.mult, None, reads=[t1_.r, rc1.r], writes=[tmp.r])
                    self.stt("dve", o_.t[:, s_, :], t0_.t[:, s_, 0:64], rc0.t[:, s_, :], tmp.t[:, s_, :], ALU.mult, ALU.subtract,
                             reads=[t0_.r, rc0.r, tmp.r], writes=[o_.r])
                col = colbase + h * 64
                self.store("O_tok", self.O_tok[q0:q0 + 512, col:col + 64].rearrange("(s p) c -> p s c", p=128), o_, o_.t[:])

            def pairA(st):
                h, qt, kb = st["h"], st["qt"], st["kb"]
                if st["i"] == 0:
                    head_loads(0)
                if st["hstep"] == 2 and h + 1 < nh:
                    head_loads(h + 1)
                qt_, kt_ = qts[h % 2], kts[h % 2]
                q0 = qt * 512
                j = kb - 4 * qt
                c0 = 128 * j if j > 0 else 0
                st["c0"] = c0
                Dk = Dt[st["i"] % 2]
                self.act(Dk.t[:, c0:512], posq.t[:, q0 + c0:q0 + 512], AF.Abs, reads=[posq.r, npk.r], writes=[Dk.r],
                         bias=npk.t[:, kb:kb + 1], scale=1.0)
                sp = Sp[st["i"] % 2]
                for m in range(2):
                    r0, r1 = 32 * m, 32 * m + 32
                    self.mm(sp.t[:, m, c0:512], kt_.t[r0:r1, kb * 128:(kb + 1) * 128], qt_.t[r0:r1, q0 + c0:q0 + 512], True, True,
                            reads=[kt_.r, qt_.r], writes=[sp.r])
                st["Dk"] = Dk
                st["sp"] = sp
                st["j"] = j

            def pairB(st):
                h = st["h"]
                c0, Dk, sp, j = st["c0"], st["Dk"], st["sp"], st["j"]
                tt_ = Tt[st["i"] % 3]
                pt = Pt[st["i"] % 3]
                st["pt"] = pt
                n = 512 - c0
                self.stt("dve", tt_.t[:, :, c0:512], Dk.t[:, c0:512].unsqueeze(1).to_broadcast([128, 2, n]), -SLOPES[h], sp.t[:, :, c0:512],
                         ALU.mult, ALU.add, reads=[Dk.r, sp.r], writes=[tt_.r])
                self.act(pt.t[:, :, c0:512], tt_.t[:, :, c0:512], AF.Exp, reads=[tt_.r], writes=[pt.r])
                if j >= 0:
                    self.memset("pool", pt.t[64:128, :, c0:c0 + 64], 0.0, writes=[pt.r])

            def pairD(st):
                h, qt, kb = st["h"], st["qt"], st["kb"]
                vt_ = vts[h % 2]
                c0 = st["c0"]
                pt = st["pt"]
                for m in range(2):
                    self.mm(Ac[m].t[:, c0:512], vt_.t[:, kb, :], pt.t[:, m, c0:512], st["first"], st["last"], reads=[vt_.r, pt.r], writes=[Ac[m].r])
                if st["last"]:
                    finalize_diff_evac()
                    pend.append((h, qt))

            if kind == "diff":
                psteps = []
                for h in range(nh):
                    hfirst = len(psteps)
                    for qt in range(NT):
                        kbs = list(range(0, 4 * qt + 4))
                        for idx, kb in enumerate(kbs):
                            psteps.append(dict(h=h, qt=qt, kb=kb, first=(idx == 0), last=(idx == len(kbs) - 1),
                                               hstep=len(psteps) - hfirst, i=len(psteps)))
                N = len(psteps)
                pend = []
                for i in range(N + 2):
                    if i < N:
                        pairA(psteps[i])
                    while pend:
                        finalize_diff(*pend.pop(0))
                    if 0 <= i - 1 < N:
                        pairB(psteps[i - 1])
                    if i - 2 >= 0:
                        pairD(psteps[i - 2])
                while pend:
                    finalize_diff(*pend.pop(0))
            else:
                N = len(steps)
                gpend = []
                for i in range(N + LAG):
                    if i < N:
                        stageA(steps[i])
                    while gpend:
                        stageF(gpend.pop(0))
                    if i - LAG >= 0:
                        stageD(steps[i - LAG])
                while gpend:
                    stageF(gpend.pop(0))

    def p3a(self, l, xsrc, xsrc_name):
        S, NT = self.S, self.NT
        lam_init = 0.8 - 0.6 * math.exp(-0.3 * l)
        with self.phase("p3a"):
            epsb = self.sb("epsb", [128, 2], F32)
            self.memset("pool", epsb.t[:, 0:1], EPS, writes=[epsb.r])
            self.memset("pool", epsb.t[:, 1:2], -0.5, writes=[epsb.r])
            idf = self.sb("idf", [128, 128], F32)
            self.load(idf, idf.t[:], self.identf_in)
            idb = self.sb("idb", [128, 128], BF16)
            self.cp("dve", idb.t[:], idf.t[:], reads=[idf.r], writes=[idb.r])
            go = self.sb("go", [128, 8], F32)
            self.load(go, go.t[:], self.g_o[l])
            stg = [self.sb("stg%d" % i, [128, 2048], F32) for i in range(2)]
            wo = self.sb("wo", [128, 8, D], BF16)
            self.load_w(wo, wo.t, self.w_out[l], 8, D, stg)
            ot = [self.sb("ot%d" % i, [128, 4, D], F32) for i in range(2)]
            xt = [self.sb("xt%d" % i, [128, 4, D], F32) for i in range(2)]
            on = self.sb("on", [128, 4, D], BF16)
            onT = self.sb("onT", [128, 8, 512], BF16)
            xo = [self.sb("xo%d" % i, [128, 4, D], F32) for i in range(2)]
            ssg = self.sb("ssg", [128, 4, 8], F32)
            junk = self.sb("junk", [128, 384], BF16)
            pT = [self.ps("pT%d" % i, [128, 512], BF16) for i in range(2)]
            pA = [self.ps("pA%d" % i, [128, 512], F32) for i in range(4)]
            groups = [(0, 384)] + [(384 + 64 * g, 64) for g in range(6)] + [(768, 256)]
            for t in range(NT):
                tok = slice(t * 512, (t + 1) * 512)
                o = ot[t % 2]
                x = xt[t % 2]
                self.load(o, o.t[:], self.O_tok[tok, :].rearrange("(s p) c -> p s c", p=128), src_name="O_tok")
                self.load(x, x.t[:], xsrc[tok, :].rearrange("(s p) c -> p s c", p=128), src_name=xsrc_name)
                for s in range(4):
                    for gi, (c0, n) in enumerate(groups):
                        self.act(junk.t[:, 0:n], o.t[:, s, c0:c0 + n], AF.Square, reads=[o.r], writes=[junk.r, ssg.r], accum=ssg.t[:, s, gi:gi + 1])
                self.act(ssg.t[:, :, 0:1], ssg.t[:, :, 0:1], AF.Ln, reads=[ssg.r, epsb.r], writes=[ssg.r], bias=epsb.t[:, 0:1], scale=1.0 / 384)
                self.act(ssg.t[:, :, 1:7], ssg.t[:, :, 1:7], AF.Ln, reads=[ssg.r, epsb.r], writes=[ssg.r], bias=epsb.t[:, 0:1], scale=1.0 / 64)
                self.act(ssg.t[:, :, 7:8], ssg.t[:, :, 7:8], AF.Ln, reads=[ssg.r, epsb.r], writes=[ssg.r], bias=epsb.t[:, 0:1], scale=1.0 / 256)
                self.act(ssg.t[:], ssg.t[:], AF.Exp, reads=[ssg.r], writes=[ssg.r], scale=-0.5)
                for s in range(4):
                    for gi, (c0, n) in enumerate(groups):
                        eng = "dve"
                        self.ts(eng, on.t[:, s, c0:c0 + n], o.t[:, s, c0:c0 + n], ssg.t[:, s, gi:gi + 1], None, ALU.mult, None, reads=[o.r, ssg.r], writes=[on.r])
                for c in range(8):
                    p = pT[c % 2]
                    for s in range(4):
                        self.tr(p.t[:, s * 128:(s + 1) * 128], on.t[:, s, c * 128:(c + 1) * 128], idb.t[:], reads=[on.r, idb.r], writes=[p.r])
                    if c in (3, 4, 5):
                        self.ts("dve", onT.t[:, c, :], p.t[:, 0:512], go.t[:, c:c + 1], 1.0 - lam_init, ALU.mult, ALU.mult, reads=[p.r, go.r], writes=[onT.r])
                    else:
                        self.act(onT.t[:, c, :], p.t[:, 0:512], AF.Copy, reads=[p.r, go.r], writes=[onT.r], scale=go.t[:, c:c + 1])
                xo_ = xo[t % 2]
                k = 0
                for s in range(4):
                    for hf in range(2):
                        pa = pA[k % 4]
                        k += 1
                        for c in range(8):
                            self.mm(pa.t[:], onT.t[:, c, s * 128:(s + 1) * 128], wo.t[:, c, hf * 512:(hf + 1) * 512], c == 0, c == 7, reads=[onT.r, wo.r], writes=[pa.r])
                        self.tt("dve", xo_.t[:, s, hf * 512:(hf + 1) * 512], x.t[:, s, hf * 512:(hf + 1) * 512], pa.t[:], ALU.add, reads=[x.r, pa.r], writes=[xo_.r])
                self.store("x1", self.x1[tok, :].rearrange("(s p) c -> p s c", p=128), xo_, xo_.t[:])

    def p3b(self, l, final):
        S = self.S
        TT = 256
        NTT = S // TT
        with self.phase("p3b"):
            epsb = self.sb("epsb", [128, 2], F32)
            self.memset("pool", epsb.t[:, 0:1], EPS, writes=[epsb.r])
            self.memset("pool", epsb.t[:, 1:2], -0.5, writes=[epsb.r])
            idf = self.sb("idf", [128, 128], F32)
            self.load(idf, idf.t[:], self.identf_in)
            idb = self.sb("idb", [128, 128], BF16)
            self.cp("dve", idb.t[:], idf.t[:], reads=[idf.r], writes=[idb.r])
            gT = self.sb("gT", [128, 8], F32)
            self.load(gT, gT.t[:], self.g_ffn[l])
            cw = self.sb("cw", [128, 44, 3], F32)
            self.load(cw, cw.t[:], self.cw_in[l].rearrange("p (j k) -> p j k", k=3))
            cb = self.sb("cb", [128, 44], F32)
            self.load(cb, cb.t[:], self.cb_in[l])
            stg = [self.sb("stg%d" % i, [128, 512], F32) for i in range(2)]
            wf1 = self.sb("wf1", [128, 8, 2 * DFF], BF16)
            wf2 = self.sb("wf2", [128, 22, D], BF16)

            def lw(wt, src2, nchunk, ncol):
                k = 0
                for c in range(nchunk):
                    for c0 in range(0, ncol, 512):
                        w = min(512, ncol - c0)
                        s = stg[k % 2]
                        k += 1
                        self.load(s, s.t[:, 0:w], src2[c * 128:(c + 1) * 128, c0:c0 + w])
                        self.cp("pool", wt.t[:, c, c0:c0 + w], s.t[:, 0:w], reads=[s.r], writes=[wt.r])
            lw(wf1, self.w_f1[l], 8, 2 * DFF)
            lw(wf2, self.w_f2[l], 22, D)
            if final:
                gfb = self.sb("gfb", [128, D], F32)
                self.load(gfb, gfb.t[:], self.g_fin.partition_broadcast(128))
            xt = [self.sb("xt%d" % i, [128, 2, D], F32) for i in range(1)]
            xns = [self.sb("xn%d" % i, [128, 2, D], BF16) for i in range(2)]
            xnTs = [self.sb("xnT%d" % i, [128, 8, TT], BF16) for i in range(2)]
            actTs = [self.sb("actT%d" % i, [128, 22, TT], BF16) for i in range(2)]
            ss = self.sb("ss", [128, 4], F32)
            junk = self.sb("junk", [128, D], BF16)
            hal = self.sb("hal", [128, 44, 2], F32)
            hal2 = self.sb("hal2", [128, 44, 2], F32)
            c1 = [self.sb("c1_%d" % i, [128, TT], F32) for i in range(4)]
            c2 = c1
            c3 = c1
            sg = [self.sb("sg%d" % i, [128, TT], F32) for i in range(2)]
            stt_ = self.sb("cst", [128, 44, 2], F32)
            self.memset("pool", stt_.t[:], 0.0, writes=[stt_.r])
            xo = self.sb("xo", [128, 2, D], F32)
            pT = [self.ps("pT%d" % i, [128, 512], BF16) for i in range(2)]
            pH = [self.ps("pH%d" % i, [128, 512], F32) for i in range(4)]
            pY = [self.ps("pY%d" % i, [128, 512], F32) for i in range(2)]
            kk = 0

            dmy = pT[1].t[:, 0:1024].bitcast(F32)

            def o_chunk(tp, i):
                tokp = slice(tp * TT, (tp + 1) * TT)
                aT = actTs[tp % 2]
                if i == 0:
                    self.load(xo, xo.t[:], self.x1[tokp, :].rearrange("(s p) c -> p s c", p=128))
                g0 = 0 if i < 11 else 2
                ii = (i % 11) * 2
                for g in (g0, g0 + 1):
                    s_, hf = divmod(g, 2)
                    py = pY[g % 2]
                    for k_ in (ii, ii + 1):
                        self.mm(py.t[:], aT.t[:, k_, s_ * 128:(s_ + 1) * 128], wf2.t[:, k_, hf * 512:(hf + 1) * 512], k_ == 0, k_ == 21,
                                reads=[aT.r, wf2.r], writes=[py.r])
                if i % 11 == 10:
                    for g in (g0, g0 + 1):
                        s_, hf = divmod(g, 2)
                        py = pY[g % 2]
                        self.tt("dve", xo.t[:, s_, hf * 512:(hf + 1) * 512], xo.t[:, s_, hf * 512:(hf + 1) * 512], py.t[:], ALU.add,
                                reads=[xo.r, py.r], writes=[xo.r])
                if i == 21:
                    if not final:
                        self.store("x2", self.x2[tokp, :].rearrange("(s p) c -> p s c", p=128), xo, xo.t[:])
                    else:
                        for s_ in range(2):
                            self.act(junk.t[:], xo.t[:, s_, :], AF.Square, reads=[xo.r], writes=[junk.r, ss.r], accum=ss.t[:, 2 + s_:3 + s_])
                        self.rstd(ss.t[:, 2:4], ss.t[:, 2:4], float(D), epsb, reads=[ss.r], writes=[ss.r])
                        for s_ in range(2):
                            self.stt("dve", xo.t[:, s_, :], xo.t[:, s_, :], ss.t[:, 2 + s_:3 + s_], gfb.t[:], ALU.mult, ALU.mult,
                                     reads=[xo.r, ss.r, gfb.r], writes=[xo.r])
                        self.store("out", self.out[tokp, :].rearrange("(s p) c -> p s c", p=128), xo, xo.t[:])

            def emit_O(tp, i):
                if i is None:
                    for i_ in range(22):
                        o_chunk(tp, i_)
                else:
                    o_chunk(tp, i)

            def prep(t_):
                tok_ = slice(t_ * TT, (t_ + 1) * TT)
                self.load(xt[0], xt[0].t[:], self.x1[tok_, :].rearrange("(s p) c -> p s c", p=128))
                self.norm_T(xt[0], 2, gT, xns[t_ % 2], xnTs[t_ % 2], pT, idb, epsb, ss, junk)

            prep(0)
            for t in range(NTT):
                tok = slice(t * TT, (t + 1) * TT)
                x = xt[0]
                actT = actTs[t % 2]
                xnT = xnTs[t % 2]
                self.tt("dve", hal.t[:, :, 0], cw.t[:, :, 1], stt_.t[:, :, 1], ALU.mult, reads=[cw.r, stt_.r], writes=[hal.r])
                self.tt("dve", hal2.t[:, :, 0], cw.t[:, :, 0], stt_.t[:, :, 0], ALU.mult, reads=[cw.r, stt_.r], writes=[hal2.r])
                self.tt("dve", hal.t[:, :, 1], cw.t[:, :, 0], stt_.t[:, :, 1], ALU.mult, reads=[cw.r, stt_.r, hal.r], writes=[hal.r])
                self.tt("dve", hal.t[:, :, 0], hal.t[:, :, 0], hal2.t[:, :, 0], ALU.add, reads=[hal.r, hal2.r], writes=[hal.r])
                for i in range(22):
                    un = []
                    for which in range(2):
                        idx = i + 22 * which
                        col0 = idx * 128
                        ph = pH[kk % 4]
                        a = c1[kk % 4]
                        kk += 1
                        for c in range(8):
                            self.mm(ph.t[:, 0:TT], wf1.t[:, c, col0:col0 + 128], xnT.t[:, c, :], c == 0, c == 7, reads=[wf1.r, xnT.r], writes=[ph.r])
                        un.append((idx, ph, a))
                    for (idx, ph, a) in un:
                        self.act(a.t[:], ph.t[:, 0:TT], AF.Identity, reads=[ph.r, cw.r, cb.r], writes=[a.r], bias=cb.t[:, idx:idx + 1], scale=cw.t[:, idx, 2:3])
                    for (idx, ph, a) in un:
                        self.stt("dve", a.t[:, 1:TT], ph.t[:, 0:TT - 1], cw.t[:, idx, 1:2], a.t[:, 1:TT], ALU.mult, ALU.add, reads=[ph.r, cw.r, a.r], writes=[a.r])
                    for (idx, ph, a) in un:
                        self.stt("dve", a.t[:, 2:TT], ph.t[:, 0:TT - 2], cw.t[:, idx, 0:1], a.t[:, 2:TT], ALU.mult, ALU.add, reads=[ph.r, cw.r, a.r], writes=[a.r])
                    for (idx, ph, a) in un:
                        self.tt("pool", a.t[:, 0:2], a.t[:, 0:2], hal.t[:, idx, :], ALU.add, reads=[a.r, hal.r], writes=[a.r])
                    for (idx, ph, a) in un:
                        self.cp("act", stt_.t[:, idx, :], ph.t[:, TT - 2:TT], reads=[ph.r], writes=[stt_.r])
                    s_ = sg[i % 2]
                    self.act(s_.t[:], un[1][2].t[:], AF.Silu, reads=[un[1][2].r], writes=[s_.r])
                    self.tt("pool", actT.t[:, i, :], s_.t[:], un[0][2].t[:], ALU.mult, reads=[s_.r, un[0][2].r], writes=[actT.r])
                    if t > 0:
                        emit_O(t - 1, i)
                    if i == 12 and t + 1 < NTT:
                        prep(t + 1)
                    for _ in range(DUMMY_FFN):
                        self.mm(dmy, wf2.t[:, 0, 0:128], wf2.t[:, 1, 0:512], True, True, reads=[wf2.r], writes=[pT[1].r])
            emit_O(NTT - 1, None)

    def build(self):
        import os
        ph = os.environ.get("K_PHASES", "p0,p1,mla,diff,ch,p3a,p3b").split(",")
        nl = int(os.environ.get("K_LAYERS", str(DEPTH)))
        self.declare()
        if "p0" in ph:
            self.p0()
        xsrc, xname = self.x_in, None
        for l in range(nl):
            if "p1" in ph:
                self.p1(l, xsrc, xname)
            for kd in ("mla", "diff", "ch"):
                if kd in ph:
                    self.attn(l, kd)
            if "p3a" in ph:
                self.p3a(l, xsrc, xname)
            if "p3b" in ph:
                self.p3b(l, final=(l == DEPTH - 1))
            xsrc, xname = self.x2, None
        self.gst.close()
        return self.nc


def _host_consts():
    identf = np.eye(128, dtype=np.float32)
    p = np.arange(128)
    inv = (10000.0 ** (-(np.arange(16, dtype=np.float32)) / 16)).astype(np.float32)
    invf = np.zeros((128, 2), np.float32)
    invf[:, 0] = inv[p % 16]
    sign = np.where((p % 32) < 16, -1.0, 1.0).astype(np.float32)
    invf[:, 1] = inv[p % 16] * sign
    cmask = np.zeros((8, 128, 512), np.float32)
    ki = np.arange(128)[:, None] // 64
    qi = np.arange(512)[None, :] // 64
    for i in range(8):
        delta = 512 - 128 * i
        d = delta // 64 + qi - ki
        cmask[i] = np.where((d >= 0) & (d <= 8), 0.0, NEG)
    return identf, invf, cmask


def _prep_weights(inp):
    f = lambda a: np.ascontiguousarray(np.asarray(a, dtype=np.float32))
    L = DEPTH
    w_in = f(inp["w_in"])
    cq, ckv, kr = w_in[:, :, 0:256], w_in[:, :, 256:384], w_in[:, :, 384:416]
    dq, dk, dv = w_in[:, :, 416:800], w_in[:, :, 800:1184], w_in[:, :, 1184:1568]
    chq, chk, chv = w_in[:, :, 1568:1824], w_in[:, :, 1824:2080], w_in[:, :, 2080:2336]
    swap = np.concatenate([np.arange(16, 32), np.arange(0, 16)])
    w_in_d = np.concatenate([cq, ckv, dv, chv, dq, dk, chq, chk, kr, kr[:, :, swap]], axis=2)
    assert w_in_d.shape[2] == NCOL_IN
    wuq = f(inp["mla_w_uq"]).reshape(L, 256, 6, 96)
    nope = wuq[..., 0:64].reshape(L, 256, 384)
    rope = wuq[..., 64:96]
    w_uq_d = np.concatenate([nope, rope.reshape(L, 256, 192), rope[..., swap].reshape(L, 256, 192)], axis=2)
    wukv = f(inp["mla_w_ukv"]).reshape(L, 128, 6, 128)
    w_ukv_d = np.concatenate([wukv[..., 0:64].reshape(L, 128, 384), wukv[..., 64:128].reshape(L, 128, 384)], axis=2)

    def colT(g, n):
        return np.ascontiguousarray(f(g).reshape(L, n, 128).transpose(0, 2, 1))
    g_o = np.concatenate([f(inp["mla_out_norm"]), f(inp["diff_norm"]), f(inp["chunk_out_norm"])], axis=1)
    cwt = f(inp["ffn_conv_w"])
    cw = np.ascontiguousarray(cwt.reshape(L, 3, 44, 128).transpose(0, 3, 2, 1)).reshape(L, 128, 132)
    cb = np.ascontiguousarray(f(inp["ffn_conv_b"]).reshape(L, 44, 128).transpose(0, 2, 1))
    return {
        "w_in": np.ascontiguousarray(w_in_d), "w_uq": np.ascontiguousarray(w_uq_d), "w_ukv": np.ascontiguousarray(w_ukv_d),
        "w_out": f(inp["w_out"]), "w_f1": f(inp["w_ffn_in"]), "w_f2": f(inp["w_ffn_out"]),
        "g_attn": colT(inp["attn_norm"], 8), "g_ffn": colT(inp["ffn_norm"], 8),
        "g_q": colT(inp["mla_q_norm"], 2), "g_kv": colT(inp["mla_kv_norm"], 1), "g_o": colT(g_o, 8),
        "lam": f(inp["diff_lambda"]).reshape(L, 1, 128), "rb": f(inp["chunk_rel_bias"]),
        "cw": cw, "cb": cb, "g_fin": f(inp["final_norm"]).reshape(1, D),
    }


_NC_CACHE = {}


def run_cores(inp, S, ncores, debug=False):
    key = (S, debug)
    if key not in _NC_CACHE:
        _NC_CACHE[key] = K(S, debug).build()
    nc = _NC_CACHE[key]
    identf, invf, cmask = _host_consts()
    wd = _prep_weights(inp)
    x = np.asarray(inp["x"], dtype=np.float32)
    pos = np.asarray(inp["positions"]).astype(np.int32)
    in_maps = []
    for b in range(ncores):
        m = dict(wd)
        m["x"] = np.ascontiguousarray(x[b, :S])
        m["pos"] = np.ascontiguousarray(pos[b, :S].reshape(S // 128, 128))
        m["identf"] = identf
        m["invf"] = invf
        m["cmask"] = cmask
        in_maps.append(m)
    res = run_bass_kernel_spmd(nc, in_maps, core_ids=list(range(ncores)))
    return res


def kernel(**inputs):
    B, S = inputs["x"].shape[0], inputs["x"].shape[1]
    res = run_cores(inputs, S, B)
    return np.stack([np.asarray(r["out"], dtype=np.float32) for r in res.results], axis=0)
```

```python
import contextlib

ENGS = ("pe", "act", "dve", "pool", "sp")
ENGOBJ = {"pe": "tensor", "act": "scalar", "dve": "vector", "pool": "gpsimd", "sp": "sync"}


class Res:
    __slots__ = ("name", "lw", "rd", "slot", "base", "dcount", "excl")

    def __init__(self, name="", excl=False):
        self.name = name
        self.excl = excl
        self.lw = None
        self.rd = []
        self.slot = None
        self.base = 0
        self.dcount = 0


class Op:
    __slots__ = ("eng", "fn", "deps", "ddeps", "pos", "sig", "count", "dres", "kind")

    def __init__(self, eng, fn, kind):
        self.eng = eng
        self.fn = fn
        self.kind = kind
        self.deps = []
        self.ddeps = []
        self.sig = False
        self.count = 0
        self.dres = None


class Prog:
    def __init__(self, nc, stack, nslots=88, same_engine_sync=True):
        self.nc = nc
        self.same_engine_sync = same_engine_sync
        self.esem = {e: stack.enter_context(nc.semaphore("s_" + e)) for e in ENGS}
        self.ecount = {e: 0 for e in ENGS}
        self.slots = [[stack.enter_context(nc.semaphore("d%d" % i)), 0] for i in range(nslots)]
        self.free = list(range(nslots))
        self.total_ops = 0
        self.begin()

    def begin(self):
        self.streams = {e: [] for e in ENGS}
        self.dma_res = []

    def _add(self, op, reads, writes):
        if any(r.excl for r in reads):
            writes = list(writes) + [r for r in reads if r.excl and r not in writes]
            reads = [r for r in reads if not r.excl]
        deps = []
        for r in reads:
            if r.lw is not None:
                deps.append(r.lw)
        for w in writes:
            if w.lw is not None:
                deps.append(w.lw)
            deps.extend(w.rd)
        seen = set()
        for d in deps:
            if id(d) in seen or d is op:
                continue
            seen.add(id(d))
            if d.kind == "d":
                op.ddeps.append((d.dres, d.dres.dcount * 16))
            else:
                if d.eng == op.eng:
                    if d.eng == "pe":
                        continue
                    if not self.same_engine_sync:
                        continue
                op.deps.append(d)
        for r in reads:
            r.rd.append(op)
        for w in writes:
            w.lw = op
            w.rd = []
        op.pos = len(self.streams[op.eng])
        self.streams[op.eng].append(op)
        return op

    def op(self, eng, fn, reads=(), writes=(), mm=False):
        o = Op(eng, fn, "mm" if mm else "c")
        return self._add(o, reads, writes)

    def dma(self, queue, out_ap, in_ap, dst, reads=(), writes=(), **kw):
        if dst.slot is None:
            dst.slot = self.free.pop()
            dst.base = self.slots[dst.slot][1]
            dst.dcount = 0
            self.dma_res.append(dst)
        o = Op(queue, lambda e: e.dma_start(out=out_ap, in_=in_ap, **kw), "d")
        o.dres = dst
        self._add(o, reads, writes)
        dst.dcount += 1
        return o

    def end(self):
        nc = self.nc
        waited = {e: {f: -1 for f in ENGS} for e in ENGS}
        dwaited = {e: {} for e in ENGS}
        plan = {e: [] for e in ENGS}
        for e in ENGS:
            for op in self.streams[e]:
                best = {}
                for d in op.deps:
                    if d.pos > waited[e][d.eng]:
                        if d.eng not in best or d.pos > best[d.eng].pos:
                            best[d.eng] = d
                for f, d in best.items():
                    waited[e][f] = d.pos
                    d.sig = True
                dws = []
                for (res, cnt) in op.ddeps:
                    if dwaited[e].get(id(res), 0) < cnt:
                        dwaited[e][id(res)] = cnt
                        dws.append((res, cnt))
                plan[e].append((op, list(best.values()), dws))
        for e in ENGS:
            c = self.ecount[e]
            for op in self.streams[e]:
                if op.sig:
                    c += 1
                    op.count = c
            self.ecount[e] = c
        esem = self.esem
        slots = self.slots
        dma_res = list(self.dma_res)
        with nc.Block() as block:
            def run(e):
                def body(eng):
                    for (op, cw, dw) in plan[e]:
                        for d in cw:
                            eng.wait_ge(esem[d.eng], d.count)
                        for (res, cnt) in dw:
                            eng.wait_ge(slots[res.slot][0], res.base + cnt)
                        ins = op.fn(eng)
                        if op.kind == "d":
                            ins.then_inc(slots[op.dres.slot][0], 16)
                        elif op.sig:
                            ins.then_inc(esem[e], 1)
                    if e == "sp":
                        for r in dma_res:
                            eng.wait_ge(slots[r.slot][0], r.base + r.dcount * 16)
                return body

            for e in ENGS:
                getattr(block, ENGOBJ[e])(run(e))
        for r in dma_res:
            slots[r.slot][1] = r.base + r.dcount * 16
            self.free.append(r.slot)
            r.slot = None
            r.lw = None
            r.rd = []
        n = sum(len(self.streams[e]) for e in ENGS)
        self.total_ops += n
        self.begin()
        return n
import math
import numpy as np
import ml_dtypes
import concourse.bass as bass
import concourse.mybir as mybir
from concourse.bass_utils import run_bass_kernel_spmd

F32 = mybir.dt.float32
BF16 = mybir.dt.bfloat16
I32 = mybir.dt.int32
AF = mybir.ActivationFunctionType
ALU = mybir.AluOpType
AX = mybir.AxisListType

D = 1024
DEPTH = 2
EPS = 1e-6
NCOL_IN = 2368
DFF = 2816
TWO_PI = 2.0 * math.pi
NEG = -30000.0
SLOPES = [2.0 ** (-8.0 * (i + 1) / 6) for i in range(6)]
DUMMY_MLA = 256
DUMMY_FFN = 0


class Tl:
    __slots__ = ("t", "r")

    def __init__(self, t, r):
        self.t = t
        self.r = r


class K:
    def __init__(self, S, debug=False):
        self.S = S
        self.NT = S // 512
        self.NB = S // 128
        self.debug = debug
        self.nc = bass.Bass("TRN2", target_bir_lowering=False)
        self.gst = contextlib.ExitStack()
        self.P = Prog(self.nc, self.gst)
        self.st = None
        self.dres = {}
        self.uid = 0

    def din(self, name, shape, dt):
        return self.nc.dram_tensor(name, list(shape), dt, kind="ExternalInput").ap()

    def dscr(self, name, shape, dt):
        kind = "ExternalOutput" if self.debug else "Internal"
        return self.nc.dram_tensor(name, list(shape), dt, kind=kind).ap()

    def sb(self, name, shape, dt):
        self.uid += 1
        name = "sb%d_%s" % (self.uid, name)
        t = self.st.enter_context(self.nc.sbuf_tensor(name, list(shape), dt))
        return Tl(t, Res(name))

    def ps(self, name, shape, dt):
        self.uid += 1
        name = "ps%d_%s" % (self.uid, name)
        if dt == BF16 and list(shape) == [128, 512]:
            shape = [128, 1024]
        t = self.st.enter_context(self.nc.psum_tensor(name, list(shape), dt))
        return Tl(t, Res(name, excl=True))

    def dr(self, name):
        if name not in self.dres:
            self.dres[name] = Res(name)
        return self.dres[name]

    @contextlib.contextmanager
    def phase(self, name):
        self.st = contextlib.ExitStack()
        self.dres = {}
        with self.st:
            yield
            n = self.P.end()
        self.st = None

    def load(self, tl, dst_ap, src_ap, src_name=None, q="sp", **kw):
        reads = [self.dr(src_name)] if src_name else []
        self.P.dma(q, dst_ap, src_ap, tl.r, reads=reads, writes=[tl.r], **kw)

    def store(self, dname, dst_ap, tl, src_ap, q="pool", **kw):
        r = self.dr(dname)
        self.P.dma(q, dst_ap, src_ap, r, reads=[tl.r], writes=[r], **kw)

    def mm(self, out, lhsT, rhs, start, stop, reads, writes):
        self.P.op("pe", lambda e: e.matmul(out, lhsT=lhsT, rhs=rhs, start=start, stop=stop),
                  reads=reads, writes=writes, mm=True)

    def tr(self, out, in_, ident, reads, writes):
        self.P.op("pe", lambda e: e.transpose(out, in_, ident), reads=reads, writes=writes, mm=True)

    def act(self, out, in_, func, reads, writes, bias=None, scale=None, accum=None):
        kw = {}
        if bias is not None:
            kw["bias"] = bias
        if scale is not None:
            kw["scale"] = scale
        if accum is not None:
            kw["accum_out"] = accum
        self.P.op("act", lambda e: e.activation(out=out, in_=in_, func=func, **kw), reads=reads, writes=writes)

    def ts(self, eng, out, in0, s1, s2, op0, op1, reads, writes):
        if op1 is None:
            self.P.op(eng, lambda e: e.tensor_scalar(out=out, in0=in0, scalar1=s1, scalar2=None, op0=op0),
                      reads=reads, writes=writes)
        else:
            self.P.op(eng, lambda e: e.tensor_scalar(out=out, in0=in0, scalar1=s1, scalar2=s2, op0=op0, op1=op1),
                      reads=reads, writes=writes)

    def tt(self, eng, out, in0, in1, op, reads, writes):
        self.P.op(eng, lambda e: e.tensor_tensor(out=out, in0=in0, in1=in1, op=op), reads=reads, writes=writes)

    def stt(self, eng, out, in0, scalar, in1, op0, op1, reads, writes):
        self.P.op(eng, lambda e: e.scalar_tensor_tensor(out=out, in0=in0, scalar=scalar, in1=in1, op0=op0, op1=op1),
                  reads=reads, writes=writes)

    def cp(self, eng, out, in_, reads, writes):
        if eng == "act":
            self.P.op(eng, lambda e: e.copy(out=out, in_=in_), reads=reads, writes=writes)
        else:
            self.P.op(eng, lambda e: e.tensor_copy(out=out, in_=in_), reads=reads, writes=writes)

    def memset(self, eng, ap, val, writes):
        self.P.op(eng, lambda e: e.memset(ap, val), reads=(), writes=writes)

    def recip(self, out, in_, reads, writes):
        self.P.op("dve", lambda e: e.reciprocal(out=out, in_=in_), reads=reads, writes=writes)

    def rstd(self, out, ss, n, epsb, reads, writes):
        self.act(out, ss, AF.Ln, reads=list(reads) + [epsb.r], writes=writes, bias=epsb.t[:, 0:1], scale=1.0 / n)
        self.act(out, out, AF.Exp, reads=writes, writes=writes, scale=-0.5)

    def load_w(self, wt, dst3, src2, nchunk, ncol, stg, src_name=None):
        CW = 2048
        k = 0
        for c in range(nchunk):
            for c0 in range(0, ncol, CW):
                w = min(CW, ncol - c0)
                s = stg[k % len(stg)]
                k += 1
                self.load(s, s.t[:, 0:w], src2[c * 128:(c + 1) * 128, c0:c0 + w])
                self.cp("pool", dst3[:, c, c0:c0 + w], s.t[:, 0:w], reads=[s.r], writes=[wt.r])

    def declare(self):
        S = self.S
        self.x_in = self.din("x", [S, D], F32)
        self.pos_in = self.din("pos", [S // 128, 128], I32)
        self.identf_in = self.din("identf", [128, 128], F32)
        self.invf_in = self.din("invf", [128, 2], F32)
        self.cmask_in = self.din("cmask", [8, 128, 512], F32)
        self.w_in = self.din("w_in", [DEPTH, D, NCOL_IN], F32)
        self.w_uq = self.din("w_uq", [DEPTH, 256, 768], F32)
        self.w_ukv = self.din("w_ukv", [DEPTH, 128, 768], F32)
        self.w_out = self.din("w_out", [DEPTH, D, D], F32)
        self.w_f1 = self.din("w_f1", [DEPTH, D, 2 * DFF], F32)
        self.w_f2 = self.din("w_f2", [DEPTH, DFF, D], F32)
        self.g_attn = self.din("g_attn", [DEPTH, 128, 8], F32)
        self.g_ffn = self.din("g_ffn", [DEPTH, 128, 8], F32)
        self.g_q = self.din("g_q", [DEPTH, 128, 2], F32)
        self.g_kv = self.din("g_kv", [DEPTH, 128, 1], F32)
        self.g_o = self.din("g_o", [DEPTH, 128, 8], F32)
        self.lam_in = self.din("lam", [DEPTH, 1, 128], F32)
        self.rb_in = self.din("rb", [DEPTH, 4, 320], F32)
        self.cw_in = self.din("cw", [DEPTH, 128, 44 * 3], F32)
        self.cb_in = self.din("cb", [DEPTH, 128, 44], F32)
        self.g_fin = self.din("g_fin", [1, D], F32)
        self.out = self.nc.dram_tensor("out", [S, D], F32, kind="ExternalOutput").ap()
        self.posf = self.dscr("posf", [1, S], F32)
        self.negposk = self.dscr("negposk", [128, S // 128], F32)
        self.cosT = self.dscr("cosT", [self.NT, 128, 512], F32)
        self.sinT = self.dscr("sinT", [self.NT, 128, 512], F32)
        self.QTn_mla = self.dscr("QTn_mla", [6, 64, S], BF16)
        self.QTr_mla = self.dscr("QTr_mla", [6, 32, S], BF16)
        self.KTn_mla = self.dscr("KTn_mla", [6, 64, S], BF16)
        self.KTr_mla = self.dscr("KTr_mla", [32, S], BF16)
        self.V_all = self.dscr("V_all", [16, S, 128], BF16)
        self.QT_diff = self.dscr("QT_diff", [6, 64, S], BF16)
        self.KT_diff = self.dscr("KT_diff", [6, 64, S], BF16)
        self.QT_ch = self.dscr("QT_ch", [4, 64, S], BF16)
        self.KT_ch = self.dscr("KT_ch", [4, 64, S], BF16)
        self.O_tok = self.dscr("O_tok", [S, D], F32)
        self.x1 = self.dscr("x1", [S, D], F32)
        self.x2 = self.dscr("x2", [S, D], F32)
        self.ext = self.dscr("ext", [4, 128, 1536], F32)

    def p0(self):
        S, NB, NT = self.S, self.NB, self.NT
        with self.phase("p0"):
            idf = self.sb("idf", [128, 128], F32)
            self.load(idf, idf.t[:], self.identf_in)
            invf = self.sb("invf", [128, 2], F32)
            self.load(invf, invf.t[:], self.invf_in)
            pi = self.sb("pi", [NB, 128], I32)
            self.load(pi, pi.t[:], self.pos_in)
            pf = self.sb("pf", [NB, 128], F32)
            self.cp("dve", pf.t[:], pi.t[:], reads=[pi.r], writes=[pf.r])
            self.store("posf", self.posf.rearrange("o (j p) -> (o j) p", p=128), pf, pf.t[:])
            pst = self.ps("pst", [128, 512], F32)
            self.tr(pst.t[:, 0:NB], pf.t[0:NB, :], idf.t[0:NB, 0:NB], reads=[pf.r, idf.r], writes=[pst.r])
            npk = self.sb("npk", [128, NB], F32)
            self.ts("dve", npk.t[:], pst.t[:, 0:NB], -1.0, None, ALU.mult, None, reads=[pst.r], writes=[npk.r])
            self.store("negposk", self.negposk, npk, npk.t[:])
            pbc = [self.sb("pbc%d" % i, [128, 512], F32) for i in range(2)]
            ang = self.sb("ang", [128, 512], F32)
            ni = self.sb("ni", [128, 512], I32)
            nf = self.sb("nf", [128, 512], F32)
            y = self.sb("y", [128, 512], F32)
            res = [self.sb("res%d" % i, [128, 512], F32) for i in range(2)]
            k = 0
            for t in range(NT):
                pb = pbc[t % 2]
                self.load(pb, pb.t[:], self.posf[:, t * 512:(t + 1) * 512].partition_broadcast(128), src_name="posf")
                for which in range(2):
                    col = invf.t[:, which:which + 1]
                    if which == 0:
                        self.ts("dve", ang.t[:], pb.t[:], col, math.pi / 2, ALU.mult, ALU.add, reads=[pb.r, invf.r], writes=[ang.r])
                    else:
                        self.ts("dve", ang.t[:], pb.t[:], col, None, ALU.mult, None, reads=[pb.r, invf.r], writes=[ang.r])
                    self.ts("dve", ni.t[:], ang.t[:], 1.0 / TWO_PI, None, ALU.mult, None, reads=[ang.r], writes=[ni.r])
                    self.cp("dve", nf.t[:], ni.t[:], reads=[ni.r], writes=[nf.r])
                    self.stt("dve", y.t[:], nf.t[:], -TWO_PI, ang.t[:], ALU.mult, ALU.add, reads=[nf.r, ang.r], writes=[y.r])
                    self.ts("dve", nf.t[:], y.t[:], math.pi, -TWO_PI, ALU.is_gt, ALU.mult, reads=[y.r], writes=[nf.r])
                    self.tt("dve", y.t[:], y.t[:], nf.t[:], ALU.add, reads=[y.r, nf.r], writes=[y.r])
                    self.ts("dve", nf.t[:], y.t[:], -math.pi, TWO_PI, ALU.is_lt, ALU.mult, reads=[y.r], writes=[nf.r])
                    self.tt("dve", y.t[:], y.t[:], nf.t[:], ALU.add, reads=[y.r, nf.r], writes=[y.r])
                    self.ts("dve", y.t[:], y.t[:], math.pi, -math.pi, ALU.min, ALU.max, reads=[y.r], writes=[y.r])
                    r_ = res[k % 2]
                    k += 1
                    self.act(r_.t[:], y.t[:], AF.Sin, reads=[y.r], writes=[r_.r])
                    dst = self.cosT if which == 0 else self.sinT
                    self.store("cosT" if which == 0 else "sinT", dst[t], r_, r_.t[:])

    def norm_T(self, xt, ns, gT, xn, xnT, pT, idb, epsb, ss, junk, evac_engs=("act", "dve")):
        for s in range(ns):
            self.act(junk.t[:], xt.t[:, s, :], AF.Square, reads=[xt.r], writes=[junk.r, ss.r], accum=ss.t[:, s:s + 1])
        self.rstd(ss.t[:, 0:ns], ss.t[:, 0:ns], float(D), epsb, reads=[ss.r], writes=[ss.r])
        for s in range(ns):
            self.ts("dve", xn.t[:, s, :], xt.t[:, s, :], ss.t[:, s:s + 1], None, ALU.mult, None, reads=[xt.r, ss.r], writes=[xn.r])
        for c in range(8):
            p = pT[c % len(pT)]
            for s in range(ns):
                self.tr(p.t[:, s * 128:(s + 1) * 128], xn.t[:, s, c * 128:(c + 1) * 128], idb.t[:], reads=[xn.r, idb.r], writes=[p.r])
            eng = evac_engs[c % len(evac_engs)]
            if eng == "act":
                self.act(xnT.t[:, c, :], p.t[:, 0:ns * 128], AF.Copy, reads=[p.r, gT.r], writes=[xnT.r], scale=gT.t[:, c:c + 1])
            else:
                self.ts("dve", xnT.t[:, c, :], p.t[:, 0:ns * 128], gT.t[:, c:c + 1], None, ALU.mult, None, reads=[p.r, gT.r], writes=[xnT.r])

    def p1(self, l, xsrc, xsrc_name):
        S, NT = self.S, self.NT
        with self.phase("p1"):
            epsb = self.sb("epsb", [128, 2], F32)
            self.memset("pool", epsb.t[:, 0:1], EPS, writes=[epsb.r])
            self.memset("pool", epsb.t[:, 1:2], -0.5, writes=[epsb.r])
            idf = self.sb("idf", [128, 128], F32)
            self.load(idf, idf.t[:], self.identf_in)
            idb = self.sb("idb", [128, 128], BF16)
            self.cp("dve", idb.t[:], idf.t[:], reads=[idf.r], writes=[idb.r])
            gT = self.sb("gT", [128, 8], F32)
            self.load(gT, gT.t[:], self.g_attn[l])
            gq = self.sb("gq", [128, 2], F32)
            self.load(gq, gq.t[:], self.g_q[l])
            gkv = self.sb("gkv", [128, 1], F32)
            self.load(gkv, gkv.t[:], self.g_kv[l])
            stg = [self.sb("stg%d" % i, [128, 2048], F32) for i in range(2)]
            win = self.sb("win", [128, 8, NCOL_IN], BF16)
            self.load_w(win, win.t, self.w_in[l], 8, NCOL_IN, stg)
            wuq = self.sb("wuq", [128, 2, 768], BF16)
            self.load_w(wuq, wuq.t, self.w_uq[l], 2, 768, stg)
            wukv = self.sb("wukv", [128, 1, 768], BF16)
            self.load_w(wukv, wukv.t, self.w_ukv[l], 1, 768, stg)
            xt = [self.sb("xt%d" % i, [128, 4, D], F32) for i in range(2)]
            xns = [self.sb("xn%d" % i, [128, 4, D], BF16) for i in range(2)]
            xnTs = [self.sb("xnT%d" % i, [128, 8, 512], BF16) for i in range(2)]
            sss = [self.sb("ss%d" % i, [128, 4], F32) for i in range(2)]
            junks = [self.sb("junk%d" % i, [128, D], BF16) for i in range(2)]
            cs = [self.sb("cs%d" % i, [128, 512], F32) for i in range(2)]
            sn = [self.sb("sn%d" % i, [128, 512], F32) for i in range(2)]
            pT = [self.ps("pT%d" % i, [128, 512], BF16) for i in range(2)]
            pA = [self.ps("pA%d" % i, [128, 512], F32) for i in range(4)]
            pB = [self.ps("pB%d" % i, [128, 512], F32) for i in range(2)]
            vst = [self.sb("vst%d" % i, [128, 16, 128], BF16) for i in range(4)]
            for v in vst:
                self.memset("pool", v.t[:, :, 64:128], 1.0, writes=[v.r])
            cqns = [self.sb("cqn%d" % i, [128, 384], BF16) for i in range(2)]
            cqTs = [self.sb("cqT%d" % i, [128, 2, 512], BF16) for i in range(2)]
            ckvTs = [self.sb("ckvT%d" % i, [128, 512], BF16) for i in range(2)]
            ss2s = [self.sb("ss2_%d" % i, [128, 2], F32) for i in range(2)]
            fst = [self.sb("fst%d" % i, [128, 512], BF16) for i in range(4)]
            rt1 = self.sb("rt1", [128, 512], F32)
            rt2 = self.sb("rt2", [128, 512], F32)
            fk = [0]
            sq = [0]

            def fm_out(ps_ap, rows, scale, dsts, preads):
                f = fst[fk[0] % 4]
                fk[0] += 1
                if fk[0] % 2 == 0:
                    self.act(f.t[0:rows, :], ps_ap, AF.Copy, reads=preads, writes=[f.r], scale=scale)
                else:
                    self.ts("dve", f.t[0:rows, :], ps_ap, scale, None, ALU.mult, None, reads=preads, writes=[f.r])
                for (dn, dap, r0, r1) in dsts:
                    sq[0] += 1
                    self.store(dn, dap, f, f.t[r0:r1, :], q=("sp" if sq[0] % 2 else "pool"))

            def tile_loads(t):
                x = xt[t % 2]
                tok = slice(t * 512, (t + 1) * 512)
                self.load(x, x.t[:], xsrc[tok, :].rearrange("(s p) c -> p s c", p=128), src_name=xsrc_name)
                self.load(cs[t % 2], cs[t % 2].t[:], self.cosT[t], src_name="cosT")
                self.load(sn[t % 2], sn[t % 2].t[:], self.sinT[t], src_name="sinT")

            tile_loads(0)
            for t in range(NT):
                x = xt[t % 2]
                tok = slice(t * 512, (t + 1) * 512)
                c_ = cs[t % 2]
                s_ = sn[t % 2]
                if t + 1 < NT:
                    tile_loads(t + 1)
                xn, xnT, ss, junk = xns[t % 2], xnTs[t % 2], sss[t % 2], junks[t % 2]
                cqT, ckvT = cqTs[t % 2], ckvTs[t % 2]
                if t == 0:
                    self.norm_T(x, 4, gT, xn, xnT, pT, idb, epsb, ss, junk)
                for s in range(4):
                    pa0 = pA[(2 * s) % 4]
                    pa1 = pA[(2 * s + 1) % 4]
                    for hf, pa in ((0, pa0), (1, pa1)):
                        for c in range(8):
                            self.mm(pa.t[:], xnT.t[:, c, s * 128:(s + 1) * 128], win.t[:, c, hf * 512:(hf + 1) * 512],
                                    c == 0, c == 7, reads=[xnT.r, win.r], writes=[pa.r])
                    cqn, ss2 = cqns[s % 2], ss2s[s % 2]
                    V = vst[s]
                    junk = junks[(s + 1) % 2]
                    self.act(junk.t[:, 0:256], pa0.t[:, 0:256], AF.Square, reads=[pa0.r], writes=[junk.r, ss2.r], accum=ss2.t[:, 0:1])
                    self.act(junk.t[:, 0:128], pa0.t[:, 256:384], AF.Square, reads=[pa0.r], writes=[junk.r, ss2.r], accum=ss2.t[:, 1:2])
                    self.rstd(ss2.t[:, 0:1], ss2.t[:, 0:1], 256.0, epsb, reads=[ss2.r], writes=[ss2.r])
                    self.rstd(ss2.t[:, 1:2], ss2.t[:, 1:2], 128.0, epsb, reads=[ss2.r], writes=[ss2.r])
                    self.ts("dve", cqn.t[:, 0:256], pa0.t[:, 0:256], ss2.t[:, 0:1], None, ALU.mult, None, reads=[pa0.r, ss2.r], writes=[cqn.r])
                    self.ts("dve", cqn.t[:, 256:384], pa0.t[:, 256:384], ss2.t[:, 1:2], None, ALU.mult, None, reads=[pa0.r, ss2.r], writes=[cqn.r])
                    self.cp("act", V.t[:, 6:8, 0:64], pa0.t[:, 384:512].rearrange("p (h c) -> p h c", c=64), reads=[pa0.r], writes=[V.r])
                    self.cp("act", V.t[:, 8:16, 0:64], pa1.t[:, 0:512].rearrange("p (h c) -> p h c", c=64), reads=[pa1.r], writes=[V.r])
                    p = pT[s % 2]
                    for j in range(3):
                        self.tr(p.t[:, j * 128:(j + 1) * 128], cqn.t[:, j * 128:(j + 1) * 128], idb.t[:], reads=[cqn.r, idb.r], writes=[p.r])
                    for j in range(2):
                        self.ts("dve", cqT.t[:, j, s * 128:(s + 1) * 128], p.t[:, j * 128:(j + 1) * 128], gq.t[:, j:j + 1], None, ALU.mult, None,
                                reads=[p.r, gq.r], writes=[cqT.r])
                    self.ts("dve", ckvT.t[:, s * 128:(s + 1) * 128], p.t[:, 256:384], gkv.t[:, 0:1], None, ALU.mult, None,
                            reads=[p.r, gkv.r], writes=[ckvT.r])
                    pb = pB[s % 2]
                    self.mm(pb.t[:, 0:384], ckvT.t[:, s * 128:(s + 1) * 128], wukv.t[:, 0, 384:768], True, True, reads=[ckvT.r, wukv.r], writes=[pb.r])
                    self.cp("act", V.t[:, 0:6, 0:64], pb.t[:, 0:384].rearrange("p (h c) -> p h c", c=64), reads=[pb.r], writes=[V.r])
                    t0 = t * 512 + s * 128
                    self.store("V_all", self.V_all[:, t0:t0 + 128, :].rearrange("h p c -> p h c"), V, V.t[:, :, :], q="pool")
                if t + 1 < NT:
                    t1_ = t + 1
                    self.norm_T(xt[t1_ % 2], 4, gT, xns[t1_ % 2], xnTs[t1_ % 2], pT, idb, epsb, sss[t1_ % 2], junks[t1_ % 2])
                pk = [0]

                def fm_mm(col0, m, rhsT, nchunk, wt, wsel):
                    pa = pA[pk[0] % 4]
                    pk[0] += 1
                    for c in range(nchunk):
                        self.mm(pa.t[0:m, :], wsel(c, col0, m), rhsT(c), c == 0, c == nchunk - 1, reads=[wt.r, xnT.r, cqT.r, ckvT.r], writes=[pa.r])
                    return pa

                w_in_sel = lambda c, col0, m: win.t[:, c, col0:col0 + m]
                x_rhs = lambda c: xnT.t[:, c, :]
                sc_d = 32 ** -0.5
                sc_c = 64 ** -0.5
                sc_m = 96 ** -0.5
                for j in range(3):
                    pa = fm_mm(1024 + j * 128, 128, x_rhs, 8, win, w_in_sel)
                    fm_out(pa.t[:, :], 128, sc_d, [("QT_diff", self.QT_diff[2 * j:2 * j + 2, :, tok].rearrange("h r c -> (h r) c"), 0, 128)], [pa.r])
                for j in range(3):
                    pa = fm_mm(1408 + j * 128, 128, x_rhs, 8, win, w_in_sel)
                    fm_out(pa.t[:, :], 128, 1.0, [("KT_diff", self.KT_diff[2 * j:2 * j + 2, :, tok].rearrange("h r c -> (h r) c"), 0, 128)], [pa.r])
                for j in range(2):
                    pa = fm_mm(1792 + j * 128, 128, x_rhs, 8, win, w_in_sel)
                    fm_out(pa.t[:, :], 128, sc_c, [("QT_ch", self.QT_ch[2 * j:2 * j + 2, :, tok].rearrange("h r c -> (h r) c"), 0, 128)], [pa.r])
                for j in range(2):
                    pa = fm_mm(2048 + j * 128, 128, x_rhs, 8, win, w_in_sel)
                    fm_out(pa.t[:, :], 128, 1.0, [("KT_ch", self.KT_ch[2 * j:2 * j + 2, :, tok].rearrange("h r c -> (h r) c"), 0, 128)], [pa.r])
                paA = fm_mm(2304, 32, x_rhs, 8, win, w_in_sel)
                paB = fm_mm(2336, 32, x_rhs, 8, win, w_in_sel)
                self.tt("dve", rt1.t[0:32, :], paA.t[0:32, :], c_.t[0:32, :], ALU.mult, reads=[paA.r, c_.r], writes=[rt1.r])
                self.tt("dve", rt2.t[0:32, :], paB.t[0:32, :], s_.t[0:32, :], ALU.mult, reads=[paB.r, s_.r], writes=[rt2.r])
                f = fst[fk[0] % 4]
                fk[0] += 1
                self.tt("dve", f.t[0:32, :], rt1.t[0:32, :], rt2.t[0:32, :], ALU.add, reads=[rt1.r, rt2.r], writes=[f.r])
                self.store("KTr_mla", self.KTr_mla[:, tok], f, f.t[0:32, :], q="sp")
                wuq_sel = lambda c, col0, m: wuq.t[:, c, col0:col0 + m]
                cq_rhs = lambda c: cqT.t[:, c, :]
                for j in range(3):
                    pa = fm_mm(j * 128, 128, cq_rhs, 2, wuq, wuq_sel)
                    fm_out(pa.t[:, :], 128, sc_m, [("QTn_mla", self.QTn_mla[2 * j:2 * j + 2, :, tok].rearrange("h r c -> (h r) c"), 0, 128)], [pa.r])
                for (h0, nh) in ((0, 4), (4, 2)):
                    m = nh * 32
                    paA = fm_mm(384 + h0 * 32, m, cq_rhs, 2, wuq, wuq_sel)
                    paB = fm_mm(576 + h0 * 32, m, cq_rhs, 2, wuq, wuq_sel)
                    self.stt("dve", rt1.t[0:m, :], paA.t[0:m, :], sc_m, c_.t[0:m, :], ALU.mult, ALU.mult, reads=[paA.r, c_.r], writes=[rt1.r])
                    self.stt("dve", rt2.t[0:m, :], paB.t[0:m, :], sc_m, s_.t[0:m, :], ALU.mult, ALU.mult, reads=[paB.r, s_.r], writes=[rt2.r])
                    f = fst[fk[0] % 4]
                    fk[0] += 1
                    self.tt("dve", f.t[0:m, :], rt1.t[0:m, :], rt2.t[0:m, :], ALU.add, reads=[rt1.r, rt2.r], writes=[f.r])
                    self.store("QTr_mla", self.QTr_mla[h0:h0 + nh, :, tok].rearrange("h r c -> (h r) c"), f, f.t[0:m, :], q="sp")
                wukv_sel = lambda c, col0, m: wukv.t[:, 0, col0:col0 + m]
                ckv_rhs = lambda c: ckvT.t[:, :]
                for j in range(3):
                    pa = fm_mm(j * 128, 128, ckv_rhs, 1, wukv, wukv_sel)
                    fm_out(pa.t[:, :], 128, 1.0, [("KTn_mla", self.KTn_mla[2 * j:2 * j + 2, :, tok].rearrange("h r c -> (h r) c"), 0, 128)], [pa.r])

    def attn(self, l, kind):
        S, NT, NB = self.S, self.NT, self.NB
        nh = {"mla": 6, "diff": 6, "ch": 4}[kind]
        Kd = {"mla": 96, "diff": 64, "ch": 64}[kind]
        QT_s = {"mla": None, "diff": self.QT_diff, "ch": self.QT_ch}[kind]
        KT_s = {"mla": None, "diff": self.KT_diff, "ch": self.KT_ch}[kind]
        vbase = {"mla": 0, "diff": 6, "ch": 12}[kind]
        colbase = {"mla": 0, "diff": 384, "ch": 768}[kind]
        nmaps = 2 if kind == "diff" else 1
        lam_init = 0.8 - 0.6 * math.exp(-0.3 * l)
        with self.phase("attn_" + kind):
            idf = self.sb("idf", [128, 128], F32)
            self.load(idf, idf.t[:], self.identf_in)
            qts = [self.sb("qt%d" % i, [Kd, S], BF16) for i in range(2)]
            kts = [self.sb("kt%d" % i, [Kd, S], BF16) for i in range(2)]
            vts = [self.sb("vt%d" % i, [128, NB, 128], BF16) for i in range(2)]
            if kind == "diff":
                Sp = [self.ps("Sp%d" % i, [128, 2, 512], F32) for i in range(2)]
            elif kind == "mla":
                Sp = [self.ps("Sp%d" % i, [128, 512], F32) for i in range(3)]
                Dm = self.ps("Dm", [128, 512], F32)
            else:
                Sp = [self.ps("Sp%d" % i, [128, 512], F32) for i in range(4)]
            Ac = [self.ps("Ac%d" % i, [128, 512], F32) for i in range(2)]
            Tp = [self.ps("Tp%d" % i, [128, 4, 128], F32) for i in range(2)]
            if kind == "diff":
                Pt = [self.sb("Pt%d" % i, [128, 2, 512], BF16) for i in range(3)]
            else:
                Pt = [self.sb("Pt%d" % i, [128, 512], BF16) for i in range(4)]
            accS = [self.sb("accS%d" % i, [128, 512], F32) for i in range(2)]
            ost = [self.sb("ost%d" % i, [128, 4, 64], F32) for i in range(2)]
            rc = [self.sb("rc%d" % i, [128, 4, 1], F32) for i in range(2)]
            if kind != "mla":
                if kind == "diff":
                    Tt = [self.sb("Tt%d" % i, [128, 2, 512], F32) for i in range(3)]
                else:
                    Tt = [self.sb("Tt%d" % i, [128, 512], F32) for i in range(4)]
            if kind == "diff":
                posq = self.sb("posq", [128, S], F32)
                for c0 in range(0, S, 2048):
                    w = min(2048, S - c0)
                    self.load(posq, posq.t[:, c0:c0 + w], self.posf[:, c0:c0 + w].partition_broadcast(128))
                npk = self.sb("npk", [128, NB], F32)
                self.load(npk, npk.t[:], self.negposk)
                Dt = [self.sb("Dt%d" % i, [128, 512], F32) for i in range(2)]
                zcol = self.sb("zcol", [128, 1], F32)
                self.memset("pool", zcol.t[:], 0.0, writes=[zcol.r])
                tmp = self.sb("tmp", [128, 4, 64], F32)
                lt = self.sb("lt", [128, 128], F32)
                self.load(lt, lt.t[:], self.lam_in[l].partition_broadcast(128))
                lp = self.sb("lp", [128, 64], F32)
                ls = self.sb("ls", [128, 2], F32)
                lam = self.sb("lam", [128, 1], F32)
                self.tt("dve", lp.t[:, 0:32], lt.t[:, 0:32], lt.t[:, 32:64], ALU.mult, reads=[lt.r], writes=[lp.r])
                self.tt("dve", lp.t[:, 32:64], lt.t[:, 64:96], lt.t[:, 96:128], ALU.mult, reads=[lt.r], writes=[lp.r])
                self.P.op("dve", lambda e: e.reduce_sum(out=ls.t[:, 0:1], in_=lp.t[:, 0:32], axis=AX.X), reads=[lp.r], writes=[ls.r])
                self.P.op("dve", lambda e: e.reduce_sum(out=ls.t[:, 1:2], in_=lp.t[:, 32:64], axis=AX.X), reads=[lp.r], writes=[ls.r])
                self.act(ls.t[:], ls.t[:], AF.Exp, reads=[ls.r], writes=[ls.r])
                self.tt("dve", lam.t[:], ls.t[:, 0:1], ls.t[:, 1:2], ALU.subtract, reads=[ls.r], writes=[lam.r])
                self.ts("dve", lam.t[:], lam.t[:], lam_init, None, ALU.add, None, reads=[lam.r], writes=[lam.r])
            if kind == "ch":
                cm = self.sb("cm", [128, 8, 512], F32)
                for i in range(8):
                    self.load(cm, cm.t[:, i, :], self.cmask_in[i])
                Bt = [self.sb("Bt%d" % i, [128, 8, 512], F32) for i in range(2)]
                rbt = self.sb("rbt", [128, 320], F32)
                ex = self.sb("ex", [128, 1536], F32)
                for hh in range(4):
                    self.load(rbt, rbt.t[:], self.rb_in[l, hh:hh + 1, :].partition_broadcast(128))
                    self.cp("dve", ex.t[:, 0:448], rbt.t[:, 0:1].to_broadcast([128, 448]), reads=[rbt.r], writes=[ex.r])
                    self.cp("dve", ex.t[:, 448:768], rbt.t[:, :], reads=[rbt.r], writes=[ex.r])
                    self.cp("dve", ex.t[:, 768:1536], rbt.t[:, 319:320].to_broadcast([128, 768]), reads=[rbt.r], writes=[ex.r])
                    self.store("ext", self.ext[hh], ex, ex.t[:])
            LAG = 2
            NS = len(Sp)

            def head_loads(h):
                qt_, kt_, vt_ = qts[h % 2], kts[h % 2], vts[h % 2]
                if kind == "mla":
                    self.load(qt_, qt_.t[0:64, :], self.QTn_mla[h])
                    self.load(qt_, qt_.t[64:96, :], self.QTr_mla[h])
                    self.load(kt_, kt_.t[0:64, :], self.KTn_mla[h])
                    self.load(kt_, kt_.t[64:96, :], self.KTr_mla)
                else:
                    self.load(qt_, qt_.t[:], QT_s[h, 0:Kd, :])
                    self.load(kt_, kt_.t[:], KT_s[h, 0:Kd, :])
                for j0 in range(0, NB, 16):
                    j1 = min(NB, j0 + 16)
                    self.load(vt_, vt_.t[:, j0:j1, :], self.V_all[vbase + h, j0 * 128:j1 * 128, :].rearrange("(j p) c -> p j c", p=128))
                if kind == "ch":
                    B = Bt[h % 2]
                    for i in range(8):
                        delta = 512 - 128 * i
                        src = bass.AP(self.ext.tensor, h * 128 * 1536 + delta + 511, [[1535, 128], [1, 512]])
                        self.load(B, B.t[:, i, :], src, src_name="ext")
                    self.tt("pool", B.t[:], B.t[:], cm.t[:], ALU.add, reads=[B.r, cm.r], writes=[B.r])

            steps = []
            for h in range(nh):
                hfirst = len(steps)
                for qt in range(NT):
                    if kind == "ch":
                        kbs = list(range(max(0, 4 * qt - 4), 4 * qt + 4))
                    else:
                        kbs = list(range(0, 4 * qt + 4))
                    for idx, kb in enumerate(kbs):
                        for m in range(nmaps):
                            steps.append(dict(h=h, qt=qt, kb=kb, m=m, first=(idx == 0), last=(idx == len(kbs) - 1),
                                              hstep=len(steps) - hfirst, i=len(steps)))
            cur = {}

            def stageA(st):
                h, qt, kb, m = st["h"], st["qt"], st["kb"], st["m"]
                if st["i"] == 0:
                    head_loads(0)
                if st["hstep"] == LAG and h + 1 < nh:
                    head_loads(h + 1)
                qt_, kt_ = qts[h % 2], kts[h % 2]
                q0 = qt * 512
                j = kb - 4 * qt
                c0 = 128 * j if (j > 0 and kind != "ch") else 0
                st["c0"] = c0
                if kind == "diff" and m == 0:
                    Dk = Dt[(st["i"] // 2) % 2]
                    cur["Dk"] = Dk
                    self.act(Dk.t[:, c0:512], posq.t[:, q0 + c0:q0 + 512], AF.Abs, reads=[posq.r, npk.r], writes=[Dk.r],
                             bias=npk.t[:, kb:kb + 1], scale=1.0)
                sp = Sp[st["i"] % NS]
                if kind == "diff":
                    r0, r1 = 32 * m, 32 * m + 32
                else:
                    r0, r1 = 0, Kd
                self.mm(sp.t[:, c0:512], kt_.t[r0:r1, kb * 128:(kb + 1) * 128], qt_.t[r0:r1, q0 + c0:q0 + 512], True, True,
                        reads=[kt_.r, qt_.r], writes=[sp.r])
                if kind == "mla" and DUMMY_MLA > 0:
                    self.mm(Dm.t[:, 0:DUMMY_MLA], kt_.t[r0:r1, kb * 128:(kb + 1) * 128], qt_.t[r0:r1, q0:q0 + DUMMY_MLA], True, True,
                            reads=[kt_.r, qt_.r], writes=[Dm.r])
                pt = Pt[st["i"] % len(Pt)]
                st["pt"] = pt
                if kind == "mla":
                    self.act(pt.t[:, c0:512], sp.t[:, c0:512], AF.Exp, reads=[sp.r], writes=[pt.r])
                else:
                    tt_ = Tt[st["i"] % len(Tt)]
                    if kind == "diff":
                        Dk = cur["Dk"]
                        self.stt("dve", tt_.t[:, c0:512], Dk.t[:, c0:512], -SLOPES[h], sp.t[:, c0:512], ALU.mult, ALU.add,
                                 reads=[Dk.r, sp.r], writes=[tt_.r])
                    else:
                        i = kb - (4 * qt - 4)
                        B = Bt[h % 2]
                        self.tt("dve", tt_.t[:, :], B.t[:, i, :], sp.t[:, :], ALU.add, reads=[B.r, sp.r], writes=[tt_.r])
                    self.act(pt.t[:, c0:512], tt_.t[:, c0:512], AF.Exp, reads=[tt_.r], writes=[pt.r])
                if j >= 0 and kind != "ch":
                    self.memset("pool", pt.t[64:128, c0:c0 + 64], 0.0, writes=[pt.r])

            def stageD(st):
                h, qt, kb, m = st["h"], st["qt"], st["kb"], st["m"]
                vt_ = vts[h % 2]
                c0 = st["c0"]
                pt = st["pt"]
                q0 = qt * 512
                acc = Ac[m] if kind == "diff" else Ac[qt % 2]
                self.mm(acc.t[:, c0:512], vt_.t[:, kb, :], pt.t[:, c0:512], st["first"], st["last"], reads=[vt_.r, pt.r], writes=[acc.r])
                if not (st["last"] and m == nmaps - 1):
                    return
                self.cp("dve", accS[qt % 2].t[:], Ac[qt % 2].t[:], reads=[Ac[qt % 2].r], writes=[accS[qt % 2].r])
                gpend.append(st)

            def stageF(st):
                h, qt, kb, m = st["h"], st["qt"], st["kb"], st["m"]
                q0 = qt * 512
                o_ = ost[qt % 2]
                tps = []
                for mm_ in range(nmaps):
                    acc = Ac[mm_] if kind == "diff" else Ac[qt % 2]
                    tp = Tp[mm_] if kind == "diff" else Tp[qt % 2]
                    rc_ = rc[mm_] if kind == "diff" else rc[qt % 2]
                    a_ = accS[mm_] if kind == "diff" else accS[qt % 2]
                    for s_ in range(4):
                        self.tr(tp.t[:, s_, :], a_.t[:, s_ * 128:(s_ + 1) * 128], idf.t[:], reads=[a_.r, idf.r], writes=[tp.r])
                    self.recip(rc_.t[:], tp.t[:, :, 64:65], reads=[tp.r], writes=[rc_.r])
                    tps.append((tp, rc_))
                if kind == "diff":
                    (t0_, rc0), (t1_, rc1) = tps
                    self.ts("dve", rc1.t[:], rc1.t[:], lam.t[:, 0:1], None, ALU.mult, None, reads=[rc1.r, lam.r], writes=[rc1.r])
                    for s_ in range(4):
                        self.ts("dve", tmp.t[:, s_, :], t1_.t[:, s_, 0:64], rc1.t[:, s_, :], None, ALU.mult, None, reads=[t1_.r, rc1.r], writes=[tmp.r])
                        self.stt("dve", o_.t[:, s_, :], t0_.t[:, s_, 0:64], rc0.t[:, s_, :], tmp.t[:, s_, :], ALU.mult, ALU.subtract,
                                 reads=[t0_.r, rc0.r, tmp.r], writes=[o_.r])
                else:
                    (t0_, rc0), = tps
                    for s_ in range(4):
                        self.ts("dve", o_.t[:, s_, :], t0_.t[:, s_, 0:64], rc0.t[:, s_, :], None, ALU.mult, None, reads=[t0_.r, rc0.r], writes=[o_.r])
                col = colbase + h * 64
                self.store("O_tok", self.O_tok[q0:q0 + 512, col:col + 64].rearrange("(s p) c -> p s c", p=128), o_, o_.t[:])

            def finalize_diff_evac():
                for mm_ in range(2):
                    self.cp("dve", accS[mm_].t[:], Ac[mm_].t[:], reads=[Ac[mm_].r], writes=[accS[mm_].r])

            def finalize_diff(h, qt):
                o_ = ost[qt % 2]
                q0 = qt * 512
                tps = []
                for mm_ in range(2):
                    acc, tp, rc_, a_ = Ac[mm_], Tp[mm_], rc[mm_], accS[mm_]
                    for s_ in range(4):
                        self.tr(tp.t[:, s_, :], a_.t[:, s_ * 128:(s_ + 1) * 128], idf.t[:], reads=[a_.r, idf.r], writes=[tp.r])
                    self.recip(rc_.t[:], tp.t[:, :, 64:65], reads=[tp.r], writes=[rc_.r])
                    tps.append((tp, rc_))
                (t0_, rc0), (t1_, rc1) = tps
                self.ts("dve", rc1.t[:], rc1.t[:], lam.t[:, 0:1], None, ALU.mult, None, reads=[rc1.r, lam.r], writes=[rc1.r])
                for s_ in range(4):
                    self.ts("dve", tmp.t[:, s_, :], t1_.t[:, s_, 0:64], rc1.t[:, s_, :], None, ALU.mult, None, reads=[t1_.r, rc1.r], writes=[tmp.r])
                    self.stt("dve", o_.t[:, s_, :], t0_.t[:, s_, 0:64], rc0.t[:, s_, :], tmp.t[:, s_, :], ALU.mult, ALU.subtract,
                             reads=[t0_.r, rc0.r, tmp.r], writes=[o_.r])
                col = colbase + h * 64
                self.store("O_tok", self.O_tok[q0:q0 + 512, col:col + 64].rearrange("(s p) c -> p s c", p=128), o_, o_.t[:])

            def pairA(st):
                h, qt, kb = st["h"], st["qt"], st["kb"]
                if st["i"] == 0:
                    head_loads(0)
                if st["hstep"] == 2 and h + 1 < nh:
                    head_loads(h + 1)
                qt_, kt_ = qts[h % 2], kts[h % 2]
                q0 = qt * 512
                j = kb - 4 * qt
                c0 = 128 * j if j > 0 else 0
                st["c0"] = c0
                Dk = Dt[st["i"] % 2]
                self.act(Dk.t[:, c0:512], posq.t[:, q0 + c0:q0 + 512], AF.Abs, reads=[posq.r, npk.r], writes=[Dk.r],
                         bias=npk.t[:, kb:kb + 1], scale=1.0)
                sp = Sp[st["i"] % 2]
                for m in range(2):
                    r0, r1 = 32 * m, 32 * m + 32
                    self.mm(sp.t[:, m, c0:512], kt_.t[r0:r1, kb * 128:(kb + 1) * 128], qt_.t[r0:r1, q0 + c0:q0 + 512], True, True,
                            reads=[kt_.r, qt_.r], writes=[sp.r])
                st["Dk"] = Dk
                st["sp"] = sp
                st["j"] = j

            def pairB(st):
                h = st["h"]
                c0, Dk, sp, j = st["c0"], st["Dk"], st["sp"], st["j"]
                tt_ = Tt[st["i"] % 3]
                pt = Pt[st["i"] % 3]
                st["pt"] = pt
                n = 512 - c0
                self.stt("dve", tt_.t[:, :, c0:512], Dk.t[:, c0:512].unsqueeze(1).to_broadcast([128, 2, n]), -SLOPES[h], sp.t[:, :, c0:512],
                         ALU.mult, ALU.add, reads=[Dk.r, sp.r], writes=[tt_.r])
                self.act(pt.t[:, :, c0:512], tt_.t[:, :, c0:512], AF.Exp, reads=[tt_.r], writes=[pt.r])
                if j >= 0:
                    self.memset("pool", pt.t[64:128, :, c0:c0 + 64], 0.0, writes=[pt.r])

            def pairD(st):
                h, qt, kb = st["h"], st["qt"], st["kb"]
                vt_ = vts[h % 2]
                c0 = st["c0"]
                pt = st["pt"]
                for m in range(2):
                    self.mm(Ac[m].t[:, c0:512], vt_.t[:, kb, :], pt.t[:, m, c0:512], st["first"], st["last"], reads=[vt_.r, pt.r], writes=[Ac[m].r])
                if st["last"]:
                    finalize_diff_evac()
                    pend.append((h, qt))

            if kind == "diff":
                psteps = []
                for h in range(nh):
                    hfirst = len(psteps)
                    for qt in range(NT):
                        kbs = list(range(0, 4 * qt + 4))
                        for idx, kb in enumerate(kbs):
                            psteps.append(dict(h=h, qt=qt, kb=kb, first=(idx == 0), last=(idx == len(kbs) - 1),
                                               hstep=len(psteps) - hfirst, i=len(psteps)))
                N = len(psteps)
                pend = []
                for i in range(N + 2):
                    if i < N:
                        pairA(psteps[i])
                    while pend:
                        finalize_diff(*pend.pop(0))
                    if 0 <= i - 1 < N:
                        pairB(psteps[i - 1])
                    if i - 2 >= 0:
                        pairD(psteps[i - 2])
                while pend:
                    finalize_diff(*pend.pop(0))
            else:
                N = len(steps)
                gpend = []
                for i in range(N + LAG):
                    if i < N:
                        stageA(steps[i])
                    while gpend:
                        stageF(gpend.pop(0))
                    if i - LAG >= 0:
                        stageD(steps[i - LAG])
                while gpend:
                    stageF(gpend.pop(0))

    def p3a(self, l, xsrc, xsrc_name):
        S, NT = self.S, self.NT
        lam_init = 0.8 - 0.6 * math.exp(-0.3 * l)
        with self.phase("p3a"):
            epsb = self.sb("epsb", [128, 2], F32)
            self.memset("pool", epsb.t[:, 0:1], EPS, writes=[epsb.r])
            self.memset("pool", epsb.t[:, 1:2], -0.5, writes=[epsb.r])
            idf = self.sb("idf", [128, 128], F32)
            self.load(idf, idf.t[:], self.identf_in)
            idb = self.sb("idb", [128, 128], BF16)
            self.cp("dve", idb.t[:], idf.t[:], reads=[idf.r], writes=[idb.r])
            go = self.sb("go", [128, 8], F32)
            self.load(go, go.t[:], self.g_o[l])
            stg = [self.sb("stg%d" % i, [128, 2048], F32) for i in range(2)]
            wo = self.sb("wo", [128, 8, D], BF16)
            self.load_w(wo, wo.t, self.w_out[l], 8, D, stg)
            ot = [self.sb("ot%d" % i, [128, 4, D], F32) for i in range(2)]
            xt = [self.sb("xt%d" % i, [128, 4, D], F32) for i in range(2)]
            on = self.sb("on", [128, 4, D], BF16)
            onT = self.sb("onT", [128, 8, 512], BF16)
            xo = [self.sb("xo%d" % i, [128, 4, D], F32) for i in range(2)]
            ssg = self.sb("ssg", [128, 4, 8], F32)
            junk = self.sb("junk", [128, 384], BF16)
            pT = [self.ps("pT%d" % i, [128, 512], BF16) for i in range(2)]
            pA = [self.ps("pA%d" % i, [128, 512], F32) for i in range(4)]
            groups = [(0, 384)] + [(384 + 64 * g, 64) for g in range(6)] + [(768, 256)]
            for t in range(NT):
                tok = slice(t * 512, (t + 1) * 512)
                o = ot[t % 2]
                x = xt[t % 2]
                self.load(o, o.t[:], self.O_tok[tok, :].rearrange("(s p) c -> p s c", p=128), src_name="O_tok")
                self.load(x, x.t[:], xsrc[tok, :].rearrange("(s p) c -> p s c", p=128), src_name=xsrc_name)
                for s in range(4):
                    for gi, (c0, n) in enumerate(groups):
                        self.act(junk.t[:, 0:n], o.t[:, s, c0:c0 + n], AF.Square, reads=[o.r], writes=[junk.r, ssg.r], accum=ssg.t[:, s, gi:gi + 1])
                self.act(ssg.t[:, :, 0:1], ssg.t[:, :, 0:1], AF.Ln, reads=[ssg.r, epsb.r], writes=[ssg.r], bias=epsb.t[:, 0:1], scale=1.0 / 384)
                self.act(ssg.t[:, :, 1:7], ssg.t[:, :, 1:7], AF.Ln, reads=[ssg.r, epsb.r], writes=[ssg.r], bias=epsb.t[:, 0:1], scale=1.0 / 64)
                self.act(ssg.t[:, :, 7:8], ssg.t[:, :, 7:8], AF.Ln, reads=[ssg.r, epsb.r], writes=[ssg.r], bias=epsb.t[:, 0:1], scale=1.0 / 256)
                self.act(ssg.t[:], ssg.t[:], AF.Exp, reads=[ssg.r], writes=[ssg.r], scale=-0.5)
                for s in range(4):
                    for gi, (c0, n) in enumerate(groups):
                        eng = "dve"
                        self.ts(eng, on.t[:, s, c0:c0 + n], o.t[:, s, c0:c0 + n], ssg.t[:, s, gi:gi + 1], None, ALU.mult, None, reads=[o.r, ssg.r], writes=[on.r])
                for c in range(8):
                    p = pT[c % 2]
                    for s in range(4):
                        self.tr(p.t[:, s * 128:(s + 1) * 128], on.t[:, s, c * 128:(c + 1) * 128], idb.t[:], reads=[on.r, idb.r], writes=[p.r])
                    if c in (3, 4, 5):
                        self.ts("dve", onT.t[:, c, :], p.t[:, 0:512], go.t[:, c:c + 1], 1.0 - lam_init, ALU.mult, ALU.mult, reads=[p.r, go.r], writes=[onT.r])
                    else:
                        self.act(onT.t[:, c, :], p.t[:, 0:512], AF.Copy, reads=[p.r, go.r], writes=[onT.r], scale=go.t[:, c:c + 1])
                xo_ = xo[t % 2]
                k = 0
                for s in range(4):
                    for hf in range(2):
                        pa = pA[k % 4]
                        k += 1
                        for c in range(8):
                            self.mm(pa.t[:], onT.t[:, c, s * 128:(s + 1) * 128], wo.t[:, c, hf * 512:(hf + 1) * 512], c == 0, c == 7, reads=[onT.r, wo.r], writes=[pa.r])
                        self.tt("dve", xo_.t[:, s, hf * 512:(hf + 1) * 512], x.t[:, s, hf * 512:(hf + 1) * 512], pa.t[:], ALU.add, reads=[x.r, pa.r], writes=[xo_.r])
                self.store("x1", self.x1[tok, :].rearrange("(s p) c -> p s c", p=128), xo_, xo_.t[:])

    def p3b(self, l, final):
        S = self.S
        TT = 256
        NTT = S // TT
        with self.phase("p3b"):
            epsb = self.sb("epsb", [128, 2], F32)
            self.memset("pool", epsb.t[:, 0:1], EPS, writes=[epsb.r])
            self.memset("pool", epsb.t[:, 1:2], -0.5, writes=[epsb.r])
            idf = self.sb("idf", [128, 128], F32)
            self.load(idf, idf.t[:], self.identf_in)
            idb = self.sb("idb", [128, 128], BF16)
            self.cp("dve", idb.t[:], idf.t[:], reads=[idf.r], writes=[idb.r])
            gT = self.sb("gT", [128, 8], F32)
            self.load(gT, gT.t[:], self.g_ffn[l])
            cw = self.sb("cw", [128, 44, 3], F32)
            self.load(cw, cw.t[:], self.cw_in[l].rearrange("p (j k) -> p j k", k=3))
            cb = self.sb("cb", [128, 44], F32)
            self.load(cb, cb.t[:], self.cb_in[l])
            stg = [self.sb("stg%d" % i, [128, 512], F32) for i in range(2)]
            wf1 = self.sb("wf1", [128, 8, 2 * DFF], BF16)
            wf2 = self.sb("wf2", [128, 22, D], BF16)

            def lw(wt, src2, nchunk, ncol):
                k = 0
                for c in range(nchunk):
                    for c0 in range(0, ncol, 512):
                        w = min(512, ncol - c0)
                        s = stg[k % 2]
                        k += 1
                        self.load(s, s.t[:, 0:w], src2[c * 128:(c + 1) * 128, c0:c0 + w])
                        self.cp("pool", wt.t[:, c, c0:c0 + w], s.t[:, 0:w], reads=[s.r], writes=[wt.r])
            if final:
                gfb = self.sb("gfb", [128, D], F32)
                self.load(gfb, gfb.t[:], self.g_fin.partition_broadcast(128))
            xt = [self.sb("xt%d" % i, [128, 2, D], F32) for i in range(1)]
            xns = [self.sb("xn%d" % i, [128, 2, D], BF16) for i in range(2)]
            xnTs = [self.sb("xnT%d" % i, [128, 8, TT], BF16) for i in range(2)]
            actTs = [self.sb("actT%d" % i, [128, 22, TT], BF16) for i in range(2)]
            bigs = [a_.t[:].rearrange("p a b -> p (a b)").bitcast(F32) for a_ in actTs]
            kq = 0
            for c in range(8):
                for hh in range(2):
                    sv, sr = bigs[kq % 2], actTs[kq % 2]
                    self.load(sr, sv[:, 0:DFF], self.w_f1[l][c * 128:(c + 1) * 128, hh * DFF:(hh + 1) * DFF], q=("sp" if kq % 2 == 0 else "pool"))
                    self.cp("pool" if kq % 2 == 0 else "dve", wf1.t[:, c, hh * DFF:(hh + 1) * DFF], sv[:, 0:DFF], reads=[sr.r], writes=[wf1.r])
                    kq += 1
            for c2 in range(11):
                sv, sr = bigs[kq % 2], actTs[kq % 2]
                self.load(sr, sv[:, 0:2048].rearrange("p (c n) -> p c n", c=2),
                          self.w_f2[l][c2 * 256:(c2 + 1) * 256, :].rearrange("(c p) n -> p c n", p=128), q=("sp" if kq % 2 == 0 else "pool"))
                self.cp("pool" if kq % 2 == 0 else "dve", wf2.t[:, 2 * c2:2 * c2 + 2, :], sv[:, 0:2048].rearrange("p (c n) -> p c n", c=2), reads=[sr.r], writes=[wf2.r])
                kq += 1
            ss = self.sb("ss", [128, 4], F32)
            junk = self.sb("junk", [128, D], BF16)
            hal = self.sb("hal", [128, 44, 2], F32)
            hal2 = self.sb("hal2", [128, 44, 2], F32)
            c1 = [self.sb("c1_%d" % i, [128, TT], F32) for i in range(4)]
            c2 = c1
            c3 = c1
            sg = [self.sb("sg%d" % i, [128, TT], F32) for i in range(2)]
            stt_ = self.sb("cst", [128, 44, 2], F32)
            self.memset("pool", stt_.t[:], 0.0, writes=[stt_.r])
            xo = self.sb("xo", [128, 2, D], F32)
            pT = [self.ps("pT%d" % i, [128, 512], BF16) for i in range(2)]
            pH = [self.ps("pH%d" % i, [128, 512], F32) for i in range(4)]
            pY = [self.ps("pY%d" % i, [128, 512], F32) for i in range(2)]
            kk = 0

            dmy = pT[1].t[:, 0:1024].bitcast(F32)

            def o_chunk(tp, i):
                tokp = slice(tp * TT, (tp + 1) * TT)
                aT = actTs[tp % 2]
                if i == 0:
                    self.load(xo, xo.t[:], self.x1[tokp, :].rearrange("(s p) c -> p s c", p=128))
                g0 = 0 if i < 11 else 2
                ii = (i % 11) * 2
                for g in (g0, g0 + 1):
                    s_, hf = divmod(g, 2)
                    py = pY[g % 2]
                    for k_ in (ii, ii + 1):
                        self.mm(py.t[:], aT.t[:, k_, s_ * 128:(s_ + 1) * 128], wf2.t[:, k_, hf * 512:(hf + 1) * 512], k_ == 0, k_ == 21,
                                reads=[aT.r, wf2.r], writes=[py.r])
                if i % 11 == 10:
                    for g in (g0, g0 + 1):
                        s_, hf = divmod(g, 2)
                        py = pY[g % 2]
                        self.tt("dve", xo.t[:, s_, hf * 512:(hf + 1) * 512], xo.t[:, s_, hf * 512:(hf + 1) * 512], py.t[:], ALU.add,
                                reads=[xo.r, py.r], writes=[xo.r])
                if i == 21:
                    if not final:
                        self.store("x2", self.x2[tokp, :].rearrange("(s p) c -> p s c", p=128), xo, xo.t[:])
                    else:
                        for s_ in range(2):
                            self.act(junk.t[:], xo.t[:, s_, :], AF.Square, reads=[xo.r], writes=[junk.r, ss.r], accum=ss.t[:, 2 + s_:3 + s_])
                        self.rstd(ss.t[:, 2:4], ss.t[:, 2:4], float(D), epsb, reads=[ss.r], writes=[ss.r])
                        for s_ in range(2):
                            self.stt("dve", xo.t[:, s_, :], xo.t[:, s_, :], ss.t[:, 2 + s_:3 + s_], gfb.t[:], ALU.mult, ALU.mult,
                                     reads=[xo.r, ss.r, gfb.r], writes=[xo.r])
                        self.store("out", self.out[tokp, :].rearrange("(s p) c -> p s c", p=128), xo, xo.t[:])

            def emit_O(tp, i):
                if i is None:
                    for i_ in range(22):
                        o_chunk(tp, i_)
                else:
                    o_chunk(tp, i)

            def prep(t_):
                tok_ = slice(t_ * TT, (t_ + 1) * TT)
                self.load(xt[0], xt[0].t[:], self.x1[tok_, :].rearrange("(s p) c -> p s c", p=128))
                self.norm_T(xt[0], 2, gT, xns[t_ % 2], xnTs[t_ % 2], pT, idb, epsb, ss, junk)

            prep(0)
            for t in range(NTT):
                tok = slice(t * TT, (t + 1) * TT)
                x = xt[0]
                actT = actTs[t % 2]
                xnT = xnTs[t % 2]
                self.tt("dve", hal.t[:, :, 0], cw.t[:, :, 1], stt_.t[:, :, 1], ALU.mult, reads=[cw.r, stt_.r], writes=[hal.r])
                self.tt("dve", hal2.t[:, :, 0], cw.t[:, :, 0], stt_.t[:, :, 0], ALU.mult, reads=[cw.r, stt_.r], writes=[hal2.r])
                self.tt("dve", hal.t[:, :, 1], cw.t[:, :, 0], stt_.t[:, :, 1], ALU.mult, reads=[cw.r, stt_.r, hal.r], writes=[hal.r])
                self.tt("dve", hal.t[:, :, 0], hal.t[:, :, 0], hal2.t[:, :, 0], ALU.add, reads=[hal.r, hal2.r], writes=[hal.r])
                for i in range(22):
                    un = []
                    for which in range(2):
                        idx = i + 22 * which
                        col0 = idx * 128
                        ph = pH[kk % 4]
                        a = c1[kk % 4]
                        kk += 1
                        for c in range(8):
                            self.mm(ph.t[:, 0:TT], wf1.t[:, c, col0:col0 + 128], xnT.t[:, c, :], c == 0, c == 7, reads=[wf1.r, xnT.r], writes=[ph.r])
                        un.append((idx, ph, a))
                    for (idx, ph, a) in un:
                        self.act(a.t[:], ph.t[:, 0:TT], AF.Identity, reads=[ph.r, cw.r, cb.r], writes=[a.r], bias=cb.t[:, idx:idx + 1], scale=cw.t[:, idx, 2:3])
                    for (idx, ph, a) in un:
                        self.stt("dve", a.t[:, 1:TT], ph.t[:, 0:TT - 1], cw.t[:, idx, 1:2], a.t[:, 1:TT], ALU.mult, ALU.add, reads=[ph.r, cw.r, a.r], writes=[a.r])
                    for (idx, ph, a) in un:
                        self.stt("dve", a.t[:, 2:TT], ph.t[:, 0:TT - 2], cw.t[:, idx, 0:1], a.t[:, 2:TT], ALU.mult, ALU.add, reads=[ph.r, cw.r, a.r], writes=[a.r])
                    for (idx, ph, a) in un:
                        self.tt("pool", a.t[:, 0:2], a.t[:, 0:2], hal.t[:, idx, :], ALU.add, reads=[a.r, hal.r], writes=[a.r])
                    for (idx, ph, a) in un:
                        self.cp("act", stt_.t[:, idx, :], ph.t[:, TT - 2:TT], reads=[ph.r], writes=[stt_.r])
                    s_ = sg[i % 2]
                    self.act(s_.t[:], un[1][2].t[:], AF.Silu, reads=[un[1][2].r], writes=[s_.r])
                    self.tt("pool", actT.t[:, i, :], s_.t[:], un[0][2].t[:], ALU.mult, reads=[s_.r, un[0][2].r], writes=[actT.r])
                    if t > 0:
                        emit_O(t - 1, i)
                    if i == 12 and t + 1 < NTT:
                        prep(t + 1)
                    for _ in range(DUMMY_FFN):
                        self.mm(dmy, wf2.t[:, 0, 0:128], wf2.t[:, 1, 0:512], True, True, reads=[wf2.r], writes=[pT[1].r])
            emit_O(NTT - 1, None)

    def build(self):
        import os
        ph = os.environ.get("K_PHASES", "p0,p1,mla,diff,ch,p3a,p3b").split(",")
        nl = int(os.environ.get("K_LAYERS", str(DEPTH)))
        self.declare()
        if "p0" in ph:
            self.p0()
        xsrc, xname = self.x_in, None
        for l in range(nl):
            if "p1" in ph:
                self.p1(l, xsrc, xname)
            for kd in ("mla", "diff", "ch"):
                if kd in ph:
                    self.attn(l, kd)
            if "p3a" in ph:
                self.p3a(l, xsrc, xname)
            if "p3b" in ph:
                self.p3b(l, final=(l == DEPTH - 1))
            xsrc, xname = self.x2, None
        self.gst.close()
        return self.nc


def _host_consts():
    identf = np.eye(128, dtype=np.float32)
    p = np.arange(128)
    inv = (10000.0 ** (-(np.arange(16, dtype=np.float32)) / 16)).astype(np.float32)
    invf = np.zeros((128, 2), np.float32)
    invf[:, 0] = inv[p % 16]
    sign = np.where((p % 32) < 16, -1.0, 1.0).astype(np.float32)
    invf[:, 1] = inv[p % 16] * sign
    cmask = np.zeros((8, 128, 512), np.float32)
    ki = np.arange(128)[:, None] // 64
    qi = np.arange(512)[None, :] // 64
    for i in range(8):
        delta = 512 - 128 * i
        d = delta // 64 + qi - ki
        cmask[i] = np.where((d >= 0) & (d <= 8), 0.0, NEG)
    return identf, invf, cmask


def _prep_weights(inp):
    f = lambda a: np.ascontiguousarray(np.asarray(a, dtype=np.float32))
    L = DEPTH
    w_in = f(inp["w_in"])
    cq, ckv, kr = w_in[:, :, 0:256], w_in[:, :, 256:384], w_in[:, :, 384:416]
    dq, dk, dv = w_in[:, :, 416:800], w_in[:, :, 800:1184], w_in[:, :, 1184:1568]
    chq, chk, chv = w_in[:, :, 1568:1824], w_in[:, :, 1824:2080], w_in[:, :, 2080:2336]
    swap = np.concatenate([np.arange(16, 32), np.arange(0, 16)])
    w_in_d = np.concatenate([cq, ckv, dv, chv, dq, dk, chq, chk, kr, kr[:, :, swap]], axis=2)
    assert w_in_d.shape[2] == NCOL_IN
    wuq = f(inp["mla_w_uq"]).reshape(L, 256, 6, 96)
    nope = wuq[..., 0:64].reshape(L, 256, 384)
    rope = wuq[..., 64:96]
    w_uq_d = np.concatenate([nope, rope.reshape(L, 256, 192), rope[..., swap].reshape(L, 256, 192)], axis=2)
    wukv = f(inp["mla_w_ukv"]).reshape(L, 128, 6, 128)
    w_ukv_d = np.concatenate([wukv[..., 0:64].reshape(L, 128, 384), wukv[..., 64:128].reshape(L, 128, 384)], axis=2)

    def colT(g, n):
        return np.ascontiguousarray(f(g).reshape(L, n, 128).transpose(0, 2, 1))
    g_o = np.concatenate([f(inp["mla_out_norm"]), f(inp["diff_norm"]), f(inp["chunk_out_norm"])], axis=1)
    cwt = f(inp["ffn_conv_w"])
    cw = np.ascontiguousarray(cwt.reshape(L, 3, 44, 128).transpose(0, 3, 2, 1)).reshape(L, 128, 132)
    cb = np.ascontiguousarray(f(inp["ffn_conv_b"]).reshape(L, 44, 128).transpose(0, 2, 1))
    return {
        "w_in": np.ascontiguousarray(w_in_d), "w_uq": np.ascontiguousarray(w_uq_d), "w_ukv": np.ascontiguousarray(w_ukv_d),
        "w_out": f(inp["w_out"]), "w_f1": f(inp["w_ffn_in"]), "w_f2": f(inp["w_ffn_out"]),
        "g_attn": colT(inp["attn_norm"], 8), "g_ffn": colT(inp["ffn_norm"], 8),
        "g_q": colT(inp["mla_q_norm"], 2), "g_kv": colT(inp["mla_kv_norm"], 1), "g_o": colT(g_o, 8),
        "lam": f(inp["diff_lambda"]).reshape(L, 1, 128), "rb": f(inp["chunk_rel_bias"]),
        "cw": cw, "cb": cb, "g_fin": f(inp["final_norm"]).reshape(1, D),
    }


_NC_CACHE = {}


def run_cores(inp, S, ncores, debug=False):
    key = (S, debug)
    if key not in _NC_CACHE:
        _NC_CACHE[key] = K(S, debug).build()
    nc = _NC_CACHE[key]
    identf, invf, cmask = _host_consts()
    wd = _prep_weights(inp)
    x = np.asarray(inp["x"], dtype=np.float32)
    pos = np.asarray(inp["positions"]).astype(np.int32)
    in_maps = []
    for b in range(ncores):
        m = dict(wd)
        m["x"] = np.ascontiguousarray(x[b, :S])
        m["pos"] = np.ascontiguousarray(pos[b, :S].reshape(S // 128, 128))
        m["identf"] = identf
        m["invf"] = invf
        m["cmask"] = cmask
        in_maps.append(m)
    res = run_bass_kernel_spmd(nc, in_maps, core_ids=list(range(ncores)))
    return res


def kernel(**inputs):
    B, S = inputs["x"].shape[0], inputs["x"].shape[1]
    res = run_cores(inputs, S, B)
    return np.stack([np.asarray(r["out"], dtype=np.float32) for r in res.results], axis=0)
```
